# Optimizing a Trainium2 kernel written in Bass

```python
import math
import jax, jax.numpy as jnp
from jax import lax
import numpy as np

D_MODEL = 1024
BATCH = 4
SEQ = 8192
DEPTH = 2
DEC_BATCH = 16
DEC_SEQ = 4096
PAST_LEN = 128

HEAD_DIM = 64
GW = D_MODEL // 4
N_HEADS = GW // HEAD_DIM
MLA_Q_RANK = D_MODEL // 4
MLA_KV_RANK = D_MODEL // 8
MLA_NOPE_DIM = HEAD_DIM
MLA_ROPE_DIM = HEAD_DIM // 2
MLA_V_DIM = HEAD_DIM
ROPE_BASE = 10000.0
ATTN_BLOCK = 128
GRID_W = 64
NA_KH_MAX = 8
NA_KW = 16
NA_QROWS = 2
NA_QCOL = 16
NA_KCOL = 32
HG_CHUNK = 32
RW_W_RANK = 64
RW_A_RANK = 64
RW_G_RANK = 128
RW_DECAY_SCALE = math.exp(-0.5)
RW_GN_EPS = 64e-5
N_EXPERTS = 16
D_EXPERT = 512
CAPACITY = 2
D_PLE = 256
ALPHA = (2 * DEPTH) ** 0.25
BETA = (8 * DEPTH) ** -0.25
A_COLS = MLA_Q_RANK + MLA_KV_RANK + MLA_ROPE_DIM
B_COLS = 3 * GW
C_COLS = 5 * GW
D_COLS = 3 * GW + 2 * RW_W_RANK + 2 * RW_A_RANK + RW_G_RANK
D_IN = A_COLS + B_COLS + C_COLS + D_COLS
F32 = jnp.float32

kernel_name = 'hybrid_bidir_mla_natten_hgrn2_rwkv7_ecmoe_encoder'


def _split(z, widths):
    return jnp.split(z, np.cumsum(widths)[:-1].tolist(), axis=-1)


def _rmsnorm(x, g, eps=1e-6):
    xf = x.astype(F32)
    y = xf * lax.rsqrt(jnp.mean(xf * xf, axis=-1, keepdims=True) + eps)
    return (y * g.astype(F32)).astype(x.dtype)


def _layernorm(x, g, b, eps=1e-5):
    xf = x.astype(F32)
    mu = jnp.mean(xf, axis=-1, keepdims=True)
    var = jnp.mean(jnp.square(xf - mu), axis=-1, keepdims=True)
    return ((xf - mu) * lax.rsqrt(var + eps) * g.astype(F32) + b.astype(F32)).astype(x.dtype)


def _rope(x, cos, sin):
    xf = x.astype(F32)
    x1, x2 = jnp.split(xf, 2, axis=-1)
    return jnp.concatenate([x1 * cos - x2 * sin, x1 * sin + x2 * cos], axis=-1).astype(x.dtype)


def _bidir(fwd, bwd):
    return jnp.stack([fwd, jnp.flip(bwd, axis=1)])


def _merge_dirs(o):
    return o[0] + jnp.flip(o[1], axis=1)


def _heads(t):
    return t.reshape(t.shape[:-1] + (N_HEADS, HEAD_DIM))


def _mla(zA, gq, gkv, wuq, wuk, wuv):
    B, N, _ = zA.shape
    cq, ckv, kr = _split(zA, (MLA_Q_RANK, MLA_KV_RANK, MLA_ROPE_DIM))
    q = (_rmsnorm(cq, gq) @ wuq).reshape(B, N, N_HEADS, MLA_NOPE_DIM + MLA_ROPE_DIM)
    q_nope, q_rope = q[..., :MLA_NOPE_DIM], q[..., MLA_NOPE_DIM:]
    ckv = _rmsnorm(ckv, gkv)
    k_nope = (ckv @ wuk).reshape(B, N, N_HEADS, MLA_NOPE_DIM)
    v = (ckv @ wuv).reshape(B, N, N_HEADS, MLA_V_DIM)
    inv_freq = ROPE_BASE ** (-jnp.arange(0, MLA_ROPE_DIM, 2, dtype=F32) / MLA_ROPE_DIM)
    ang = jnp.arange(N, dtype=F32)[:, None] * inv_freq[None, :]
    cos, sin = jnp.cos(ang), jnp.sin(ang)
    q_rope = _rope(q_rope, cos[:, None, :], sin[:, None, :])
    k_rope = _rope(kr, cos, sin)
    scale = (MLA_NOPE_DIM + MLA_ROPE_DIM) ** -0.5
    nb = N // ATTN_BLOCK
    qn = q_nope.reshape(B, nb, ATTN_BLOCK, N_HEADS, MLA_NOPE_DIM).transpose(1, 0, 2, 3, 4)
    qr = q_rope.reshape(B, nb, ATTN_BLOCK, N_HEADS, MLA_ROPE_DIM).transpose(1, 0, 2, 3, 4)

    def block(args):
        qn_b, qr_b = args
        s = (jnp.einsum('bqhd,bkhd->bhqk', qn_b, k_nope)
             + jnp.einsum('bqhr,bkr->bhqk', qr_b, k_rope)).astype(F32) * scale
        pr = jax.nn.softmax(s, axis=-1).astype(v.dtype)
        return jnp.einsum('bhqk,bkhd->bqhd', pr, v)

    o = lax.map(block, (qn, qr))
    return o.transpose(1, 0, 2, 3, 4).reshape(B, N, N_HEADS * MLA_V_DIM)


def _neighbourhood_attention(zB, bias):
    B, N, _ = zB.shape
    rows = N // GRID_W
    kh = min(NA_KH_MAX, rows)
    kbh = min(kh + NA_QROWS - 1, rows)
    n_rb = rows // NA_QROWS
    n_cb = GRID_W // NA_QCOL
    q, k, v = _split(zB, (GW, GW, GW))
    grid = lambda t: t.reshape(B, rows, GRID_W, N_HEADS, HEAD_DIM)
    q = grid(q * (HEAD_DIM ** -0.5)).reshape(B, rows, n_cb, NA_QCOL, N_HEADS, HEAD_DIM)
    qcol = np.arange(n_cb)[:, None] * NA_QCOL + np.arange(NA_QCOL)[None, :]
    kstart = np.clip(np.arange(n_cb) * NA_QCOL - NA_KW // 2, 0, GRID_W - NA_KCOL)
    kcol = kstart[:, None] + np.arange(NA_KCOL)[None, :]
    k = grid(k)[:, :, kcol]
    v = grid(v)[:, :, kcol]
    cstart = np.clip(qcol - NA_KW // 2, 0, GRID_W - NA_KW)
    col_ok = (kcol[:, None, :] >= cstart[..., None]) & (kcol[:, None, :] < cstart[..., None] + NA_KW)
    dcol = np.clip(kcol[:, None, :] - qcol[..., None] + NA_KW - 1, 0, 2 * NA_KW - 2)

    def row_block(rb):
        qrow = rb * NA_QROWS + jnp.arange(NA_QROWS)
        rstart = jnp.clip(qrow - kh // 2, 0, rows - kh)
        k0 = jnp.clip(rstart[0], 0, rows - kbh)
        krow = k0 + jnp.arange(kbh)
        row_ok = (krow[None, :] >= rstart[:, None]) & (krow[None, :] < rstart[:, None] + kh)
        drow = jnp.clip(krow[None, :] - qrow[:, None] + NA_KH_MAX - 1, 0, 2 * NA_KH_MAX - 2)
        q_b = lax.dynamic_slice_in_dim(q, rb * NA_QROWS, NA_QROWS, axis=1)
        k_b = lax.dynamic_slice_in_dim(k, k0, kbh, axis=1)
        v_b = lax.dynamic_slice_in_dim(v, k0, kbh, axis=1)
        s = jnp.einsum('bicuhd,bjcwhd->bhciujw', q_b, k_b).astype(F32)
        s = s + bias[:, drow[None, :, None, :, None], dcol[:, None, :, None, :]].astype(F32)
        mask = row_ok[None, :, None, :, None] & col_ok[:, None, :, None, :]
        s = jnp.where(mask, s, -jnp.inf)
        pr = jax.nn.softmax(s.reshape(s.shape[:-2] + (kbh * NA_KCOL,)), axis=-1)
        pr = pr.reshape(s.shape).astype(v.dtype)
        return jnp.einsum('bhciujw,bjcwhd->bicuhd', pr, v_b)

    o = lax.map(row_block, jnp.arange(n_rb))
    return o.transpose(1, 0, 2, 3, 4, 5, 6).reshape(B, N, GW)


def _hgrn2(zC, lb, gnorm):
    B, N, _ = zC.shape
    L = HG_CHUNK
    nc = N // L
    zf = zC.astype(F32)
    q, f_fw, f_bw, i_in, g = _split(zf, (GW,) * 5)
    lb = lb.astype(F32)
    f_fw = lb[0] + (1.0 - lb[0]) * jax.nn.sigmoid(f_fw)
    f_bw = lb[1] + (1.0 - lb[1]) * jax.nn.sigmoid(f_bw)

    def chunked(t):
        return _heads(t).reshape(2, B, nc, L, N_HEADS, HEAD_DIM).transpose(2, 0, 1, 4, 3, 5)

    qs = chunked(_bidir(q, q))
    ks = chunked(_bidir(1.0 - f_fw, 1.0 - f_bw))
    vs = chunked(_bidir(i_in, i_in))
    gs = chunked(_bidir(jnp.log(f_fw), jnp.log(f_bw)))
    tri = np.tril(np.ones((L, L), dtype=bool))[:, :, None]

    def step(S, inp):
        qc, kc, vc, gc = inp
        b = jnp.cumsum(gc, axis=-2)
        o_inter = jnp.einsum('zbhtd,zbhdv->zbhtv', qc * jnp.exp(b), S)
        diff = b[..., :, None, :] - b[..., None, :, :]
        dec = jnp.exp(jnp.where(tri, diff, -jnp.inf))
        att = jnp.einsum('zbhtd,zbhsd,zbhtsd->zbhts', qc, kc, dec)
        o = o_inter + jnp.einsum('zbhts,zbhsv->zbhtv', att, vc)
        bl = b[..., -1:, :]
        S = (jnp.exp(bl)[..., 0, :, None] * S
             + jnp.einsum('zbhsd,zbhsv->zbhdv', kc * jnp.exp(bl - b), vc))
        return S, o

    S0 = jnp.zeros((2, B, N_HEADS, HEAD_DIM, HEAD_DIM), F32)
    _, o = lax.scan(step, S0, (qs, ks, vs, gs))
    o = o.transpose(1, 2, 0, 4, 3, 5).reshape(2, B, N, N_HEADS, HEAD_DIM)
    o = _rmsnorm(_merge_dirs(o), gnorm.reshape(N_HEADS, HEAD_DIM)).reshape(B, N, GW)
    return (o * jax.nn.silu(g)).astype(zC.dtype)


def _rwkv7(zD, mu, w0, w_up, a0, a_up, g_up, k_k, k_a, r_k, ln_w, ln_b):
    B, N, _ = zD.shape
    zf = zD.astype(F32)
    mu = mu.astype(F32)
    zp = jnp.pad(zf, ((0, 0), (1, 1), (0, 0)))
    zf = zf + mu[0] * (zp[:, :-2] - zf) + mu[1] * (zp[:, 2:] - zf)
    r, k, v, wdf, wdb, adf, adb, gd = _split(
        zf, (GW, GW, GW, RW_W_RANK, RW_W_RANK, RW_A_RANK, RW_A_RANK, RW_G_RANK))
    w0, w_up, a0, a_up = w0.astype(F32), w_up.astype(F32), a0.astype(F32), a_up.astype(F32)
    dec_f = jnp.exp(-RW_DECAY_SCALE * jax.nn.sigmoid(w0[0] + jnp.tanh(wdf) @ w_up[0]))
    dec_b = jnp.exp(-RW_DECAY_SCALE * jax.nn.sigmoid(w0[1] + jnp.tanh(wdb) @ w_up[1]))
    a_f = jax.nn.sigmoid(a0[0] + adf @ a_up[0])
    a_b = jax.nn.sigmoid(a0[1] + adb @ a_up[1])
    g = jax.nn.sigmoid(gd) @ g_up.astype(F32)
    r, k, v = _heads(r), _heads(k), _heads(v)
    dec_f, dec_b, a_f, a_b = _heads(dec_f), _heads(dec_b), _heads(a_f), _heads(a_b)
    kk = k * k_k.astype(F32).reshape(N_HEADS, HEAD_DIM)
    kk = kk / jnp.maximum(jnp.sqrt(jnp.sum(kk * kk, axis=-1, keepdims=True)), 1e-12)
    ka = k_a.astype(F32).reshape(N_HEADS, HEAD_DIM)
    kt_f = k * (1.0 + (a_f - 1.0) * ka)
    kt_b = k * (1.0 + (a_b - 1.0) * ka)
    tm = lambda t: t.transpose(2, 0, 1, 3, 4)
    xs = (tm(_bidir(r, r)), tm(_bidir(dec_f, dec_b)), tm(_bidir(kk, kk)),
          tm(_bidir(a_f, a_b)), tm(_bidir(kt_f, kt_b)), tm(_bidir(v, v)))

    def step(S, inp):
        r_t, w_t, kk_t, a_t, k_t, v_t = inp
        sa = jnp.einsum('zbhvk,zbhk->zbhv', S, -kk_t)
        S = (S * w_t[..., None, :] + sa[..., :, None] * (kk_t * a_t)[..., None, :]
             + v_t[..., :, None] * k_t[..., None, :])
        return S, jnp.einsum('zbhvk,zbhk->zbhv', S, r_t)

    S0 = jnp.zeros((2, B, N_HEADS, HEAD_DIM, HEAD_DIM), F32)
    _, o = lax.scan(step, S0, xs)
    o = _merge_dirs(o.transpose(1, 2, 0, 3, 4))
    o = _layernorm(o, ln_w.reshape(N_HEADS, HEAD_DIM), ln_b.reshape(N_HEADS, HEAD_DIM), RW_GN_EPS)
    bonus = jnp.sum(r * k * r_k.astype(F32).reshape(N_HEADS, HEAD_DIM), axis=-1, keepdims=True) * v
    return ((o + bonus).reshape(B, N, GW) * g).astype(zD.dtype)


def _expert_choice_ffn(x, w_router, w1, w3, w2):
    B, N, D = x.shape
    T = B * N
    xt = x.reshape(T, D)
    aff = jax.nn.softmax((xt @ w_router).astype(F32), axis=-1)
    cap = CAPACITY * T // N_EXPERTS
    gate, idx = lax.top_k(aff.T, cap)

    def body(y, e_in):
        w1e, w3e, w2e, ie, ge = e_in
        xe = xt[ie]
        he = jax.nn.silu(xe @ w1e) * (xe @ w3e)
        return y.at[ie].add((he @ w2e) * ge[:, None].astype(x.dtype)), None

    y, _ = lax.scan(body, jnp.zeros_like(xt), (w1, w3, w2, idx, gate))
    return y.reshape(B, N, D)


def _layer(x, p_i, i, lb_i, prm):
    z = x @ prm['w_in'][i]
    zA, zB, zC, zD = _split(z, (A_COLS, B_COLS, C_COLS, D_COLS))
    oA = _mla(zA, prm['mla_gq'][i], prm['mla_gkv'][i], prm['mla_wuq'][i],
              prm['mla_wuk'][i], prm['mla_wuv'][i])
    oB = _neighbourhood_attention(zB, prm['na_bias'][i])
    oC = _hgrn2(zC, lb_i, prm['hg_gnorm'][i])
    oD = _rwkv7(zD, prm['rw_mu'][i], prm['rw_w0'][i], prm['rw_w_up'][i], prm['rw_a0'][i],
                prm['rw_a_up'][i], prm['rw_g_up'][i], prm['rw_kk'][i], prm['rw_ka'][i],
                prm['rw_rk'][i], prm['rw_ln_w'][i], prm['rw_ln_b'][i])
    mix = jnp.concatenate([oA, oB, oC, oD], axis=-1) @ prm['w_out'][i]
    x = _layernorm(ALPHA * x + mix, prm['ln1_g'][i], prm['ln1_b'][i])
    u = ALPHA * x + _expert_choice_ffn(x, prm['moe_router'][i], prm['moe_w1'][i],
                                       prm['moe_w3'][i], prm['moe_w2'][i])
    ple = jax.nn.sigmoid(u @ prm['ple_gate'][i]) * (p_i @ prm['ple_proj'][i])
    return _layernorm(u + ple, prm['ln2_g'][i], prm['ln2_b'][i])


def _trunk(x, p, lb, prm):
    for i in range(DEPTH):
        x = _layer(x, p[i], i, lb[i], prm)
    return x


def setup_inputs(seed: int = 0) -> dict:
    key = jax.random.key(seed)
    ks = iter(jax.random.split(key, 64))

    def nrm(shape, scale):
        return jax.random.normal(next(ks), shape, F32) * scale

    def gain(shape):
        return 1.0 + nrm(shape, 0.02)

    L = DEPTH
    return {
        'x_prompt': nrm((BATCH, SEQ, D_MODEL), 1.0),
        'x_sample': nrm((DEC_BATCH, DEC_SEQ, D_MODEL), 1.0),
        'p_prompt': nrm((DEPTH, BATCH, SEQ, D_PLE), 1.0),
        'p_sample': nrm((DEPTH, DEC_BATCH, DEC_SEQ, D_PLE), 1.0),
        'w_in': nrm((L, D_MODEL, D_IN), D_MODEL ** -0.5),
        'mla_gq': gain((L, MLA_Q_RANK)),
        'mla_gkv': gain((L, MLA_KV_RANK)),
        'mla_wuq': nrm((L, MLA_Q_RANK, N_HEADS * (MLA_NOPE_DIM + MLA_ROPE_DIM)), MLA_Q_RANK ** -0.5),
        'mla_wuk': nrm((L, MLA_KV_RANK, N_HEADS * MLA_NOPE_DIM), MLA_KV_RANK ** -0.5),
        'mla_wuv': nrm((L, MLA_KV_RANK, N_HEADS * MLA_V_DIM), MLA_KV_RANK ** -0.5),
        'na_bias': nrm((L, N_HEADS, 2 * NA_KH_MAX - 1, 2 * NA_KW - 1), 0.1),
        'hg_lb': nrm((L, 2, GW), 0.1),
        'hg_gnorm': gain((L, GW)),
        'rw_mu': jax.random.uniform(next(ks), (L, 2, D_COLS), F32, 0.0, 0.5),
        'rw_w0': nrm((L, 2, GW), 0.5),
        'rw_w_up': nrm((L, 2, RW_W_RANK, GW), 0.5 * RW_W_RANK ** -0.5),
        'rw_a0': nrm((L, 2, GW), 0.5),
        'rw_a_up': nrm((L, 2, RW_A_RANK, GW), 0.5 * RW_A_RANK ** -0.5),
        'rw_g_up': nrm((L, RW_G_RANK, GW), RW_G_RANK ** -0.5),
        'rw_kk': 0.85 + nrm((L, GW), 0.02),
        'rw_ka': gain((L, GW)),
        'rw_rk': nrm((L, GW), 0.1),
        'rw_ln_w': gain((L, GW)),
        'rw_ln_b': nrm((L, GW), 0.02),
        'w_out': nrm((L, D_MODEL, D_MODEL), BETA * D_MODEL ** -0.5),
        'ln1_g': gain((L, D_MODEL)),
        'ln1_b': nrm((L, D_MODEL), 0.02),
        'moe_router': nrm((L, D_MODEL, N_EXPERTS), D_MODEL ** -0.5),
        'moe_w1': nrm((L, N_EXPERTS, D_MODEL, D_EXPERT), D_MODEL ** -0.5),
        'moe_w3': nrm((L, N_EXPERTS, D_MODEL, D_EXPERT), D_MODEL ** -0.5),
        'moe_w2': nrm((L, N_EXPERTS, D_EXPERT, D_MODEL), BETA * D_EXPERT ** -0.5),
        'ln2_g': gain((L, D_MODEL)),
        'ln2_b': nrm((L, D_MODEL), 0.02),
        'ple_gate': nrm((L, D_MODEL, D_MODEL), D_MODEL ** -0.5),
        'ple_proj': nrm((L, D_PLE, D_MODEL), BETA * D_PLE ** -0.5),
    }


def reference(x_prompt, x_sample, p_prompt, p_sample, w_in, mla_gq, mla_gkv, mla_wuq, mla_wuk,
              mla_wuv, na_bias, hg_lb, hg_gnorm, rw_mu, rw_w0, rw_w_up, rw_a0, rw_a_up, rw_g_up,
              rw_kk, rw_ka, rw_rk, rw_ln_w, rw_ln_b, w_out, ln1_g, ln1_b, moe_router, moe_w1,
              moe_w3, moe_w2, ln2_g, ln2_b, ple_gate, ple_proj):
    prm = dict(w_in=w_in, mla_gq=mla_gq, mla_gkv=mla_gkv, mla_wuq=mla_wuq, mla_wuk=mla_wuk,
               mla_wuv=mla_wuv, na_bias=na_bias, hg_gnorm=hg_gnorm, rw_mu=rw_mu, rw_w0=rw_w0,
               rw_w_up=rw_w_up, rw_a0=rw_a0, rw_a_up=rw_a_up, rw_g_up=rw_g_up, rw_kk=rw_kk,
               rw_ka=rw_ka, rw_rk=rw_rk, rw_ln_w=rw_ln_w, rw_ln_b=rw_ln_b, w_out=w_out,
               ln1_g=ln1_g, ln1_b=ln1_b, moe_router=moe_router, moe_w1=moe_w1, moe_w3=moe_w3,
               moe_w2=moe_w2, ln2_g=ln2_g, ln2_b=ln2_b, ple_gate=ple_gate, ple_proj=ple_proj)
    sm = jax.nn.softmax(hg_lb.astype(F32), axis=0)
    lb = jnp.cumsum(sm, axis=0) - sm[0]
    y_prompt = _trunk(x_prompt, p_prompt, lb, prm)
    y_sample = _trunk(x_sample, p_sample, lb, prm)
    return (y_prompt, y_sample)
```

```python
from contextlib import contextmanager, ExitStack
import math
import numpy as np
import concourse.bass as bass
import concourse.mybir as mybir
from concourse.bass_utils import run_bass_kernel_spmd

F32 = mybir.dt.float32
BF16 = mybir.dt.bfloat16
AF = mybir.ActivationFunctionType
ALU = mybir.AluOpType
AX = mybir.AxisListType

NCORES = 8
T = 12288
UNIT = 4096
NT = T // 128
D = 1024
DEPTH = 2
ALPHA = (2 * DEPTH) ** 0.25
NEG = -30000.0
FM_ROWS = 3200
D_IN = 3616
RW_DECAY = math.exp(-0.5)


class Buf:
    __slots__ = ("name", "w", "r", "dsem", "dcnt", "t", "psum")

    def __init__(self, t, name):
        self.t = t
        self.psum = False
        self.name = name
        self.w = None
        self.r = {}
        self.dsem = None
        self.dcnt = 0

    def __getitem__(self, idx):
        return self.t[idx]


class Eng:
    def __init__(self, h, sem, name):
        self.h = h
        self.sem = sem
        self.n = 0
        self.seen = {}
        self.name = name


class KB:
    def __init__(self, nc):
        self.nc = nc
        self.es = ExitStack()
        self.engs = {}
        for nm, h in (("pe", nc.tensor), ("act", nc.scalar), ("dve", nc.vector),
                      ("pool", nc.gpsimd), ("sp", nc.sync)):
            self.engs[nm] = Eng(h, nc.alloc_semaphore(name="es_" + nm), nm)
        self.bufs = []
        self.dsems = []
        self.ninst = 0
        self.rr = 0

    def sb(self, name, shape, dtype=F32, stack=None):
        t = (stack or self.es).enter_context(self.nc.sbuf_tensor(name, list(shape), dtype))
        b = Buf(t, name)
        self.bufs.append(b)
        return b

    def ps(self, name, shape, dtype=F32, stack=None):
        t = (stack or self.es).enter_context(self.nc.psum_tensor(name, list(shape), dtype))
        b = Buf(t, name)
        b.psum = True
        self.bufs.append(b)
        return b

    def drop(self, bufs):
        ids = set(id(b) for b in bufs)
        self.bufs = [b for b in self.bufs if id(b) not in ids]

    def track(self, name):
        b = Buf(None, name)
        self.bufs.append(b)
        return b

    def _waits(self, e, reads, writes, skip_self=False):
        need = {}

        def req(ev):
            if ev is None:
                return
            sem, val = ev
            k = id(sem)
            if k not in need or need[k][1] < val:
                need[k] = (sem, val)

        for b in reads:
            req(b.w)
            if b.psum:
                for ev in b.r.values():
                    if ev[0] is not e.sem:
                        req(ev)
        for b in writes:
            req(b.w)
            for ev in b.r.values():
                req(ev)
        for k, (sem, val) in need.items():
            if skip_self and sem is e.sem:
                continue
            if e.seen.get(k, 0) >= val:
                continue
            e.h.wait_ge(sem, val)
            e.seen[k] = val

    def op(self, eng, fn, reads=(), writes=()):
        e = self.engs[eng]
        self._waits(e, reads, writes, skip_self=(eng == "pe"))
        ins = fn(e.h)
        e.n += 1
        ins.then_inc(e.sem, 1)
        self.ninst += 1
        ev = (e.sem, e.n)
        for b in reads:
            b.r[id(ev[0])] = ev
        for b in writes:
            b.w = ev
            b.r = {}
        return ins

    def dma(self, q, out_ap, in_ap, reads=(), writes=(), owner=None, **kw):
        e = self.engs[q]
        self._waits(e, reads, writes)
        owner = owner or (list(writes) + list(reads))[0]
        if owner.dsem is None:
            owner.dsem = self.nc.alloc_semaphore(name="ds_%d" % len(self.dsems))
            self.dsems.append(owner.dsem)
        owner.dcnt += 16
        e.h.dma_start(out=out_ap, in_=in_ap, **kw).then_inc(owner.dsem, 16)
        self.ninst += 1
        ev = (owner.dsem, owner.dcnt)
        for b in reads:
            b.r[id(ev[0])] = ev
        for b in writes:
            b.w = ev
            b.r = {}

    def sync_all(self):
        sp = self.engs["sp"]
        for b in self.bufs:
            if b.dsem is not None and b.dcnt > 0:
                if sp.seen.get(id(b.dsem), 0) < b.dcnt:
                    sp.h.wait_ge(b.dsem, b.dcnt)
        self.nc.all_engine_barrier()
        for e in self.engs.values():
            sp.h.sem_clear(e.sem)
        for s in self.dsems:
            sp.h.sem_clear(s)
        self.nc.all_engine_barrier()
        for e in self.engs.values():
            e.n = 0
            e.seen = {}
        for b in self.bufs:
            b.w = None
            b.r = {}
            b.dcnt = 0

    @contextmanager
    def loop(self, n):
        self.sync_all()
        with self.nc.Fori(0, n) as i:
            yield i
            self.sync_all()

    def uloop(self, n, every=1):
        self.sync_all()
        for i in range(n):
            yield i
            if (i + 1) % every == 0:
                self.sync_all()

    @contextmanager
    def scope(self):
        st = ExitStack()
        n0 = len(self.bufs)
        try:
            yield st
        finally:
            self.sync_all()
            self.bufs = self.bufs[:n0]
            st.close()

    def mm(self, out, lhsT, rhs, start=True, stop=True, reads=(), writes=()):
        return self.op("pe", lambda h: h.matmul(out, lhsT, rhs, start=start, stop=stop),
                       reads=reads, writes=writes)

    def tr(self, out, in_, ident, reads=(), writes=()):
        return self.op("pe", lambda h: h.transpose(out, in_, ident), reads=reads, writes=writes)

    def act(self, out, in_, func, reads=(), writes=(), bias=None, scale=None):
        kw = {}
        if bias is not None:
            kw["bias"] = bias
        if scale is not None:
            kw["scale"] = scale
        return self.op("act", lambda h: h.activation(out, in_, func, **kw), reads=reads, writes=writes)

    def copy(self, eng, out, in_, reads=(), writes=()):
        if eng == "act":
            return self.op("act", lambda h: h.copy(out, in_), reads=reads, writes=writes)
        return self.op(eng, lambda h: h.tensor_copy(out, in_), reads=reads, writes=writes)

    def copy_rr(self, out, in_, reads=(), writes=(), engs=("act", "dve")):
        self.rr += 1
        return self.copy(engs[self.rr % len(engs)], out, in_, reads=reads, writes=writes)

    def tt(self, eng, out, in0, in1, op, reads=(), writes=()):
        return self.op(eng, lambda h: h.tensor_tensor(out, in0, in1, op), reads=reads, writes=writes)

    def ts(self, eng, out, in0, s1, s2, op0, op1=None, reads=(), writes=()):
        if op1 is None:
            return self.op(eng, lambda h: h.tensor_scalar(out, in0, s1, None, op0), reads=reads, writes=writes)
        return self.op(eng, lambda h: h.tensor_scalar(out, in0, s1, s2, op0, op1), reads=reads, writes=writes)

    def stt(self, eng, out, in0, scalar, in1, op0, op1, reads=(), writes=()):
        return self.op(eng, lambda h: h.scalar_tensor_tensor(out, in0, scalar, in1, op0, op1),
                       reads=reads, writes=writes)

    def memset(self, eng, ap, val, writes=()):
        return self.op(eng, lambda h: h.memset(ap, val), writes=writes)


A_COLS, B_COLS, C_COLS, D_COLS = 416, 768, 1280, 1152
B0, C0, D0 = 416, 416 + 768, 416 + 768 + 1280


def fm_col_index():
    idx = []
    idx += list(range(0, 256))
    idx += list(range(256, 384))
    idx += list(range(384, 416))
    idx += list(range(400, 416)) + list(range(384, 400))
    idx += list(range(384, 416)) + list(range(384, 416))
    idx += list(range(B0, B0 + 256))
    idx += list(range(B0 + 256, B0 + 512))
    idx += list(range(C0, C0 + 256))
    idx += list(range(C0 + 256, C0 + 512))
    idx += list(range(C0 + 512, C0 + 768))
    idx += list(range(C0 + 1024, C0 + 1280))
    idx += list(range(D0, D0 + 1152))
    assert len(idx) == FM_ROWS
    return np.array(idx)


def tm_col_index():
    return np.array(list(range(B0 + 512, B0 + 768)) + list(range(C0 + 768, C0 + 1024)))


R_CQ, R_CKV, R_KR, R_KRS = 0, 256, 384, 416
R_NQ, R_NK = 512, 768
R_HQ, R_HF, R_HB, R_HG = 1024, 1280, 1536, 1792
R_RW = 2048


def core_units(c):
    if c < 4:
        return [("p", c, 0), ("p", c, 1), ("s", c, 0)]
    b = 4 + 3 * (c - 4)
    return [("s", b, 0), ("s", b + 1, 0), ("s", b + 2, 0)]


def gather_tokens(c, xp, xs):
    parts = []
    for kind, i, h in core_units(c):
        if kind == "p":
            parts.append(xp[i, h * UNIT:(h + 1) * UNIT])
        else:
            parts.append(xs[i])
    return np.concatenate(parts, axis=0)


def rope_tables(c):
    pos = np.zeros(T, np.float32)
    for u, (kind, i, h) in enumerate(core_units(c)):
        pos[u * UNIT:(u + 1) * UNIT] = np.arange(UNIT, dtype=np.float32) + (UNIT * h)
    inv_freq = (10000.0 ** (-np.arange(0, 32, 2, dtype=np.float32) / 32)).astype(np.float32)
    ang = (pos[:, None] * inv_freq[None, :]).astype(np.float32)
    cos, sin = np.cos(ang).astype(np.float32).T, np.sin(ang).astype(np.float32).T
    CQ = np.concatenate([np.ones((64, T), np.float32), cos, cos], 0)
    SQ = np.concatenate([np.zeros((64, T), np.float32), -sin, sin], 0)
    return np.ascontiguousarray(CQ), np.ascontiguousarray(SQ)


NA_NKT = 7


def na_tile_plan():
    plan = []
    for qt in range(NT):
        if qt < 64:
            ks = min(max(qt - 3, 0), 64 - NA_NKT)
            if qt < 3:
                tid = qt
            elif 29 <= qt <= 34:
                tid = 4 + (qt - 29)
            elif qt >= 61:
                tid = 10 + (qt - 61)
            else:
                tid = 3
        else:
            rb = qt - 64
            ks = 64 + min(max(rb - 3, 0), 32 - NA_NKT)
            if rb < 3:
                tid = rb
            elif rb >= 29:
                tid = 13 + (rb - 29)
            else:
                tid = 3
        plan.append((ks, tid))
    return plan


N_NA_TAB = 16


def na_tables(c, bias):
    linked = c < 4
    plan = na_tile_plan()
    tabs = {}
    u_ = np.arange(64)
    for qt, (ks, tid) in enumerate(plan):
        if qt < 64:
            if linked:
                rows, qrow0, qunit = 128, 2 * qt, 0
            else:
                rows, qrow0, qunit = 64, 2 * (qt % 32), qt // 32
        else:
            rows, qrow0, qunit = 64, 2 * (qt - 64), 2
        tab = np.full((NA_NKT, 128, 4, 128), NEG, np.float32)
        for m in range(NA_NKT):
            kt = ks + m
            if qt < 64:
                if linked:
                    krow0, kunit = 2 * kt, 0
                else:
                    krow0, kunit = 2 * (kt % 32), kt // 32
            else:
                krow0, kunit = 2 * (kt - 64), 2
            if kunit != qunit:
                continue
            for i in range(2):
                qrow = qrow0 + i
                rstart = min(max(qrow - 4, 0), rows - 8)
                for j in range(2):
                    krow = krow0 + j
                    if not (rstart <= krow < rstart + 8):
                        continue
                    drow = min(max(krow - qrow + 7, 0), 14)
                    cstart = np.clip(u_ - 8, 0, 48)
                    kc = np.arange(64)[:, None]
                    ok = (kc >= cstart[None, :]) & (kc < cstart[None, :] + 16)
                    dcol = np.clip(kc - u_[None, :] + 15, 0, 30)
                    vals = bias[:, drow, :][:, dcol]
                    blk = np.where(ok[None], vals, NEG)
                    tab[m, j * 64:(j + 1) * 64, :, i * 64:(i + 1) * 64] = blk.transpose(1, 0, 2)
        if tid in tabs:
            assert np.array_equal(tabs[tid], tab), ("na table mismatch", qt, tid)
        else:
            tabs[tid] = tab
    out = np.zeros((N_NA_TAB, 128, NA_NKT, 4, 128), np.float32)
    for tid, tab in tabs.items():
        out[tid] = tab.transpose(1, 0, 2, 3)
    return out


PV_SLOTS = [("gq", 2), ("gkv", 1), ("hglb", 8), ("hgn", 2), ("mu", 18), ("w0", 4), ("a0", 4),
            ("kk", 2), ("ka", 2), ("rk", 2), ("lnw", 2), ("lnb", 2), ("ln1g", 8), ("ln1b", 8),
            ("ln2g", 8), ("ln2b", 8), ("link", 1), ("linkbias", 1), ("isP", 3), ("mugd", 2)]
PV_OFF = {}
_o = 0
for _n, _w in PV_SLOTS:
    PV_OFF[_n] = (_o, _w)
    _o += _w
NPV = _o


def _cols(v):
    v = np.asarray(v, np.float32).reshape(-1, 128)
    return v.T


def pack_pvec(c, l, inp):
    pv = np.zeros((128, NPV), np.float32)

    def put(name, arr):
        o, w = PV_OFF[name]
        arr = np.asarray(arr, np.float32)
        assert arr.shape == (128, w), (name, arr.shape)
        pv[:, o:o + w] = arr

    put("gq", _cols(inp["mla_gq"][l]))
    put("gkv", _cols(inp["mla_gkv"][l]))
    put("hglb", np.concatenate([_cols(inp["hg_lb"][ll, z]) for ll in range(2) for z in range(2)], 1))
    put("hgn", _cols(inp["hg_gnorm"][l]))
    put("mu", np.concatenate([_cols(inp["rw_mu"][l, j]) for j in range(2)], 1))
    put("w0", np.concatenate([_cols(inp["rw_w0"][l, z]) for z in range(2)], 1))
    put("a0", np.concatenate([_cols(inp["rw_a0"][l, z]) for z in range(2)], 1))
    put("kk", _cols(inp["rw_kk"][l]))
    put("ka", _cols(inp["rw_ka"][l]))
    put("rk", _cols(inp["rw_rk"][l]))
    put("lnw", _cols(inp["rw_ln_w"][l]))
    put("lnb", _cols(inp["rw_ln_b"][l]))
    put("ln1g", _cols(inp["ln1_g"][l]))
    put("ln1b", _cols(inp["ln1_b"][l]))
    put("ln2g", _cols(inp["ln2_g"][l]))
    put("ln2b", _cols(inp["ln2_b"][l]))
    linked = c < 4
    put("link", np.full((128, 1), 1.0 if linked else 0.0, np.float32))
    put("linkbias", np.full((128, 1), 0.0 if linked else NEG, np.float32))
    isp = np.zeros((128, 3), np.float32)
    if linked:
        isp[:, 0:2] = 1.0
    put("isP", isp)
    put("mugd", np.stack([inp["rw_mu"][l, 0, 1024:1152], inp["rw_mu"][l, 1, 1024:1152]], 1))
    return pv


class Ctx:
    pass


def consts(k, cx):
    nc = k.nc
    cx.identf = k.sb("identf", [128, 128], F32)
    cx.ident = k.sb("ident", [128, 128], BF16)
    cx.ones = k.sb("ones", [128, 128], F32)
    cx.blk = k.sb("blk", [128, 128], F32)
    cx.blkb = k.sb("blkb", [128, 128], BF16)
    k.memset("pool", cx.identf[:], 0.0, writes=[cx.identf])
    k.op("pool", lambda h: h.affine_select(cx.identf[:], cx.identf[:], [[-1, 128]], ALU.not_equal, 1.0,
                                           base=0, channel_multiplier=1), reads=[cx.identf], writes=[cx.identf])
    k.copy("dve", cx.ident[:], cx.identf[:], reads=[cx.identf], writes=[cx.ident])
    k.memset("pool", cx.ones[:], 1.0, writes=[cx.ones])
    k.memset("pool", cx.blk[:], 0.0, writes=[cx.blk])
    k.memset("pool", cx.blk[0:64, 0:64], 1.0, writes=[cx.blk])
    k.memset("pool", cx.blk[64:128, 64:128], 1.0, writes=[cx.blk])
    k.copy("dve", cx.blkb[:], cx.blk[:], reads=[cx.blk], writes=[cx.blkb])
    cx.one_col = k.sb("one_col", [128, 1], F32)
    k.memset("pool", cx.one_col[:], 1.0, writes=[cx.one_col])
    cx.eps6 = k.sb("eps6", [128, 1], F32)
    k.memset("pool", cx.eps6[:], 1e-6, writes=[cx.eps6])
    cx.of_tok = k.track("of_tok")
    cx.eps5 = k.sb("eps5", [128, 1], F32)
    k.memset("pool", cx.eps5[:], 1e-5, writes=[cx.eps5])
    for nm, cmp, sgn in (("m_lt", ALU.is_gt, 1), ("m_le", ALU.is_ge, 1), ("m_gt", ALU.is_gt, -1), ("m_ge", ALU.is_ge, -1)):
        mf = k.sb(nm + "f", [128, 128], F32)
        mb = k.sb(nm, [128, 128], BF16)
        k.memset("pool", mf[:], 1.0, writes=[mf])
        k.op("pool", lambda h, mf=mf, cmp=cmp, sgn=sgn: h.affine_select(mf[:], mf[:], [[sgn, 128]], cmp, 0.0,
                                                                       base=0, channel_multiplier=-sgn),
             reads=[mf], writes=[mf])
        k.copy("dve", mb[:], mf[:], reads=[mf], writes=[mb])
        setattr(cx, nm, mb)
        setattr(cx, nm + "f", mf)
    cx.m_le64 = cx.m_lef
    cx.m_ge64 = cx.m_gef


def load_cast(k, dst, dst_ap, src_ap, shape, stage, q="sp"):
    k.dma(q, stage_ap(stage, shape), src_ap, writes=[stage])
    k.copy_rr(dst_ap, stage_ap(stage, shape), reads=[stage], writes=[dst])


def stage_ap(stage, shape):
    if len(shape) == 2:
        return stage[0:shape[0], 0:shape[1]]
    return stage[0:shape[0], 0:shape[1], 0:shape[2]]


def phase_proj(k, cx, xT, wfm, wtm, zT, vi):
    with k.scope() as st:
        wb = k.sb("p1_wb", [128, 8, FM_ROWS], BF16, st)
        wtb = k.sb("p1_wtb", [128, 8, 512], BF16, st)
        stg = [k.sb("p1_stg%d" % i, [128, 8, 640], F32, st) for i in range(2)]
        wv = wfm.rearrange("(c p) n -> p c n", p=128)
        for j in range(FM_ROWS // 640):
            s = stg[j % 2]
            k.dma("sp", s[:], wv[:, :, j * 640:(j + 1) * 640], writes=[s])
            for c in range(8):
                k.copy_rr(wb[:, c, j * 640:(j + 1) * 640], s[:, c, :], reads=[s], writes=[wb])
        s = stg[1]
        k.dma("sp", s[:, :, 0:512], wtm.rearrange("(c p) n -> p c n", p=128), writes=[s])
        for c in range(8):
            k.copy_rr(wtb[:, c, :], s[:, c, 0:512], reads=[s], writes=[wtb])
        xf = k.sb("p1_xf", [128, 8, 512], F32, st)
        xb = k.sb("p1_xb", [128, 8, 512], BF16, st)
        pss = [k.ps("p1_ps%d" % i, [128, 512], F32, st) for i in range(4)]
        zo = [k.sb("p1_zo%d" % i, [128, 512], F32, st) for i in range(4)]
        xv = xT.rearrange("(c p) t -> p c t", p=128)
        import os
        MODE = int(os.environ.get("P1_MODE", "0"))
        if MODE == 1:
            return
        for i in k.uloop(T // 512):
            k.dma("sp", xf[:], xv[:, :, bass.ts(i, 512)], writes=[xf])
            for c in range(8):
                k.copy_rr(xb[:, c, :], xf[:, c, :], reads=[xf], writes=[xb], engs=("act", "dve", "pool"))
            for mt in range(FM_ROWS // 128):
                ps = pss[mt % 4]
                z = zo[mt % 4]
                for c in range(8):
                    k.mm(ps[:], wb[:, c, mt * 128:(mt + 1) * 128], xb[:, c, :], start=(c == 0), stop=(c == 7),
                         reads=[wb, xb], writes=[ps])
                k.copy_rr(z[:], ps[:], reads=[ps], writes=[z])
                k.dma("sp", zT(mt * 128, (mt + 1) * 128)[:, bass.ts(i, 512)], z[:], reads=[z])
            for tt in range(4 if MODE != 3 else 0):
                ps = pss[tt % 4]
                z = zo[tt % 4]
                for c in range(8):
                    k.mm(ps[:], xb[:, c, tt * 128:(tt + 1) * 128], wtb[:, c, :], start=(c == 0), stop=(c == 7),
                         reads=[wtb, xb], writes=[ps])
                k.copy_rr(z[:], ps[:], reads=[ps], writes=[z])
                k.dma("sp", vi[bass.ds(i * 512 + tt * 128, 128), :], z[:], reads=[z])


def declare_mixer_inputs(nc, cx):
    def din(name, shape, dt=F32):
        return nc.dram_tensor(name, list(shape), dt, kind="ExternalInput").ap()
    cx.wfm = din("wfm", [D, FM_ROWS])
    cx.wtm = din("wtm", [D, 512])
    cx.CQ = din("CQ", [96, T])
    cx.SQ = din("SQ", [96, T])
    cx.wuq = din("wuq", [256, 384])
    cx.wuqs = din("wuqs", [256, 384])
    cx.wuk = din("wuk", [128, 256])
    cx.wuv = din("wuv", [128, 256])
    cx.natab = din("natab", [N_NA_TAB, 128, NA_NKT, 4, 128])
    cx.wup = din("wup", [128, 256])
    cx.aup = din("aup", [128, 256])
    cx.gup = din("gup", [128, 256])
    cx.wout = din("wout", [D, D])
    cx.router = din("router", [D, 16])
    cx.pvec = din("pvec", [128, NPV])
    cx.pvec64 = din("pvec64", [64, NPV64])


def mixer_input_arrays(c, l, inp):
    w_in = inp["w_in"][l]
    wuq = inp["mla_wuq"][l]
    sw = []
    for h in range(4):
        b = h * 96
        sw += list(range(b, b + 64)) + list(range(b + 80, b + 96)) + list(range(b + 64, b + 80))
    CQ, SQ = rope_tables(c)
    return {
        "wfm": np.ascontiguousarray(w_in[:, fm_col_index()]),
        "wtm": np.ascontiguousarray(w_in[:, tm_col_index()]),
        "CQ": CQ, "SQ": SQ,
        "wuq": np.ascontiguousarray(wuq),
        "wuqs": np.ascontiguousarray(wuq[:, np.array(sw)]),
        "wuk": np.ascontiguousarray(inp["mla_wuk"][l]),
        "wuv": np.ascontiguousarray(inp["mla_wuv"][l]),
        "natab": na_tables(c, inp["na_bias"][l]),
        "wup": np.ascontiguousarray(inp["rw_w_up"][l].reshape(128, 256)),
        "aup": np.ascontiguousarray(inp["rw_a_up"][l].reshape(128, 256)),
        "gup": np.ascontiguousarray(inp["rw_g_up"][l]),
        "wout": np.ascontiguousarray(inp["w_out"][l]),
        "router": np.ascontiguousarray(inp["moe_router"][l]),
        "pvec": pack_pvec(c, l, inp),
        "pvec64": pack_pv64(l, inp),
    }


def build_stage_a(upto="all", debug=(), dbg={}, mixers="ABCD", layer=0):
    nc = bass.Bass("TRN2", target_bir_lowering=False)
    cx = Ctx()
    k = KB(nc)
    cx.xT = nc.dram_tensor("xT", [D, T], F32, kind="ExternalInput").ap()
    declare_mixer_inputs(nc, cx)

    def scratch(name, shape, dt=F32):
        kind = "ExternalOutput" if name in debug else "Internal"
        return nc.dram_tensor(name, list(shape), dt, kind=kind).ap()
    zgrp = [(0, 512, scratch("zA", [512, T])), (512, 1024, scratch("zB", [512, T])),
            (1024, 2048, scratch("zC", [1024, T])), (2048, 3200, scratch("zD", [1152, T]))]

    def zT(r0, r1):
        for a, b, ap in zgrp:
            if a <= r0 and r1 <= b:
                return ap[r0 - a:r1 - a, :]
        raise ValueError((r0, r1))
    cx.zT = zT
    cx.vi = scratch("vi", [T, 512])
    cx.oT = scratch("oT", [D, T], BF16)
    cx.x1T = nc.dram_tensor("x1T", [D, T], F32, kind="ExternalOutput").ap()
    cx.aff = nc.dram_tensor("aff", [T, 16], F32, kind="ExternalOutput").ap()
    consts(k, cx)
    cx.pv = k.sb("pv", [128, NPV], F32)
    k.dma("sp", cx.pv[:], cx.pvec, writes=[cx.pv])
    cx.pv64 = k.sb("pv64", [64, NPV64], F32)
    k.dma("sp", cx.pv64[:], cx.pvec64, writes=[cx.pv64])
    phase_proj(k, cx, cx.xT, cx.wfm, cx.wtm, cx.zT, cx.vi)
    def finish():
        k.sync_all()
        dt = k.track("dbgt")
        for name, (src, shape, dtp) in dbg.items():
            o = nc.dram_tensor("dbg_" + name, list(shape), dtp, kind="ExternalOutput").ap()
            k.dma("sp", o, src(cx), writes=[dt], owner=dt)
        k.sync_all()
        return nc, k
    if upto == "P1":
        return finish()
    if "A" in mixers:
        phase_mla(k, cx)
    if upto == "P2":
        return finish()
    if "B" in mixers:
        phase_na(k, cx)
    if upto == "P3":
        return finish()
    if "C" in mixers:
        phase_hgrn(k, cx, layer)
    if upto == "P4":
        return finish()
    if "D" in mixers:
        phase_rwkv(k, cx)
    if upto == "P5":
        return finish()
    phase_outproj(k, cx, cx.xT)
    return finish()


def pvs(cx, name, j=0, n=1):
    o, w = PV_OFF[name]
    return cx.pv[:, o + j:o + j + n]


def phase_mla(k, cx):
    nc = k.nc
    QT = nc.dram_tensor("mla_QT", [4, 96, T], BF16).ap()
    KT = nc.dram_tensor("mla_KT", [4, 96, T], BF16).ap()
    VA = nc.dram_tensor("mla_VA", [T, 4, 65], BF16).ap()
    zT = cx.zT
    with k.scope() as st:
        stg = k.sb("m_stg", [128, 2, 384], F32, st)
        wuq = k.sb("m_wuq", [128, 2, 384], BF16, st)
        wuqs = k.sb("m_wuqs", [128, 2, 384], BF16, st)
        wuk = k.sb("m_wuk", [128, 256], BF16, st)
        wuv = k.sb("m_wuv", [128, 256], BF16, st)
        load_cast(k, wuq, wuq[:], cx.wuq.rearrange("(c p) n -> p c n", p=128), [128, 2, 384], stg)
        load_cast(k, wuqs, wuqs[:], cx.wuqs.rearrange("(c p) n -> p c n", p=128), [128, 2, 384], stg)
        k.dma("sp", stg[:, 0, 0:256], cx.wuk, writes=[stg])
        k.copy("dve", wuk[:], stg[:, 0, 0:256], reads=[stg], writes=[wuk])
        k.dma("sp", stg[:, 1, 0:256], cx.wuv, writes=[stg])
        k.copy("dve", wuv[:], stg[:, 1, 0:256], reads=[stg], writes=[wuv])
        eps = k.sb("m_eps", [128, 1], F32, st)
        k.memset("pool", eps[:], 1e-6, writes=[eps])
        cq = k.sb("m_cq", [128, 2, 512], F32, st)
        ckv = k.sb("m_ckv", [128, 512], F32, st)
        krt = k.sb("m_krt", [128, 512], F32, st)
        krs = k.sb("m_krs", [128, 512], F32, st)
        cqt = k.sb("m_cqt", [96, 512], F32, st)
        sqt = k.sb("m_sqt", [96, 512], F32, st)
        sq = k.sb("m_sq", [128, 2, 512], F32, st)
        rstd = k.sb("m_rstd", [128, 512], F32, st)
        cqn = k.sb("m_cqn", [128, 2, 512], BF16, st)
        ckvn = k.sb("m_ckvn", [128, 512], BF16, st)
        t1 = k.sb("m_t1", [96, 512], F32, st)
        t2 = k.sb("m_t2", [96, 512], F32, st)
        qo = k.sb("m_qo", [96, 4, 512], BF16, st)
        ko = k.sb("m_ko", [96, 4, 512], BF16, st)
        vo = k.sb("m_vo", [128, 4, 4, 65], BF16, st)
        p_ms = k.ps("m_pms", [128, 512], F32, st)
        p_qa = k.ps("m_pqa", [96, 512], F32, st)
        p_qb = k.ps("m_pqb", [96, 512], F32, st)
        p_k = k.ps("m_pk", [64, 512], F32, st)
        p_v = k.ps("m_pv", [128, 256], F32, st)
        k.memset("pool", vo[:], 1.0, writes=[vo])
        for i in k.uloop(T // 512):
            tsl = bass.ts(i, 512)
            k.dma("sp", cq[:], zT(R_CQ, R_CQ + 256).rearrange("(c p) t -> p c t", p=128)[:, :, tsl], writes=[cq])
            k.dma("sp", ckv[:], zT(R_CKV, R_CKV + 128)[:, tsl], writes=[ckv])
            k.dma("sp", krt[64:96, :], zT(R_KR, R_KR + 32)[:, tsl], writes=[krt])
            k.dma("sp", krs[64:96, :], zT(R_KRS, R_KRS + 32)[:, tsl], writes=[krs])
            k.dma("sp", cqt[:], cx.CQ[:, tsl], writes=[cqt])
            k.dma("sp", sqt[:], cx.SQ[:, tsl], writes=[sqt])
            k.act(sq[:], cq[:], AF.Square, reads=[cq], writes=[sq])
            for c in range(2):
                k.mm(p_ms[:], cx.ones[:], sq[:, c, :], start=(c == 0), stop=(c == 1), reads=[cx.ones, sq], writes=[p_ms])
            k.act(rstd[:], p_ms[:], AF.Sqrt, bias=eps[:], scale=1.0 / 256, reads=[p_ms, eps], writes=[rstd])
            k.op("dve", lambda h: h.reciprocal(rstd[:], rstd[:]), reads=[rstd], writes=[rstd])
            for c in range(2):
                k.stt("dve", cqn[:, c, :], cq[:, c, :], pvs(cx, "gq", c), rstd[:], ALU.mult, ALU.mult,
                      reads=[cq, rstd, cx.pv], writes=[cqn])
            for h in range(4):
                for c in range(2):
                    k.mm(p_qa[:], wuq[:, c, h * 96:(h + 1) * 96], cqn[:, c, :], start=(c == 0), stop=(c == 1),
                         reads=[wuq, cqn], writes=[p_qa])
                for c in range(2):
                    k.mm(p_qb[:], wuqs[:, c, h * 96:(h + 1) * 96], cqn[:, c, :], start=(c == 0), stop=(c == 1),
                         reads=[wuqs, cqn], writes=[p_qb])
                k.tt("dve", t1[:], p_qa[:], cqt[:], ALU.mult, reads=[p_qa, cqt], writes=[t1])
                k.tt("dve", t2[:], p_qb[:], sqt[:], ALU.mult, reads=[p_qb, sqt], writes=[t2])
                k.tt("pool", qo[:, h, :], t1[:], t2[:], ALU.add, reads=[t1, t2], writes=[qo])
            k.dma("sp", QT.rearrange("h r t -> r h t")[:, :, tsl], qo[:], reads=[qo])
            k.act(sq[:, 0, :], ckv[:], AF.Square, reads=[ckv], writes=[sq])
            k.mm(p_ms[:], cx.ones[:], sq[:, 0, :], reads=[cx.ones, sq], writes=[p_ms])
            k.act(rstd[:], p_ms[:], AF.Sqrt, bias=eps[:], scale=1.0 / 128, reads=[p_ms, eps], writes=[rstd])
            k.op("dve", lambda h: h.reciprocal(rstd[:], rstd[:]), reads=[rstd], writes=[rstd])
            k.stt("dve", ckvn[:], ckv[:], pvs(cx, "gkv"), rstd[:], ALU.mult, ALU.mult,
                  reads=[ckv, rstd, cx.pv], writes=[ckvn])
            k.tt("dve", t1[64:96, :], krt[64:96, :], cqt[64:96, :], ALU.mult, reads=[krt, cqt], writes=[t1])
            k.tt("dve", t2[64:96, :], krs[64:96, :], sqt[64:96, :], ALU.mult, reads=[krs, sqt], writes=[t2])
            for h in range(4):
                k.mm(p_k[:], wuk[:, h * 64:(h + 1) * 64], ckvn[:], reads=[wuk, ckvn], writes=[p_k])
                k.copy("act", ko[0:64, h, :], p_k[:], reads=[p_k], writes=[ko])
                k.tt("pool", ko[64:96, h, :], t1[64:96, :], t2[64:96, :], ALU.add, reads=[t1, t2], writes=[ko])
            k.dma("sp", KT.rearrange("h r t -> r h t")[:, :, tsl], ko[:], reads=[ko])
            for tt in range(4):
                k.mm(p_v[:], ckvn[:, tt * 128:(tt + 1) * 128], wuv[:], reads=[ckvn, wuv], writes=[p_v])
                k.copy("dve", vo[:, tt, :, 0:64], p_v[:].rearrange("p (h d) -> p h d", h=4), reads=[p_v], writes=[vo])
            k.dma("sp", VA.rearrange("(n p) h d -> p n h d", p=128)[:, bass.ts(i, 4), :, :], vo[:], reads=[vo])
    scale = 96 ** -0.5
    with k.scope() as st:
        kts = k.sb("a_kts", [96, 4, 2 * UNIT], BF16, st)
        va = k.sb("a_va", [128, 64, 4, 65], BF16, st)
        qs = [k.sb("a_q%d" % j, [96, 4, 512], BF16, st) for j in range(2)]
        pb = [k.sb("a_p%d" % j, [128, 512], BF16, st) for j in range(3)]
        osb = [k.sb("a_osb%d" % j, [65, 512], F32, st) for j in range(2)]
        rec = [k.sb("a_rec%d" % j, [64, 512], F32, st) for j in range(2)]
        ob = [k.sb("a_ob%d" % j, [64, 512], BF16, st) for j in range(2)]
        sel = k.sb("a_sel", [65, 64], F32, st)
        k.memset("pool", sel[:], 0.0, writes=[sel])
        k.memset("pool", sel[64:65, :], 1.0, writes=[sel])
        p_s = [k.ps("a_ps%d" % j, [128, 512], F32, st) for j in range(3)]
        p_o = [k.ps("a_po%d" % j, [65, 512], F32, st) for j in range(2)]
        p_r = k.ps("a_pr", [64, 512], F32, st)
        KTv = KT.rearrange("h r t -> r h t")
        VAv = VA.rearrange("(n p) h d -> p n h d", p=128)
        cnt = 0
        for grp in ((0, 1), (2,)):
            nku = len(grp)
            t0 = grp[0] * UNIT
            for h in range(4):
                k.dma("sp", kts[:, h, 0:nku * UNIT], KTv[:, h, t0:t0 + nku * UNIT], writes=[kts])
            for u_ in range(nku):
                k.dma("sp", va[:, u_ * 32:(u_ + 1) * 32, :, :], VAv[:, (t0 // 128) + u_ * 32:(t0 // 128) + (u_ + 1) * 32, :, :],
                      writes=[va])
            for uq in grp:
                for qb in range(UNIT // 512):
                    q = qs[cnt % 2]
                    cnt += 1
                    tq = uq * UNIT + qb * 512
                    k.dma("sp", q[:], QT.rearrange("h r t -> r h t")[:, :, tq:tq + 512], writes=[q])
                    for h in range(4):
                        po = p_o[h % 2]
                        nk = nku * 32
                        for kt in range(nk):
                            ps_ = p_s[kt % 3]
                            pp = pb[kt % 3]
                            k.mm(ps_[:], kts[:, h, kt * 128:(kt + 1) * 128], q[:, h, :], reads=[kts, q], writes=[ps_])
                            same = (grp[kt // 32] == uq)
                            if same:
                                k.act(pp[:], ps_[:], AF.Exp, scale=scale, reads=[ps_], writes=[pp])
                            else:
                                k.act(pp[:], ps_[:], AF.Exp, scale=scale, bias=pvs(cx, "linkbias"),
                                      reads=[ps_, cx.pv], writes=[pp])
                            k.mm(po[:], va[:, kt, h, :], pp[:], start=(kt == 0), stop=(kt == nk - 1),
                                 reads=[va, pp], writes=[po])
                        o_ = osb[h % 2]
                        k.copy("dve", o_[:], po[:], reads=[po], writes=[o_])
                        k.mm(p_r[:], sel[:], o_[:], reads=[sel, o_], writes=[p_r])
                        r_ = rec[h % 2]
                        k.op("dve", lambda hh, r_=r_: hh.reciprocal(r_[:], p_r[:]), reads=[p_r], writes=[r_])
                        b_ = ob[h % 2]
                        k.tt("pool", b_[:], o_[0:64, :], r_[:], ALU.mult, reads=[o_, r_], writes=[b_])
                        k.dma("sp", cx.oT[h * 64:(h + 1) * 64, tq:tq + 512], b_[:], reads=[b_])


def phase_na(k, cx):
    nc = k.nc
    NQ = nc.dram_tensor("na_Q", [256, T], BF16).ap()
    NK = nc.dram_tensor("na_K", [256, T], BF16).ap()
    NVA = nc.dram_tensor("na_VA", [T, 4, 65], BF16).ap()
    zT = cx.zT
    with k.scope() as st:
        qf = k.sb("n_qf", [128, 4, 512], F32, st)
        qb = k.sb("n_qb", [128, 4, 512], BF16, st)
        vf = k.sb("n_vf", [128, 4, 256], F32, st)
        vo = k.sb("n_vo", [128, 4, 4, 65], BF16, st)
        k.memset("pool", vo[:], 1.0, writes=[vo])
        for i in k.uloop(T // 512):
            tsl = bass.ts(i, 512)
            k.dma("sp", qf[:, 0:2, :], zT(R_NQ, R_NQ + 256).rearrange("(c p) t -> p c t", p=128)[:, :, tsl], writes=[qf])
            k.dma("sp", qf[:, 2:4, :], zT(R_NK, R_NK + 256).rearrange("(c p) t -> p c t", p=128)[:, :, tsl], writes=[qf])
            k.dma("sp", vf[:], cx.vi.rearrange("(n p) c -> p n c", p=128)[:, bass.ts(i, 4), 0:256], writes=[vf])
            k.copy("act", qb[:, 0:2, :], qf[:, 0:2, :], reads=[qf], writes=[qb])
            k.copy("dve", qb[:, 2:4, :], qf[:, 2:4, :], reads=[qf], writes=[qb])
            for tt in range(4):
                k.copy("pool", vo[:, tt, :, 0:64], vf[:, tt, :].rearrange("p (h d) -> p h d", h=4), reads=[vf], writes=[vo])
            k.dma("sp", NQ.rearrange("(c p) t -> p c t", p=128)[:, :, tsl], qb[:, 0:2, :], reads=[qb])
            k.dma("sp", NK.rearrange("(c p) t -> p c t", p=128)[:, :, tsl], qb[:, 2:4, :], reads=[qb])
            k.dma("sp", NVA.rearrange("(n p) h d -> p n h d", p=128)[:, bass.ts(i, 4), :, :], vo[:], reads=[vo])
    plan = na_tile_plan()
    import os
    NAM = int(os.environ.get("NA_MODE", "0"))
    if NAM == 1:
        return
    with k.scope() as st:
        nq = k.sb("n_q", [64, 4, UNIT], BF16, st)
        nk_ = k.sb("n_k", [64, 4, 40 * 128], BF16, st)
        va = k.sb("n_va", [128, 40, 4, 65], BF16, st)
        tab = k.sb("n_tab", [128, NA_NKT, 4, 128], F32, st)
        tt_ = [k.sb("n_t%d" % j, [128, 4, 128], F32, st) for j in range(2)]
        pp = [k.sb("n_p%d" % j, [128, 4, 128], BF16, st) for j in range(2 * NA_NKT)]
        osb = [k.sb("n_osb%d" % j, [65, 4, 128], F32, st) for j in range(2)]
        rec = [k.sb("n_rec%d" % j, [64, 4, 128], F32, st) for j in range(2)]
        ob = [k.sb("n_ob%d" % j, [64, 4, 128], BF16, st) for j in range(2)]
        sel = k.sb("n_sel", [65, 64], F32, st)
        k.memset("pool", sel[:], 0.0, writes=[sel])
        k.memset("pool", sel[64:65, :], 1.0, writes=[sel])
        p_s = [k.ps("n_ps%d" % j, [128, 4, 128], F32, st) for j in range(3)]
        p_o = [k.ps("n_po%d" % j, [65, 4, 128], F32, st) for j in range(2)]
        p_r = k.ps("n_pr", [64, 4, 128], F32, st)
        cur_tid = -1
        for un in range(3):
            q0 = un * 32
            g0 = min(plan[qt][0] for qt in range(q0, q0 + 32))
            g1 = max(plan[qt][0] for qt in range(q0, q0 + 32)) + NA_NKT
            gn = g1 - g0
            assert gn <= 40
            k.dma("sp", nq[:, :, :], NQ.rearrange("(h d) t -> d h t", d=64)[:, :, q0 * 128:(q0 + 32) * 128], writes=[nq])
            k.dma("sp", nk_[:, :, 0:gn * 128], NK.rearrange("(h d) t -> d h t", d=64)[:, :, g0 * 128:g1 * 128], writes=[nk_])
            k.dma("sp", va[:, 0:gn, :, :], NVA.rearrange("(n p) h d -> p n h d", p=128)[:, g0:g1, :, :], writes=[va])
            for qt in range(q0, q0 + 32):
                ks, tid = plan[qt]
                if tid != cur_tid:
                    k.dma("sp", tab[:], cx.natab[tid], writes=[tab])
                    cur_tid = tid
                ql = (qt - q0) * 128
                po = p_o[qt % 2]
                if NAM == 4:
                    continue
                for m in range(NA_NKT):
                    kl = (ks + m - g0) * 128
                    ps_ = p_s[m % 3]
                    for h in range(4):
                        k.mm(ps_[:, h, :], nk_[:, h, kl:kl + 128], nq[:, h, ql:ql + 128],
                             reads=[nk_, nq], writes=[ps_])
                    t_ = tt_[m % 2]
                    k.stt("dve", t_[:], ps_[:], 0.125, tab[:, m, :, :], ALU.mult, ALU.add, reads=[ps_, tab], writes=[t_])
                    p_ = pp[(qt % 2) * NA_NKT + m]
                    k.act(p_[:], t_[:], AF.Exp, reads=[t_], writes=[p_])
                for h in range(4 if NAM != 5 else 0):
                    for m in range(NA_NKT):
                        p_ = pp[(qt % 2) * NA_NKT + m]
                        k.mm(po[:, h, :], va[:, ks + m - g0, h, :], p_[:, h, :], start=(m == 0), stop=(m == NA_NKT - 1),
                             reads=[va, p_], writes=[po])
                if NAM == 5:
                    continue
                o_ = osb[qt % 2]
                k.copy("act", o_[:], po[:], reads=[po], writes=[o_])
                k.mm(p_r[:].rearrange("p h q -> p (h q)"), sel[:], o_[:].rearrange("p h q -> p (h q)"), reads=[sel, o_], writes=[p_r])
                r_ = rec[qt % 2]
                k.op("dve", lambda hh, r_=r_: hh.reciprocal(r_[:], p_r[:]), reads=[p_r], writes=[r_])
                b_ = ob[qt % 2]
                k.tt("pool", b_[:], o_[0:64, :, :], r_[:], ALU.mult, reads=[o_, r_], writes=[b_])
                if NAM == 2:
                    continue
                if NAM == 3:
                    for h in range(4):
                        k.dma("sp", cx.oT[256 + h * 64:256 + (h + 1) * 64, qt * 128:(qt + 1) * 128], b_[:, h, :], reads=[b_])
                    continue
                k.dma("sp", cx.oT[256:512, qt * 128:(qt + 1) * 128].rearrange("(h d) t -> d h t", h=4), b_[:], reads=[b_])


PV64_SLOTS = [("hglb", 16), ("hgn", 4), ("rmu", 32), ("w0", 8), ("a0", 8), ("kk", 4), ("ka", 4), ("rk", 4),
              ("lnw", 4), ("lnb", 4)]
PV64_OFF = {}
_o = 0
for _n, _w in PV64_SLOTS:
    PV64_OFF[_n] = (_o, _w)
    _o += _w
NPV64 = _o


def _hcols(v):
    return np.asarray(v, np.float32).reshape(-1, 64).T


def pack_pv64(l, inp):
    pv = np.zeros((64, NPV64), np.float32)

    def put(name, arr):
        o, w = PV64_OFF[name]
        assert arr.shape == (64, w), (name, arr.shape)
        pv[:, o:o + w] = arr
    put("hglb", np.concatenate([_hcols(inp["hg_lb"][ll, z]) for ll in range(2) for z in range(2)], 1))
    put("hgn", _hcols(inp["hg_gnorm"][l]))
    mu = inp["rw_mu"][l]
    put("rmu", np.concatenate([_hcols(mu[j, 0:1024]) for j in range(2)], 1))
    put("w0", np.concatenate([_hcols(inp["rw_w0"][l, z]) for z in range(2)], 1))
    put("a0", np.concatenate([_hcols(inp["rw_a0"][l, z]) for z in range(2)], 1))
    for nm, key in (("kk", "rw_kk"), ("ka", "rw_ka"), ("rk", "rw_rk"), ("lnw", "rw_ln_w"), ("lnb", "rw_ln_b")):
        put(nm, _hcols(inp[key][l]))
    return pv


def p64(cx, name, j=0, n=1):
    o, w = PV64_OFF[name]
    return cx.pv64[:, o + j:o + j + n]


def bc(ap, shape):
    return ap.to_broadcast(list(shape))


def phase_hgrn(k, cx, layer):
    nc = k.nc
    OF = nc.dram_tensor("hg_of", [256, T], F32).ap()
    zT = cx.zT
    hd = lambda r0: zT(r0, r0 + 256).rearrange("(h d) t -> d h t", d=64)
    vview = cx.vi.rearrange("(n s) c -> s n c", s=64)
    with k.scope() as st:
        rst = k.sb("h_rst", [64, 4, 128], F32, st)
        k.memset("pool", rst[:], 1.0, writes=[rst])
        k.memset("pool", rst[:, :, 0:1], 0.0, writes=[rst])
        k.memset("pool", rst[:, :, 64:65], 0.0, writes=[rst])
        lb = k.sb("h_lb", [64, 2, 4], F32, st)
        oml = k.sb("h_oml", [64, 2, 4], F32, st)
        if layer == 0:
            k.memset("pool", lb[:], 0.0, writes=[lb])
        else:
            o, _ = PV64_OFF["hglb"]
            k.tt("dve", lb[:].rearrange("p z h -> p (z h)"), cx.pv64[:, o + 8:o + 16], cx.pv64[:, o:o + 8], ALU.subtract,
                 reads=[cx.pv64], writes=[lb])
            k.act(lb[:], lb[:], AF.Sigmoid, reads=[lb], writes=[lb])
        k.ts("dve", oml[:], lb[:], -1.0, 1.0, ALU.mult, ALU.add, reads=[lb], writes=[oml])
        ones64 = k.sb("h_ones", [64, 64], F32, st)
        k.memset("pool", ones64[:], 1.0, writes=[ones64])
        S = [k.sb("h_S%d" % z, [64, 4, 64], F32, st) for z in range(2)]
        Sin = [k.sb("h_Sin%d" % z, [64, 4, 64], BF16, st) for z in range(2)]
        NB = 2
        def mk(name, shape, dt=F32):
            return [k.sb("h_%s%d" % (name, j), shape, dt, st) for j in range(NB)]
        qf, pf, gf = mk("qf", [64, 4, 128]), mk("pf", [64, 4, 128]), mk("gf", [64, 4, 128])
        vf = mk("vf", [64, 2, 256])
        vb = mk("vb", [64, 2, 4, 64], BF16)
        e_, A_, B_, gl, r_, kk_ = (mk(n, [64, 4, 128]) for n in ("e", "A", "B", "gl", "r", "kk"))
        b_, arg, E1 = mk("b", [64, 4, 128]), mk("arg", [64, 4, 128]), mk("E1", [64, 4, 128])
        qt, kt = mk("qt", [64, 4, 128], BF16), mk("kt", [64, 4, 128], BF16)
        cc = mk("cc", [64, 4, 2, 3])
        ec = mk("ec", [64, 4, 2, 3])
        ktT = mk("ktT", [64, 8, 64], BF16)
        attm = mk("attm", [64, 8, 64], BF16)
        tmp = mk("tmp", [64, 4, 64])
        osb = mk("osb", [64, 4, 128])
        ofw = mk("ofw", [64, 4, 128])
        sq = mk("sq", [64, 4, 128])
        rs = mk("rs", [64, 4, 128])
        ob = mk("ob", [64, 4, 128], BF16)
        p_att = [k.ps("h_patt%d" % j, [64, 8, 64], F32, st) for j in range(2)]
        p_tr = k.ps("h_ptr", [64, 8, 64], BF16, st)
        p_o = [k.ps("h_po%d" % j, [64, 8, 64], F32, st) for j in range(2)]
        p_kv = k.ps("h_pkv", [64, 4, 64], F32, st)
        p_n = k.ps("h_pn", [64, 4, 128], F32, st)
        for z in range(2):
            k.memset("pool", S[z][:], 0.0, writes=[S[z]])
        it = 0
        for step in range(NT):
            for z in range(2):
                ti = step if z == 0 else NT - 1 - step
                j = it % NB
                it += 1
                t0 = ti * 128
                if ti % 32 == (0 if z == 0 else 31) and step > 0:
                    first_of = ti // 32
                    if (z == 0 and first_of == 1) or (z == 1 and first_of == 0):
                        k.ts("dve", S[z][:], S[z][:], pvs(cx, "link")[0:64, :], None, ALU.mult, reads=[S[z], cx.pv], writes=[S[z]])
                    else:
                        k.memset("pool", S[z][:], 0.0, writes=[S[z]])
                k.dma("sp", qf[j][:], hd(R_HQ)[:, :, t0:t0 + 128], writes=[qf[j]])
                k.dma("sp", pf[j][:], hd(R_HF if z == 0 else R_HB)[:, :, t0:t0 + 128], writes=[pf[j]])
                k.dma("sp", vf[j][:], vview[:, 2 * ti:2 * ti + 2, 256:512], writes=[vf[j]])
                k.copy("pool", vb[j][:].rearrange("s c h d -> s c (h d)"), vf[j][:], reads=[vf[j]], writes=[vb[j]])
                lbz = bc(lb[:, z, :].rearrange("p (h o) -> p h o", o=1), [64, 4, 128])
                omz = bc(oml[:, z, :].rearrange("p (h o) -> p h o", o=1), [64, 4, 128])
                k.act(e_[j][:], pf[j][:], AF.Exp, scale=-1.0, reads=[pf[j]], writes=[e_[j]])
                k.act(A_[j][:], e_[j][:], AF.Ln, bias=cx.one_col[0:64, :], reads=[e_[j], cx.one_col], writes=[A_[j]])
                k.tt("dve", B_[j][:], e_[j][:], lbz, ALU.mult, reads=[e_[j], lb], writes=[B_[j]])
                k.act(B_[j][:], B_[j][:], AF.Ln, bias=cx.one_col[0:64, :], reads=[B_[j], cx.one_col], writes=[B_[j]])
                k.tt("dve", gl[j][:], B_[j][:], A_[j][:], ALU.subtract, reads=[B_[j], A_[j]], writes=[gl[j]])
                k.act(r_[j][:], A_[j][:], AF.Exp, scale=-1.0, reads=[A_[j]], writes=[r_[j]])
                k.tt("pool", kk_[j][:], e_[j][:], r_[j][:], ALU.mult, reads=[e_[j], r_[j]], writes=[kk_[j]])
                k.tt("pool", kk_[j][:], kk_[j][:], omz, ALU.mult, reads=[kk_[j], oml], writes=[kk_[j]])
                flat = lambda t_: t_[:].rearrange("p h t -> p (h t)")
                k.op("dve", lambda hh, j=j: hh.tensor_tensor_scan(flat(b_[j]), flat(rst), flat(gl[j]), 0.0, ALU.mult, ALU.add),
                     reads=[rst, gl[j]], writes=[b_[j]])
                b4 = b_[j][:].rearrange("p h (c t) -> p h c t", c=2)
                if z == 1:
                    k.tt("dve", arg[j][:], gl[j][:], b_[j][:], ALU.subtract, reads=[gl[j], b_[j]], writes=[arg[j]])
                    k.copy("pool", cc[j][:, :, :, 1:2], b4[:, :, :, 63:64], reads=[b_[j]], writes=[cc[j]])
                    k.tt("dve", b4, arg[j][:].rearrange("p h (c t) -> p h c t", c=2), bc(cc[j][:, :, :, 1:2], [64, 4, 2, 64]), ALU.add,
                         reads=[arg[j], cc[j]], writes=[b_[j]])
                    k.copy("pool", cc[j][:, :, :, 0:1], b4[:, :, :, 32:33], reads=[b_[j]], writes=[cc[j]])
                else:
                    k.copy("pool", cc[j][:, :, :, 1:2], b4[:, :, :, 63:64], reads=[b_[j]], writes=[cc[j]])
                    k.copy("pool", cc[j][:, :, :, 0:1], b4[:, :, :, 31:32], reads=[b_[j]], writes=[cc[j]])
                k.tt("pool", cc[j][:, :, :, 2:3], cc[j][:, :, :, 1:2], cc[j][:, :, :, 0:1], ALU.subtract, reads=[cc[j]], writes=[cc[j]])
                k.act(ec[j][:], cc[j][:], AF.Exp, reads=[cc[j]], writes=[ec[j]])
                k.tt("dve", arg[j][:].rearrange("p h (c t) -> p h c t", c=2), b4, bc(cc[j][:, :, :, 0:1], [64, 4, 2, 64]), ALU.subtract,
                     reads=[b_[j], cc[j]], writes=[arg[j]])
                k.act(E1[j][:], arg[j][:], AF.Exp, reads=[arg[j]], writes=[E1[j]])
                k.tt("dve", qt[j][:], qf[j][:], E1[j][:], ALU.mult, reads=[qf[j], E1[j]], writes=[qt[j]])
                k.act(E1[j][:], arg[j][:], AF.Exp, scale=-1.0, reads=[arg[j]], writes=[E1[j]])
                k.tt("dve", kt[j][:], kk_[j][:], E1[j][:], ALU.mult, reads=[kk_[j], E1[j]], writes=[kt[j]])
                pa = p_att[it % 2]
                for h in range(4):
                    for c in range(2):
                        sl = slice(c * 64, (c + 1) * 64)
                        k.mm(pa[:, h * 2 + c, :], kt[j][:, h, sl], qt[j][:, h, sl], reads=[kt[j], qt[j]], writes=[pa])
                msk = cx.m_le64 if z == 0 else cx.m_ge64
                k.tt("dve", attm[j][:], pa[:], bc(msk[0:64, 0:64].rearrange("p (o t) -> p o t", o=1), [64, 8, 64]), ALU.mult,
                     reads=[pa, msk], writes=[attm[j]])
                for h in range(4):
                    for c in range(2):
                        sl = slice(c * 64, (c + 1) * 64)
                        k.tr(p_tr[:, h * 2 + c, :], kt[j][:, h, sl], cx.ident[0:64, 0:64], reads=[kt[j], cx.ident], writes=[p_tr])
                k.copy("act", ktT[j][:], p_tr[:], reads=[p_tr], writes=[ktT[j]])
                po = p_o[it % 2]
                for c in ((0, 1) if z == 0 else (1, 0)):
                    sl = slice(c * 64, (c + 1) * 64)
                    k.tt("dve", Sin[z][:], S[z][:], bc(ec[j][:, :, c, 0:1], [64, 4, 64]), ALU.mult, reads=[S[z], ec[j]], writes=[Sin[z]])
                    for h in range(4):
                        k.mm(po[:, h * 2 + c, :], vb[j][:, c, h, :], attm[j][:, h * 2 + c, :], start=True, stop=False,
                             reads=[vb[j], attm[j]], writes=[po])
                        k.mm(po[:, h * 2 + c, :], Sin[z][:, h, :], qt[j][:, h, sl], start=False, stop=True,
                             reads=[Sin[z], qt[j]], writes=[po])
                    for h in range(4):
                        k.mm(p_kv[:, h, :], ktT[j][:, h * 2 + c, :], vb[j][:, c, h, :], reads=[ktT[j], vb[j]], writes=[p_kv])
                    k.tt("dve", tmp[j][:], p_kv[:], bc(ec[j][:, :, c, 2:3], [64, 4, 64]), ALU.mult, reads=[p_kv, ec[j]], writes=[tmp[j]])
                    k.tt("dve", S[z][:], S[z][:], bc(ec[j][:, :, c, 1:2], [64, 4, 64]), ALU.mult, reads=[S[z], ec[j]], writes=[S[z]])
                    k.tt("dve", S[z][:], S[z][:], tmp[j][:], ALU.add, reads=[S[z], tmp[j]], writes=[S[z]])
                k.copy("act", osb[j][:].rearrange("p h (c t) -> p (h c) t", c=2), po[:], reads=[po], writes=[osb[j]])
                ofv = OF.rearrange("(h d) t -> d h t", d=64)[:, :, t0:t0 + 128]
                if step < NT // 2:
                    k.dma("sp", ofv, osb[j][:], reads=[osb[j]], writes=[cx.of_tok], owner=osb[j])
                else:
                    k.dma("sp", ofw[j][:], ofv, reads=[cx.of_tok], writes=[ofw[j]])
                    k.tt("dve", osb[j][:], osb[j][:], ofw[j][:], ALU.add, reads=[osb[j], ofw[j]], writes=[osb[j]])
                    k.act(sq[j][:], osb[j][:], AF.Square, reads=[osb[j]], writes=[sq[j]])
                    k.mm(p_n[:].rearrange("p h t -> p (h t)"), ones64[:], sq[j][:].rearrange("p h t -> p (h t)"),
                         reads=[ones64, sq[j]], writes=[p_n])
                    k.act(rs[j][:], p_n[:], AF.Sqrt, bias=cx.eps6[0:64, :], scale=1.0 / 64, reads=[p_n, cx.eps6], writes=[rs[j]])
                    k.op("dve", lambda hh, j=j: hh.reciprocal(rs[j][:], rs[j][:]), reads=[rs[j]], writes=[rs[j]])
                    k.tt("dve", osb[j][:], osb[j][:], rs[j][:], ALU.mult, reads=[osb[j], rs[j]], writes=[osb[j]])
                    k.tt("pool", osb[j][:], osb[j][:], bc(p64(cx, "hgn", 0, 4).rearrange("p (h o) -> p h o", o=1), [64, 4, 128]), ALU.mult,
                         reads=[osb[j], cx.pv64], writes=[osb[j]])
                    k.dma("sp", gf[j][:], hd(R_HG)[:, :, t0:t0 + 128], writes=[gf[j]])
                    k.act(sq[j][:], gf[j][:], AF.Exp, scale=-1.0, reads=[gf[j]], writes=[sq[j]])
                    k.ts("dve", sq[j][:], sq[j][:], 1.0, None, ALU.add, reads=[sq[j]], writes=[sq[j]])
                    k.op("dve", lambda hh, j=j: hh.reciprocal(sq[j][:], sq[j][:]), reads=[sq[j]], writes=[sq[j]])
                    k.tt("pool", sq[j][:], sq[j][:], gf[j][:], ALU.mult, reads=[sq[j], gf[j]], writes=[sq[j]])
                    k.tt("dve", ob[j][:], osb[j][:], sq[j][:], ALU.mult, reads=[osb[j], sq[j]], writes=[ob[j]])
                    k.dma("sp", cx.oT[512:768, t0:t0 + 128].rearrange("(h d) t -> d h t", d=64), ob[j][:], reads=[ob[j]])


def phase_rwkv(k, cx):
    import os
    RWM = int(os.environ.get("RW_MODE", "0"))
    nc = k.nc
    OF = nc.dram_tensor("rw_of", [256, T], F32).ap()
    zT = cx.zT
    hd = lambda r0: zT(r0, r0 + 256).rearrange("(h d) t -> d h t", d=64)
    with k.scope() as st:
        NB = 2
        def mk(name, shape, dt=F32, nb=NB):
            return [k.sb("r_%s%d" % (name, j), shape, dt, st) for j in range(nb)]
        def one(name, shape, dt=F32):
            return k.sb("r_" + name, shape, dt, st)
        stg = one("stg", [128, 256], F32)
        wup = one("wup", [64, 2, 256], BF16)
        aup = one("aup", [64, 2, 256], BF16)
        gup = one("gup", [128, 256], BF16)
        for z in range(2):
            k.dma("sp", stg[0:64, :], cx.wup[z * 64:(z + 1) * 64, :], writes=[stg])
            k.copy("dve", wup[:, z, :], stg[0:64, :], reads=[stg], writes=[wup])
            k.dma("sp", stg[0:64, :], cx.aup[z * 64:(z + 1) * 64, :], writes=[stg])
            k.copy("dve", aup[:, z, :], stg[0:64, :], reads=[stg], writes=[aup])
        k.dma("sp", stg[:], cx.gup, writes=[stg])
        k.copy("dve", gup[:], stg[:], reads=[stg], writes=[gup])
        rst = one("rst", [64, 4, 128], F32)
        k.memset("pool", rst[:], 1.0, writes=[rst])
        k.memset("pool", rst[:, :, 0:1], 0.0, writes=[rst])
        ones64 = one("ones", [64, 64], F32)
        k.memset("pool", ones64[:], 1.0, writes=[ones64])
        omka = one("omka", [64, 4], F32)
        k.ts("dve", omka[:], p64(cx, "ka", 0, 4), -1.0, 1.0, ALU.mult, ALU.add, reads=[cx.pv64], writes=[omka])
        identb = one("identb", [128, 4, 128], BF16)
        for h in range(4):
            k.copy("dve", identb[:, h, :], cx.identf[:], reads=[cx.identf], writes=[identb])
        epsln = one("epsln", [64, 1], F32)
        k.memset("pool", epsln[:], 64e-5, writes=[epsln])
        H = [one("H%d" % z, [64, 4, 64], F32) for z in range(2)]
        Hb = [one("Hb%d" % z, [64, 4, 64], BF16) for z in range(2)]
        for z in range(2):
            k.memset("pool", H[z][:], 0.0, writes=[H[z]])
            k.memset("pool", Hb[z][:], 0.0, writes=[Hb[z]])
        X = {n: mk("x" + n, [64, 4, 130]) for n in ("r", "k", "v", "l")}
        xg = mk("xg", [128, 130])
        F = {n: mk("f" + n, [64, 4, 128]) for n in ("r", "k", "v", "l")}
        d0, d1 = mk("d0", [64, 4, 128]), mk("d1", [64, 4, 128])
        fg = mk("fg", [128, 128])
        sgb = mk("sgb", [128, 128], BF16)
        lin = mk("lin", [64, 2, 128], BF16)
        lw, av = mk("lw", [64, 4, 128]), mk("av", [64, 4, 128])
        t1, t2 = mk("t1", [64, 4, 128]), mk("t2", [64, 4, 128])
        kkn, ktt, bb = mk("kkn", [64, 4, 128]), mk("ktt", [64, 4, 128]), mk("bb", [64, 4, 128])
        cl = mk("cl", [64, 4, 128])
        Ep, Em, Ek, El = (mk(n, [64, 4, 128]) for n in ("Ep", "Em", "Ek", "El"))
        gL = mk("gL", [64, 4, 1])
        RgT, KKgT, BiT, KTiT, XT, YT, VT = (mk(n, [64, 4, 128], BF16) for n in ("RgT", "KKgT", "BiT", "KTiT", "XT", "YT", "VT"))
        Pm = [mk("P%d" % i, [128, 4, 128], BF16, 1)[0] for i in range(2)]
        PmT = [mk("PT%d" % i, [128, 4, 128], BF16, 1)[0] for i in range(2)]
        TT = [mk("TT%d" % i, [128, 4, 128], BF16, 1)[0] for i in range(2)]
        AktT, BbT, BktT = (mk(n, [128, 4, 128], BF16, 1)[0] for n in ("AktT", "BbT", "BktT"))
        W2 = one("W2", [128, 4, 128], BF16)
        VTM = one("VTM", [128, 4, 64], BF16)
        XTM = one("XTM", [128, 4, 64], BF16)
        YTM = one("YTM", [128, 4, 64], BF16)
        TKT = one("TKT", [128, 4, 128], BF16)
        nTAV = one("nTAV", [128, 4, 64], BF16)
        McT = one("McT", [64, 4, 64], BF16)
        RhT = one("RhT", [64, 4, 128], BF16)
        osb, ofw, o2 = mk("osb", [64, 4, 128]), mk("ofw", [64, 4, 128]), mk("o2", [64, 4, 128])
        ob = mk("ob", [64, 4, 128], BF16)
        pA = [k.ps("r_pA%d" % j, [128, 4, 128], F32, st) for j in range(3)]
        p_t = k.ps("r_pt", [128, 4, 64], BF16, st)
        p_u = k.ps("r_pu", [128, 4, 128], F32, st)
        p_s = k.ps("r_ps", [64, 4, 64], F32, st)
        p_o = [k.ps("r_po%d" % j, [64, 4, 128], F32, st) for j in range(2)]
        pac = [0]

        def nextA():
            pac[0] += 1
            return pA[pac[0] % 3]
        mu = lambda jmu, off: bc(p64(cx, "rmu", jmu * 16 + off, 4).rearrange("p (h o) -> p h o", o=1), [64, 4, 128])
        b4 = lambda name: bc(p64(cx, name, 0, 4).rearrange("p (h o) -> p h o", o=1), [64, 4, 128])
        it = 0
        for step in range(NT):
            if step % 8 == 0 and step > 0:
                k.sync_all()
            for z in range(2):
                ti = step if z == 0 else NT - 1 - step
                j = it % NB
                it += 1
                t0 = ti * 128
                second = step >= NT // 2
                if ti % 32 == (0 if z == 0 else 31) and step > 0:
                    first_of = ti // 32
                    if (z == 0 and first_of == 1) or (z == 1 and first_of == 0):
                        k.ts("dve", H[z][:], H[z][:], pvs(cx, "link")[0:64, :], None, ALU.mult, reads=[H[z], cx.pv], writes=[H[z]])
                        k.copy("dve", Hb[z][:], H[z][:], reads=[H[z]], writes=[Hb[z]])
                    else:
                        k.memset("pool", H[z][:], 0.0, writes=[H[z]])
                        k.memset("pool", Hb[z][:], 0.0, writes=[Hb[z]])
                lo, hi = max(t0 - 1, 0), min(t0 + 129, T)
                c0, c1 = lo - (t0 - 1), hi - (t0 - 1)
                for n, r0 in (("r", R_RW), ("k", R_RW + 256), ("v", R_RW + 512), ("l", R_RW + 768)):
                    k.dma("sp", X[n][j][:, :, c0:c1], hd(r0)[:, :, lo:hi], writes=[X[n][j]])
                k.dma("sp", xg[j][:, c0:c1], zT(R_RW + 1024, R_RW + 1152)[:, lo:hi], writes=[xg[j]])
                allx = [X[n][j] for n in ("r", "k", "v", "l")] + [xg[j]]
                for side, col in ((0, 0), (1, 129)):
                    at_edge = (ti % 32 == 0) if side == 0 else (ti % 32 == 31)
                    if not at_edge:
                        continue
                    linked = (ti == 32 and side == 0) or (ti == 31 and side == 1)
                    for xb_ in allx:
                        sl = xb_[:, :, col:col + 1] if xb_ is not xg[j] else xb_[:, col:col + 1]
                        npart = 64 if xb_ is not xg[j] else 128
                        if linked:
                            k.ts("dve", sl, sl, pvs(cx, "link")[0:npart, :], None, ALU.mult, reads=[xb_, cx.pv], writes=[xb_])
                        else:
                            k.memset("pool", sl, 0.0, writes=[xb_])
                for qi, n in enumerate(("r", "k", "v", "l")):
                    x_ = X[n][j]
                    e1, e2 = ("dve", "pool") if qi % 2 == 0 else ("pool", "dve")
                    k.tt(e1, d0[j][:], x_[:, :, 0:128], x_[:, :, 1:129], ALU.subtract, reads=[x_], writes=[d0[j]])
                    k.tt(e1, d0[j][:], d0[j][:], mu(0, qi * 4), ALU.mult, reads=[d0[j], cx.pv64], writes=[d0[j]])
                    k.tt(e2, d1[j][:], x_[:, :, 2:130], x_[:, :, 1:129], ALU.subtract, reads=[x_], writes=[d1[j]])
                    k.tt(e2, d1[j][:], d1[j][:], mu(1, qi * 4), ALU.mult, reads=[d1[j], cx.pv64], writes=[d1[j]])
                    k.tt(e1, d0[j][:], d0[j][:], d1[j][:], ALU.add, reads=[d0[j], d1[j]], writes=[d0[j]])
                    k.tt(e1, F[n][j][:], d0[j][:], x_[:, :, 1:129], ALU.add, reads=[d0[j], x_], writes=[F[n][j]])
                if RWM == 1:
                    continue
                rf, kf, vf, lf = F["r"][j], F["k"][j], F["v"][j], F["l"][j]
                k.act(t1[j][:, 0, :], lf[:, z, :], AF.Exp, scale=-2.0, reads=[lf], writes=[t1[j]])
                k.ts("dve", t1[j][:, 0, :], t1[j][:, 0, :], 1.0, None, ALU.add, reads=[t1[j]], writes=[t1[j]])
                k.op("dve", lambda hh, j=j: hh.reciprocal(t1[j][:, 0, :], t1[j][:, 0, :]), reads=[t1[j]], writes=[t1[j]])
                k.ts("dve", lin[j][:, 0, :], t1[j][:, 0, :], 2.0, -1.0, ALU.mult, ALU.add, reads=[t1[j]], writes=[lin[j]])
                k.copy("pool", lin[j][:, 1, :], lf[:, 2 + z, :], reads=[lf], writes=[lin[j]])
                pw = nextA()
                for h in range(4):
                    k.mm(pw[0:64, h, :], wup[:, z, h * 64:(h + 1) * 64], lin[j][:, 0, :], reads=[wup, lin[j]], writes=[pw])
                k.tt("dve", t1[j][:], pw[0:64, :, :], bc(p64(cx, "w0", z * 4, 4).rearrange("p (h o) -> p h o", o=1), [64, 4, 128]), ALU.add,
                     reads=[pw, cx.pv64], writes=[t1[j]])
                pa_ = nextA()
                for h in range(4):
                    k.mm(pa_[0:64, h, :], aup[:, z, h * 64:(h + 1) * 64], lin[j][:, 1, :], reads=[aup, lin[j]], writes=[pa_])
                k.tt("dve", t2[j][:], pa_[0:64, :, :], bc(p64(cx, "a0", z * 4, 4).rearrange("p (h o) -> p h o", o=1), [64, 4, 128]), ALU.add,
                     reads=[pa_, cx.pv64], writes=[t2[j]])
                for src, dst, mulc in ((t1[j], lw[j], -RW_DECAY), (t2[j], av[j], 1.0)):
                    k.act(src[:], src[:], AF.Exp, scale=-1.0, reads=[src], writes=[src])
                    k.ts("dve", src[:], src[:], 1.0, None, ALU.add, reads=[src], writes=[src])
                    k.op("dve", lambda hh, src=src: hh.reciprocal(src[:], src[:]), reads=[src], writes=[src])
                    k.ts("pool", dst[:], src[:], mulc, None, ALU.mult, reads=[src], writes=[dst])
                if RWM == 2:
                    continue
                k.tt("pool", t1[j][:], kf[:], b4("kk"), ALU.mult, reads=[kf, cx.pv64], writes=[t1[j]])
                k.tt("pool", t2[j][:], t1[j][:], t1[j][:], ALU.mult, reads=[t1[j]], writes=[t2[j]])
                pn = nextA()
                k.mm(pn[0:64, :, :].rearrange("p h t -> p (h t)"), ones64[:], t2[j][:].rearrange("p h t -> p (h t)"),
                     reads=[ones64, t2[j]], writes=[pn])
                k.ts("dve", t2[j][:], pn[0:64, :, :], 1e-24, None, ALU.max, reads=[pn], writes=[t2[j]])
                k.act(t2[j][:], t2[j][:], AF.Ln, reads=[t2[j]], writes=[t2[j]])
                k.act(t2[j][:], t2[j][:], AF.Exp, scale=-0.5, reads=[t2[j]], writes=[t2[j]])
                k.tt("dve", kkn[j][:], t1[j][:], t2[j][:], ALU.mult, reads=[t1[j], t2[j]], writes=[kkn[j]])
                k.tt("pool", t1[j][:], av[j][:], b4("ka"), ALU.mult, reads=[av[j], cx.pv64], writes=[t1[j]])
                k.tt("pool", t1[j][:], t1[j][:], bc(omka[:, :].rearrange("p (h o) -> p h o", o=1), [64, 4, 128]), ALU.add,
                     reads=[t1[j], omka], writes=[t1[j]])
                k.tt("pool", ktt[j][:], t1[j][:], kf[:], ALU.mult, reads=[t1[j], kf], writes=[ktt[j]])
                k.tt("dve", bb[j][:], kkn[j][:], av[j][:], ALU.mult, reads=[kkn[j], av[j]], writes=[bb[j]])
                flat = lambda t_: t_[:].rearrange("p h t -> p (h t)")
                k.op("dve", lambda hh, j=j: hh.tensor_tensor_scan(flat(cl[j]), flat(rst), flat(lw[j]), 0.0, ALU.mult, ALU.add),
                     reads=[rst, lw[j]], writes=[cl[j]])
                k.copy("pool", gL[j][:], cl[j][:, :, 127:128], reads=[cl[j]], writes=[gL[j]])
                if z == 1:
                    k.tt("dve", cl[j][:], lw[j][:], cl[j][:], ALU.subtract, reads=[lw[j], cl[j]], writes=[cl[j]])
                    k.tt("dve", cl[j][:], cl[j][:], bc(gL[j][:], [64, 4, 128]), ALU.add, reads=[cl[j], gL[j]], writes=[cl[j]])
                k.act(Ep[j][:], cl[j][:], AF.Exp, reads=[cl[j]], writes=[Ep[j]])
                k.act(Em[j][:], cl[j][:], AF.Exp, scale=-1.0, reads=[cl[j]], writes=[Em[j]])
                k.tt("pool", t1[j][:], cl[j][:], lw[j][:], ALU.subtract, reads=[cl[j], lw[j]], writes=[t1[j]])
                k.act(Ek[j][:], t1[j][:], AF.Exp, reads=[t1[j]], writes=[Ek[j]])
                k.tt("pool", t2[j][:], bc(gL[j][:], [64, 4, 128]), cl[j][:], ALU.subtract, reads=[cl[j], gL[j]], writes=[t2[j]])
                k.act(El[j][:], t2[j][:], AF.Exp, reads=[t2[j]], writes=[El[j]])
                k.act(gL[j][:], gL[j][:], AF.Exp, reads=[gL[j]], writes=[gL[j]])
                k.tt("dve", RgT[j][:], rf[:], Ep[j][:], ALU.mult, reads=[rf, Ep[j]], writes=[RgT[j]])
                k.tt("pool", KKgT[j][:], kkn[j][:], Ek[j][:], ALU.mult, reads=[kkn[j], Ek[j]], writes=[KKgT[j]])
                k.tt("dve", BiT[j][:], bb[j][:], Em[j][:], ALU.mult, reads=[bb[j], Em[j]], writes=[BiT[j]])
                k.tt("pool", KTiT[j][:], ktt[j][:], Em[j][:], ALU.mult, reads=[ktt[j], Em[j]], writes=[KTiT[j]])
                k.tt("dve", XT[j][:], ktt[j][:], El[j][:], ALU.mult, reads=[ktt[j], El[j]], writes=[XT[j]])
                k.tt("pool", YT[j][:], bb[j][:], El[j][:], ALU.mult, reads=[bb[j], El[j]], writes=[YT[j]])
                k.copy("act", VT[j][:], vf[:], reads=[vf], writes=[VT[j]])
                if RWM == 3:
                    continue
                if z == 0:
                    M1, M2, M3 = cx.m_lt, cx.m_gt, cx.m_le
                else:
                    M1, M2, M3 = cx.m_gt, cx.m_lt, cx.m_ge
                mb = lambda m_: bc(m_[:, :].rearrange("p (o t) -> p o t", o=1), [128, 4, 128])
                def pairprod(lhs, rhs, mask, out, neg=False):
                    p_ = nextA()
                    for h in range(4):
                        k.mm(p_[:, h, :], lhs[:, h, :], rhs[:, h, :], reads=[lhs, rhs], writes=[p_])
                    if neg:
                        k.stt("dve", out[:], p_[:], -1.0, mb(mask), ALU.mult, ALU.mult, reads=[p_, mask], writes=[out])
                    else:
                        k.tt("dve", out[:], p_[:], mb(mask), ALU.mult, reads=[p_, mask], writes=[out])
                pairprod(BiT[j], KKgT[j], M1, PmT[0], neg=True)
                pairprod(KKgT[j], BiT[j], M2, Pm[0], neg=True)
                pairprod(KTiT[j], KKgT[j], M1, AktT)
                pairprod(BiT[j], RgT[j], M3, BbT)
                pairprod(KTiT[j], RgT[j], M3, BktT)
                if RWM == 4:
                    continue
                k.tt("pool", TT[0][:], PmT[0][:], identb[:], ALU.add, reads=[PmT[0], identb], writes=[TT[0]])
                cur = 0
                for lev in range(6):
                    nxt = 1 - cur
                    p1 = nextA()
                    for h in range(4):
                        k.mm(p1[:, h, :], PmT[cur][:, h, :], Pm[cur][:, h, :], reads=[PmT[cur], Pm[cur]], writes=[p1])
                    k.copy("act", Pm[nxt][:], p1[:], reads=[p1], writes=[Pm[nxt]])
                    if lev < 5:
                        p2 = nextA()
                        for h in range(4):
                            k.mm(p2[:, h, :], Pm[cur][:, h, :], PmT[cur][:, h, :], reads=[PmT[cur], Pm[cur]], writes=[p2])
                        k.copy("dve", PmT[nxt][:], p2[:], reads=[p2], writes=[PmT[nxt]])
                    p3 = nextA()
                    tcur, tnxt = TT[lev % 2], TT[(lev + 1) % 2]
                    for h in range(4):
                        k.mm(p3[:, h, :], Pm[nxt][:, h, :], tcur[:, h, :], reads=[Pm[nxt], tcur], writes=[p3])
                    k.tt("dve", tnxt[:], p3[:], tcur[:], ALU.add, reads=[p3, tcur], writes=[tnxt])
                    cur = nxt
                if RWM == 5:
                    continue
                TTf = TT[0]
                for src, dst, col in ((KKgT[j], W2, 0), (VT[j], VTM, None), (XT[j], XTM, None), (YT[j], YTM, None)):
                    for h in range(4):
                        k.tr(p_t[:, h, :], src[:, h, :], cx.ident[0:64, 0:64], reads=[src, cx.ident], writes=[p_t])
                    if col is None:
                        k.copy_rr(dst[:], p_t[:], reads=[p_t], writes=[dst])
                    else:
                        k.copy_rr(dst[:, :, 0:64], p_t[:], reads=[p_t], writes=[dst])
                pav = nextA()
                for h in range(4):
                    k.mm(pav[:, h, 0:64], AktT[:, h, :], VTM[:, h, :], reads=[AktT, VTM], writes=[pav])
                k.copy("act", W2[:, :, 64:128], pav[:, :, 0:64], reads=[pav], writes=[W2])
                for h in range(4):
                    k.mm(p_u[:, h, :], TTf[:, h, :], W2[:, h, :], reads=[TTf, W2], writes=[p_u])
                k.copy("act", TKT[:], p_u[:], reads=[p_u], writes=[TKT])
                k.ts("pool", nTAV[:], TKT[:, :, 64:128], -1.0, None, ALU.mult, reads=[TKT], writes=[nTAV])
                if RWM == 6:
                    continue
                for h in range(4):
                    k.mm(p_s[:, h, :], TKT[:, h, 0:64], YTM[:, h, :], reads=[TKT, YTM], writes=[p_s])
                for h in range(4):
                    k.stt("dve", McT[:, h, :], cx.identf[0:64, 0:64], gL[j][:, h, :], p_s[:, h, :], ALU.mult, ALU.subtract,
                          reads=[cx.identf, gL[j], p_s], writes=[McT])
                pr_ = p_o[0]
                for h in range(4):
                    k.mm(pr_[:, h, :], TKT[:, h, 0:64], BbT[:, h, :], reads=[TKT, BbT], writes=[pr_])
                k.tt("dve", RhT[:], RgT[j][:], pr_[:], ALU.subtract, reads=[RgT[j], pr_], writes=[RhT])
                po = p_o[1]
                for h in range(4):
                    k.mm(po[:, h, :], VTM[:, h, :], BktT[:, h, :], start=True, stop=False, reads=[VTM, BktT], writes=[po])
                    k.mm(po[:, h, :], nTAV[:, h, :], BbT[:, h, :], start=False, stop=False, reads=[nTAV, BbT], writes=[po])
                    k.mm(po[:, h, :], Hb[z][:, h, :], RhT[:, h, :], start=False, stop=True, reads=[Hb[z], RhT], writes=[po])
                k.copy("act", osb[j][:], po[:], reads=[po], writes=[osb[j]])
                for h in range(4):
                    k.mm(p_s[:, h, :], XTM[:, h, :], VTM[:, h, :], start=True, stop=False, reads=[XTM, VTM], writes=[p_s])
                    k.mm(p_s[:, h, :], YTM[:, h, :], nTAV[:, h, :], start=False, stop=False, reads=[YTM, nTAV], writes=[p_s])
                    k.mm(p_s[:, h, :], McT[:, h, :], Hb[z][:, h, :], start=False, stop=True, reads=[McT, Hb[z]], writes=[p_s])
                k.copy("dve", H[z][:], p_s[:], reads=[p_s], writes=[H[z]])
                k.copy("act", Hb[z][:], H[z][:], reads=[H[z]], writes=[Hb[z]])
                if RWM == 7:
                    continue
                ofv = OF.rearrange("(h d) t -> d h t", d=64)[:, :, t0:t0 + 128]
                if not second:
                    k.dma("sp", ofv, osb[j][:], reads=[osb[j]], writes=[cx.of_tok], owner=osb[j])
                    continue
                k.dma("sp", ofw[j][:], ofv, reads=[cx.of_tok], writes=[ofw[j]])
                k.tt("dve", osb[j][:], osb[j][:], ofw[j][:], ALU.add, reads=[osb[j], ofw[j]], writes=[osb[j]])
                pm_ = nextA()
                k.mm(pm_[0:64, :, :].rearrange("p h t -> p (h t)"), ones64[:], osb[j][:].rearrange("p h t -> p (h t)"),
                     reads=[ones64, osb[j]], writes=[pm_])
                k.stt("dve", o2[j][:], pm_[0:64, :, :], -1.0 / 64, osb[j][:], ALU.mult, ALU.add, reads=[pm_, osb[j]], writes=[o2[j]])
                k.tt("pool", t1[j][:], o2[j][:], o2[j][:], ALU.mult, reads=[o2[j]], writes=[t1[j]])
                pv_ = nextA()
                k.mm(pv_[0:64, :, :].rearrange("p h t -> p (h t)"), ones64[:], t1[j][:].rearrange("p h t -> p (h t)"),
                     reads=[ones64, t1[j]], writes=[pv_])
                k.ts("dve", t2[j][:], pv_[0:64, :, :], 1.0 / 64, 64e-5, ALU.mult, ALU.add, reads=[pv_], writes=[t2[j]])
                k.act(t2[j][:], t2[j][:], AF.Ln, reads=[t2[j]], writes=[t2[j]])
                k.act(t2[j][:], t2[j][:], AF.Exp, scale=-0.5, reads=[t2[j]], writes=[t2[j]])
                k.tt("dve", o2[j][:], o2[j][:], t2[j][:], ALU.mult, reads=[o2[j], t2[j]], writes=[o2[j]])
                k.tt("pool", o2[j][:], o2[j][:], b4("lnw"), ALU.mult, reads=[o2[j], cx.pv64], writes=[o2[j]])
                k.tt("pool", o2[j][:], o2[j][:], b4("lnb"), ALU.add, reads=[o2[j], cx.pv64], writes=[o2[j]])
                k.tt("dve", t1[j][:], rf[:], kf[:], ALU.mult, reads=[rf, kf], writes=[t1[j]])
                k.tt("pool", t1[j][:], t1[j][:], b4("rk"), ALU.mult, reads=[t1[j], cx.pv64], writes=[t1[j]])
                pb_ = nextA()
                k.mm(pb_[0:64, :, :].rearrange("p h t -> p (h t)"), ones64[:], t1[j][:].rearrange("p h t -> p (h t)"),
                     reads=[ones64, t1[j]], writes=[pb_])
                k.tt("dve", t1[j][:], pb_[0:64, :, :], vf[:], ALU.mult, reads=[pb_, vf], writes=[t1[j]])
                k.tt("pool", o2[j][:], o2[j][:], t1[j][:], ALU.add, reads=[o2[j], t1[j]], writes=[o2[j]])
                k.tt("dve", fg[j][:], xg[j][:, 0:128], xg[j][:, 1:129], ALU.subtract, reads=[xg[j]], writes=[fg[j]])
                k.ts("dve", fg[j][:], fg[j][:], pvs(cx, "mugd", 0), None, ALU.mult, reads=[fg[j], cx.pv], writes=[fg[j]])
                k.tt("dve", fg[j][:], fg[j][:], xg[j][:, 1:129], ALU.add, reads=[fg[j], xg[j]], writes=[fg[j]])
                fg2 = d0[j][:].rearrange("p h t -> p (h t)")
                k.tt("dve", xg[j][:, 0:128], xg[j][:, 2:130], xg[j][:, 1:129], ALU.subtract, reads=[xg[j]], writes=[xg[j]])
                k.stt("dve", fg[j][:], xg[j][:, 0:128], pvs(cx, "mugd", 1), fg[j][:], ALU.mult, ALU.add, reads=[xg[j], fg[j], cx.pv], writes=[fg[j]])
                k.act(fg[j][:], fg[j][:], AF.Exp, scale=-1.0, reads=[fg[j]], writes=[fg[j]])
                k.ts("dve", fg[j][:], fg[j][:], 1.0, None, ALU.add, reads=[fg[j]], writes=[fg[j]])
                k.op("dve", lambda hh, j=j: hh.reciprocal(fg[j][:], fg[j][:]), reads=[fg[j]], writes=[fg[j]])
                k.copy("pool", sgb[j][:], fg[j][:], reads=[fg[j]], writes=[sgb[j]])
                pg_ = nextA()
                for h in range(4):
                    k.mm(pg_[0:64, h, :], gup[:, h * 64:(h + 1) * 64], sgb[j][:], reads=[gup, sgb[j]], writes=[pg_])
                k.tt("dve", ob[j][:], o2[j][:], pg_[0:64, :, :], ALU.mult, reads=[o2[j], pg_], writes=[ob[j]])
                k.dma("sp", cx.oT[768:1024, t0:t0 + 128].rearrange("(h d) t -> d h t", d=64), ob[j][:], reads=[ob[j]])


def layernorm_fm(k, cx, st, hbuf, sq, out, gname, bname, p_m, p_v, tmp_m, tmp_r):
    k.act(sq[:], hbuf[:], AF.Square, reads=[hbuf], writes=[sq])
    for mc in range(8):
        k.mm(p_m[:], cx.ones[:], hbuf[:, mc, :], start=(mc == 0), stop=(mc == 7), reads=[cx.ones, hbuf], writes=[p_m])
    for mc in range(8):
        k.mm(p_v[:], cx.ones[:], sq[:, mc, :], start=(mc == 0), stop=(mc == 7), reads=[cx.ones, sq], writes=[p_v])
    k.op("act", lambda h: h.mul(tmp_m[:], p_m[:], 1.0 / D), reads=[p_m], writes=[tmp_m])
    k.tt("pool", tmp_r[:], tmp_m[:], tmp_m[:], ALU.mult, reads=[tmp_m], writes=[tmp_r])
    k.stt("dve", tmp_r[:], p_v[:], 1.0 / D, tmp_r[:], ALU.mult, ALU.subtract, reads=[p_v, tmp_r], writes=[tmp_r])
    k.act(tmp_r[:], tmp_r[:], AF.Ln, bias=cx.eps5[:], reads=[tmp_r, cx.eps5], writes=[tmp_r])
    k.act(tmp_r[:], tmp_r[:], AF.Exp, scale=-0.5, reads=[tmp_r], writes=[tmp_r])
    for mc in range(8):
        k.tt("pool", sq[:, mc, :], hbuf[:, mc, :], tmp_m[:], ALU.subtract, reads=[hbuf, tmp_m], writes=[sq])
        k.tt("pool" if mc % 2 else "dve", sq[:, mc, :], sq[:, mc, :], tmp_r[:], ALU.mult, reads=[sq, tmp_r], writes=[sq])
        k.ts("dve", out[:, mc, :], sq[:, mc, :], pvs(cx, gname, mc), pvs(cx, bname, mc), ALU.mult, ALU.add,
             reads=[sq, cx.pv], writes=[out])


def phase_outproj(k, cx, xT_res):
    with k.scope() as st:
        stg = k.sb("o_stg", [128, 8, 512], F32, st)
        wb = k.sb("o_wb", [128, 8, D], BF16, st)
        wv = cx.wout.rearrange("(c p) n -> p c n", p=128)
        for jj in range(2):
            k.dma("sp", stg[:], wv[:, :, jj * 512:(jj + 1) * 512], writes=[stg])
            for c in range(8):
                k.copy_rr(wb[:, c, jj * 512:(jj + 1) * 512], stg[:, c, :], reads=[stg], writes=[wb])
        rt = k.sb("o_rt", [128, 8, 16], F32, st)
        k.dma("sp", rt[:], cx.router.rearrange("(c p) e -> p c e", p=128), writes=[rt])
        ob = k.sb("o_ob", [128, 8, 512], BF16, st)
        xf = k.sb("o_xf", [128, 8, 512], F32, st)
        hb = k.sb("o_hb", [128, 8, 512], F32, st)
        sq = k.sb("o_sq", [128, 8, 512], F32, st)
        x1 = k.sb("o_x1", [128, 8, 512], F32, st)
        tm = k.sb("o_tm", [128, 512], F32, st)
        tr_ = k.sb("o_tr", [128, 512], F32, st)
        lg = k.sb("o_lg", [128, 4, 16], F32, st)
        mx = k.sb("o_mx", [128, 4], F32, st)
        sm = k.sb("o_sm", [128, 4], F32, st)
        p_mix = [k.ps("o_pm%d" % j, [128, 512], F32, st) for j in range(2)]
        p_m = k.ps("o_pmean", [128, 512], F32, st)
        p_v = k.ps("o_pvar", [128, 512], F32, st)
        p_l = k.ps("o_pl", [128, 4, 16], F32, st)
        for i in k.uloop(T // 512):
            tsl = bass.ts(i, 512)
            k.dma("sp", ob[:], cx.oT.rearrange("(c p) t -> p c t", p=128)[:, :, tsl], writes=[ob])
            k.dma("sp", xf[:], xT_res.rearrange("(c p) t -> p c t", p=128)[:, :, tsl], writes=[xf])
            for mc in range(8):
                pm = p_mix[mc % 2]
                for c in range(8):
                    k.mm(pm[:], wb[:, c, mc * 128:(mc + 1) * 128], ob[:, c, :], start=(c == 0), stop=(c == 7),
                         reads=[wb, ob], writes=[pm])
                k.stt("dve", hb[:, mc, :], xf[:, mc, :], ALPHA, pm[:], ALU.mult, ALU.add, reads=[xf, pm], writes=[hb])
            layernorm_fm(k, cx, st, hb, sq, x1, "ln1g", "ln1b", p_m, p_v, tm, tr_)
            k.dma("sp", cx.x1T.rearrange("(c p) t -> p c t", p=128)[:, :, tsl], x1[:], reads=[x1])
            for tt in range(4):
                for mc in range(8):
                    k.mm(p_l[:, tt, :], x1[:, mc, tt * 128:(tt + 1) * 128], rt[:, mc, :], start=(mc == 0), stop=(mc == 7),
                         reads=[x1, rt], writes=[p_l])
            k.copy("act", lg[:], p_l[:], reads=[p_l], writes=[lg])
            k.op("dve", lambda h: h.tensor_reduce(mx[:], lg[:], AX.X, ALU.max), reads=[lg], writes=[mx])
            k.tt("dve", lg[:], lg[:], bc(mx[:].rearrange("p (t o) -> p t o", o=1), [128, 4, 16]), ALU.subtract, reads=[lg, mx], writes=[lg])
            k.act(lg[:], lg[:], AF.Exp, reads=[lg], writes=[lg])
            k.op("dve", lambda h: h.tensor_reduce(sm[:], lg[:], AX.X, ALU.add), reads=[lg], writes=[sm])
            k.op("dve", lambda h: h.reciprocal(sm[:], sm[:]), reads=[sm], writes=[sm])
            k.tt("dve", lg[:], lg[:], bc(sm[:].rearrange("p (t o) -> p t o", o=1), [128, 4, 16]), ALU.mult, reads=[lg, sm], writes=[lg])
            k.dma("sp", cx.aff.rearrange("(n p) e -> p n e", p=128)[:, bass.ts(i, 4), :], lg[:], reads=[lg])


TB = 2048
NSB = TB // 512
CAP_P = 2 * 4 * 8192 // 16
CAP_S = 2 * 16 * 4096 // 16
N_BISECT = 30


def phase_thresholds(k, cx, aff_all):
    thr_u = [k.sb("thr_u%d" % u, [128, 16], F32) for u in range(3)]
    with k.scope() as st:
        affP = k.sb("t_affP", [128, 256, 16], F32, st)
        affS = k.sb("t_affS", [128, 512, 16], F32, st)
        cmpb = k.sb("t_cmp", [128, 512, 16], BF16, st)
        for c in range(4):
            k.dma("sp", affP[:, c * 64:(c + 1) * 64, :],
                  aff_all[c * T:c * T + 2 * UNIT, :].rearrange("(p j) e -> p j e", p=128), writes=[affP])
            k.dma("sp", affS[:, c * 32:(c + 1) * 32, :],
                  aff_all[c * T + 2 * UNIT:(c + 1) * T, :].rearrange("(p j) e -> p j e", p=128), writes=[affS])
        for c in range(4, 8):
            k.dma("sp", affS[:, 128 + (c - 4) * 96:128 + (c - 3) * 96, :],
                  aff_all[c * T:(c + 1) * T, :].rearrange("(p j) e -> p j e", p=128), writes=[affS])
        res = []
        for gname, aff, J, cap in (("P", affP, 256, CAP_P), ("S", affS, 512, CAP_S)):
            lo = k.sb("t_lo" + gname, [128, 16], F32, st)
            hi = k.sb("t_hi" + gname, [128, 16], F32, st)
            mid = k.sb("t_mid" + gname, [128, 16], F32, st)
            cnt = k.sb("t_cnt" + gname, [128, 16], F32, st)
            ge = k.sb("t_ge" + gname, [128, 16], F32, st)
            d1 = k.sb("t_d1" + gname, [128, 16], F32, st)
            d2 = k.sb("t_d2" + gname, [128, 16], F32, st)
            p_c = k.ps("t_pc" + gname, [128, 16], F32, st)
            k.memset("pool", lo[:], 0.0, writes=[lo])
            k.memset("pool", hi[:], 1.0, writes=[hi])
            for itn in range(N_BISECT):
                k.tt("dve", mid[:], lo[:], hi[:], ALU.add, reads=[lo, hi], writes=[mid])
                k.ts("dve", mid[:], mid[:], 0.5, None, ALU.mult, reads=[mid], writes=[mid])
                k.tt("dve", cmpb[:, 0:J, :], aff[:, 0:J, :], bc(mid[:].rearrange("p (o e) -> p o e", o=1), [128, J, 16]), ALU.is_gt,
                     reads=[aff, mid], writes=[cmpb])
                with k.nc.allow_low_precision("0/1 counts are exact in bf16 inputs, fp32 accumulate"):
                    k.op("dve", lambda h, J=J, cnt=cnt: h.tensor_reduce(cnt[:], cmpb[:, 0:J, :].rearrange("p j e -> p e j"), AX.X, ALU.add),
                         reads=[cmpb], writes=[cnt])
                k.mm(p_c[:], cx.ones[:], cnt[:], reads=[cx.ones, cnt], writes=[p_c])
                k.ts("dve", ge[:], p_c[:], float(cap), None, ALU.is_ge, reads=[p_c], writes=[ge])
                k.tt("pool", d1[:], mid[:], lo[:], ALU.subtract, reads=[mid, lo], writes=[d1])
                k.tt("pool", d2[:], hi[:], mid[:], ALU.subtract, reads=[mid, hi], writes=[d2])
                k.tt("pool", d1[:], d1[:], ge[:], ALU.mult, reads=[d1, ge], writes=[d1])
                k.tt("pool", d2[:], d2[:], ge[:], ALU.mult, reads=[d2, ge], writes=[d2])
                k.tt("pool", lo[:], lo[:], d1[:], ALU.add, reads=[lo, d1], writes=[lo])
                k.tt("pool", hi[:], mid[:], d2[:], ALU.add, reads=[mid, d2], writes=[hi])
            res.append(lo)
        thrP, thrS = res
        dif = k.sb("t_dif", [128, 16], F32, st)
        k.tt("dve", dif[:], thrP[:], thrS[:], ALU.subtract, reads=[thrP, thrS], writes=[dif])
        for u in range(3):
            k.stt("dve", thr_u[u][:], dif[:], pvs(cx, "isP", u), thrS[:], ALU.mult, ALU.add,
                  reads=[dif, thrS, cx.pv], writes=[thr_u[u]])
    return thr_u


def phase_ffn(k, cx, thr_u, x1T, aff_own, pT, xnT):
    nc = k.nc
    with k.scope() as st:
        stg = [k.sb("f_stg%d" % j, [128, 4096], F32, st) for j in range(2)]
        s8 = lambda b_: b_[:].rearrange("p (c n) -> p c n", c=8)
        s4 = lambda b_: b_[:].rearrange("p (c n) -> p c n", c=4)
        w1b = k.sb("f_w1b", [128, 8, 512], BF16, st)
        w3b = k.sb("f_w3b", [128, 8, 512], BF16, st)
        w2b = k.sb("f_w2b", [128, 4, 1024], BF16, st)
        pg = k.sb("f_pg", [128, 8, D], BF16, st)
        pp = k.sb("f_pp", [128, 2, D], BF16, st)
        for jj in range(2):
            k.dma("sp", s8(stg[0]), cx.ple_gate.rearrange("(c p) n -> p c n", p=128)[:, :, jj * 512:(jj + 1) * 512], writes=[stg[0]])
            for c in range(8):
                k.copy_rr(pg[:, c, jj * 512:(jj + 1) * 512], s8(stg[0])[:, c, :], reads=[stg[0]], writes=[pg])
        k.dma("sp", stg[1][:, 0:2048].rearrange("p (c n) -> p c n", c=2), cx.ple_proj.rearrange("(c p) n -> p c n", p=128), writes=[stg[1]])
        k.copy("dve", pp[:], stg[1][:, 0:2048].rearrange("p (c n) -> p c n", c=2), reads=[stg[1]], writes=[pp])
        sel = k.sb("f_sel", [16, 16, 128], BF16, st)
        with k.scope() as st2:
            self_ = k.sb("f_self", [16, 16, 128], F32, st2)
            k.memset("pool", self_[:], 0.0, writes=[self_])
            k.op("pool", lambda h: h.affine_select(self_[:], self_[:], [[1, 16], [0, 128]], ALU.not_equal, 1.0,
                                                   base=0, channel_multiplier=-1), reads=[self_], writes=[self_])
            k.copy("dve", sel[:], self_[:], reads=[self_], writes=[sel])
        x1b = k.sb("f_x1b", [128, 8, TB], BF16, st)
        yacc = k.sb("f_yacc", [128, 8, TB], F32, st)
        gmT = k.sb("f_gmT", [16, TB], BF16, st)
        afft = k.sb("f_afft", [128, 16], F32, st)
        mk_ = k.sb("f_mk", [128, 16], F32, st)
        G = [k.sb("f_G%d" % j, [128, 512], BF16, st) for j in range(2)]
        s1 = [k.sb("f_s1%d" % j, [128, 512], BF16, st) for j in range(2)]
        t3 = [k.sb("f_t3%d" % j, [128, 512], BF16, st) for j in range(2)]
        ub = k.sb("f_ub", [128, 8, 512], BF16, st)
        he = ub
        pb_ = k.sb("f_pb", [128, 2, 512], BF16, st)
        sg = [k.sb("f_sg%d" % j, [128, 512], F32, st) for j in range(2)]
        tm, tr_ = sg[0], sg[1]
        p_h1 = k.ps("f_ph1", [128, 512], F32, st)
        p_h3 = k.ps("f_ph3", [128, 512], F32, st)
        p_y = [k.ps("f_py%d" % j, [128, 512], F32, st) for j in range(2)]
        p_g = k.ps("f_pg_", [128, 512], F32, st)
        p_t = k.ps("f_pt", [16, 128], F32, st)
        p_m = k.ps("f_pm", [128, 512], F32, st)
        p_v = k.ps("f_pv", [128, 512], F32, st)
        x1v = x1T.rearrange("(c p) t -> p c t", p=128)
        for blk in k.uloop(T // TB):
            tb0 = blk * TB
            un = tb0 // UNIT
            for sb in range(NSB):
                s_ = stg[sb % 2]
                k.dma("sp", s8(s_), x1v[:, :, tb0 + sb * 512:tb0 + (sb + 1) * 512], writes=[s_])
                for c in range(8):
                    k.copy_rr(x1b[:, c, sb * 512:(sb + 1) * 512], s8(s_)[:, c, :], reads=[s_], writes=[x1b], engs=("act", "dve", "pool"))
            for tt in range(TB // 128):
                k.dma("sp", afft[:], aff_own[tb0 + tt * 128:tb0 + (tt + 1) * 128, :], writes=[afft])
                k.tt("dve", mk_[:], afft[:], thr_u[un][:], ALU.is_gt, reads=[afft, thr_u[un]], writes=[mk_])
                k.tt("dve", mk_[:], mk_[:], afft[:], ALU.mult, reads=[mk_, afft], writes=[mk_])
                k.op("pe", lambda h: h.transpose(p_t[:], mk_[:], cx.identf[:]), reads=[mk_, cx.identf], writes=[p_t])
                k.copy("act", gmT[:, tt * 128:(tt + 1) * 128], p_t[:], reads=[p_t], writes=[gmT])
            for e in range(16):
                k.dma("sp", s8(stg[0]), cx.w1[e].rearrange("(c p) n -> p c n", p=128), writes=[stg[0]])
                k.dma("sp", s8(stg[1]), cx.w3[e].rearrange("(c p) n -> p c n", p=128), writes=[stg[1]])
                for c in range(8):
                    k.copy_rr(w1b[:, c, :], s8(stg[0])[:, c, :], reads=[stg[0]], writes=[w1b], engs=("act", "dve", "pool"))
                for c in range(8):
                    k.copy_rr(w3b[:, c, :], s8(stg[1])[:, c, :], reads=[stg[1]], writes=[w3b], engs=("act", "dve", "pool"))
                k.dma("sp", s4(stg[0]), cx.w2[e].rearrange("(c p) n -> p c n", p=128), writes=[stg[0]])
                for c in range(4):
                    k.copy_rr(w2b[:, c, :], s4(stg[0])[:, c, :], reads=[stg[0]], writes=[w2b], engs=("act", "dve", "pool"))
                for sb in range(NSB):
                    ssl = slice(sb * 512, (sb + 1) * 512)
                    g_ = G[sb % 2]
                    k.mm(p_g[:], sel[:, e, :], gmT[:, ssl], reads=[sel, gmT], writes=[p_g])
                    k.copy("act", g_[:], p_g[:], reads=[p_g], writes=[g_])
                    for dc in range(4):
                        for c in range(8):
                            k.mm(p_h1[:], w1b[:, c, dc * 128:(dc + 1) * 128], x1b[:, c, ssl], start=(c == 0), stop=(c == 7),
                                 reads=[w1b, x1b], writes=[p_h1])
                        for c in range(8):
                            k.mm(p_h3[:], w3b[:, c, dc * 128:(dc + 1) * 128], x1b[:, c, ssl], start=(c == 0), stop=(c == 7),
                                 reads=[w3b, x1b], writes=[p_h3])
                        a_, b3 = s1[dc % 2], t3[dc % 2]
                        k.act(a_[:], p_h1[:], AF.Silu, reads=[p_h1], writes=[a_])
                        k.tt("dve", b3[:], p_h3[:], g_[:], ALU.mult, reads=[p_h3, g_], writes=[b3])
                        k.tt("pool", he[:, dc, :], a_[:], b3[:], ALU.mult, reads=[a_, b3], writes=[he])
                    for mc in range(8):
                        py = p_y[mc % 2]
                        for dc in range(4):
                            k.mm(py[:], w2b[:, dc, mc * 128:(mc + 1) * 128], he[:, dc, :], start=(dc == 0), stop=(dc == 3),
                                 reads=[w2b, he], writes=[py])
                        if e == 0:
                            k.copy("dve", yacc[:, mc, ssl], py[:], reads=[py], writes=[yacc])
                        else:
                            k.tt("dve", yacc[:, mc, ssl], yacc[:, mc, ssl], py[:], ALU.add, reads=[yacc, py], writes=[yacc])
            for sb in range(NSB):
                ssl = slice(sb * 512, (sb + 1) * 512)
                t0 = tb0 + sb * 512
                xs = stg[0]
                k.dma("sp", s8(xs), x1v[:, :, t0:t0 + 512], writes=[xs])
                k.dma("sp", stg[1][:, 0:1024].rearrange("p (c n) -> p c n", c=2), pT.rearrange("(c p) t -> p c t", p=128)[:, :, t0:t0 + 512],
                      writes=[stg[1]])
                k.copy("pool", pb_[:], stg[1][:, 0:1024].rearrange("p (c n) -> p c n", c=2), reads=[stg[1]], writes=[pb_])
                for mc in range(8):
                    k.stt("dve", yacc[:, mc, ssl], s8(xs)[:, mc, :], ALPHA, yacc[:, mc, ssl], ALU.mult, ALU.add,
                          reads=[xs, yacc], writes=[yacc])
                    k.copy("act" if mc % 2 else "pool", ub[:, mc, :], yacc[:, mc, ssl], reads=[yacc], writes=[ub])
                for mc in range(8):
                    for c in range(8):
                        k.mm(p_h1[:], pg[:, c, mc * 128:(mc + 1) * 128], ub[:, c, :], start=(c == 0), stop=(c == 7),
                             reads=[pg, ub], writes=[p_h1])
                    for c in range(2):
                        k.mm(p_h3[:], pp[:, c, mc * 128:(mc + 1) * 128], pb_[:, c, :], start=(c == 0), stop=(c == 1),
                             reads=[pp, pb_], writes=[p_h3])
                    s_ = sg[mc % 2]
                    k.act(s_[:], p_h1[:], AF.Sigmoid, reads=[p_h1], writes=[s_])
                    k.tt("dve", s_[:], p_h3[:], s_[:], ALU.mult, reads=[p_h3, s_], writes=[s_])
                    k.tt("pool", yacc[:, mc, ssl], yacc[:, mc, ssl], s_[:], ALU.add, reads=[yacc, s_], writes=[yacc])
                hb = stg[1]
                for mc in range(8):
                    k.copy("act" if mc % 2 else "pool", s8(hb)[:, mc, :], yacc[:, mc, ssl], reads=[yacc], writes=[hb])
                ln_in = BufView(hb, s8(hb))
                ln_sq = BufView(stg[0], s8(stg[0]))
                layernorm_fm(k, cx, st, ln_in, ln_sq, ln_sq, "ln2g", "ln2b", p_m, p_v, tm, tr_)
                k.dma("sp", xnT.rearrange("(c p) t -> p c t", p=128)[:, :, t0:t0 + 512], s8(stg[0]), reads=[stg[0]])


class BufView:
    def __init__(self, parent, ap):
        self._p = parent
        self._ap = ap

    def __getitem__(self, idx):
        return self._ap[idx]

    def __getattr__(self, name):
        return getattr(self._p, name)

    def __setattr__(self, name, val):
        if name in ("_p", "_ap"):
            object.__setattr__(self, name, val)
        else:
            setattr(self._p, name, val)


def declare_ffn_inputs(nc, cx):
    def din(name, shape, dt=F32):
        return nc.dram_tensor(name, list(shape), dt, kind="ExternalInput").ap()
    cx.x1in = din("x1in", [D, T])
    cx.aff_own = din("aff_own", [T, 16])
    cx.aff_all = din("aff_all", [NCORES * T, 16])
    cx.pT = din("pT", [256, T])
    cx.w1 = din("w1", [16, D, 512])
    cx.w3 = din("w3", [16, D, 512])
    cx.w2 = din("w2", [16, 512, D])
    cx.ple_gate = din("ple_gate", [D, D])
    cx.ple_proj = din("ple_proj", [256, D])
    cx.pvecF = din("pvecF", [128, NPV])


def ffn_input_arrays(c, l, inp, pT_c):
    return {
        "pT": pT_c,
        "w1": inp["moe_w1"][l], "w3": inp["moe_w3"][l], "w2": inp["moe_w2"][l],
        "ple_gate": inp["ple_gate"][l], "ple_proj": inp["ple_proj"][l],
        "pvecF": pack_pvec(c, l, inp),
    }


def build_stage(kind, dbg={}):
    nc = bass.Bass("TRN2", target_bir_lowering=False)
    cx = Ctx()
    k = KB(nc)
    consts(k, cx)
    if kind in ("B", "C"):
        declare_ffn_inputs(nc, cx)
        cx.pv = k.sb("pvF", [128, NPV], F32)
        k.dma("sp", cx.pv[:], cx.pvecF, writes=[cx.pv])
        if kind == "B":
            xn = nc.dram_tensor("xn", [D, T], F32).ap()
        else:
            xn = nc.dram_tensor("yT", [D, T], F32, kind="ExternalOutput").ap()
        thr_u = phase_thresholds(k, cx, cx.aff_all)
        phase_ffn(k, cx, thr_u, cx.x1in, cx.aff_own, cx.pT, xn)
        k.sync_all()
    if kind in ("A", "B"):
        if kind == "A":
            cx.xT = nc.dram_tensor("xT", [D, T], F32, kind="ExternalInput").ap()
        else:
            cx.xT = xn
        declare_mixer_inputs(nc, cx)
        zgrp = [(0, 512, nc.dram_tensor("zA", [512, T], F32).ap()), (512, 1024, nc.dram_tensor("zB", [512, T], F32).ap()),
                (1024, 2048, nc.dram_tensor("zC", [1024, T], F32).ap()), (2048, 3200, nc.dram_tensor("zD", [1152, T], F32).ap())]

        def zT(r0, r1):
            for a, b, ap in zgrp:
                if a <= r0 and r1 <= b:
                    return ap[r0 - a:r1 - a, :]
            raise ValueError((r0, r1))
        cx.zT = zT
        cx.vi = nc.dram_tensor("vi", [T, 512], F32).ap()
        cx.oT = nc.dram_tensor("oT", [D, T], BF16).ap()
        cx.x1T = nc.dram_tensor("x1T", [D, T], F32, kind="ExternalOutput").ap()
        cx.aff = nc.dram_tensor("aff", [T, 16], F32, kind="ExternalOutput").ap()
        cx.pv = k.sb("pv", [128, NPV], F32)
        k.dma("sp", cx.pv[:], cx.pvec, writes=[cx.pv])
        cx.pv64 = k.sb("pv64", [64, NPV64], F32)
        k.dma("sp", cx.pv64[:], cx.pvec64, writes=[cx.pv64])
        phase_proj(k, cx, cx.xT, cx.wfm, cx.wtm, cx.zT, cx.vi)
        phase_mla(k, cx)
        phase_na(k, cx)
        phase_hgrn(k, cx, 0 if kind == "A" else 1)
        phase_rwkv(k, cx)
        phase_outproj(k, cx, cx.xT)
    k.sync_all()
    dt = k.track("dbgt")
    for name, (src, shape, dtp) in dbg.items():
        o = nc.dram_tensor("dbg_" + name, list(shape), dtp, kind="ExternalOutput").ap()
        k.dma("sp", o, src(cx), writes=[dt], owner=dt)
    k.sync_all()
    return nc, k


def run_stage(kind, in_maps):
    nc, k = build_stage(kind)
    res = run_bass_kernel_spmd(nc, in_maps, core_ids=list(range(NCORES)))
    return res.results


def kernel(**inp):
    inp = {kk: np.asarray(v) for kk, v in inp.items()}
    xp, xs = inp["x_prompt"], inp["x_sample"]
    pp, ps = inp["p_prompt"], inp["p_sample"]
    maps = []
    for c in range(NCORES):
        m = mixer_input_arrays(c, 0, inp)
        m["xT"] = np.ascontiguousarray(gather_tokens(c, xp, xs).T)
        maps.append(m)
    rA = run_stage("A", maps)
    aff_all = np.ascontiguousarray(np.concatenate([np.asarray(rA[c]["aff"]) for c in range(NCORES)], axis=0))
    maps = []
    for c in range(NCORES):
        pT = np.ascontiguousarray(gather_tokens(c, pp[0], ps[0]).T)
        m = ffn_input_arrays(c, 0, inp, pT)
        m.update(mixer_input_arrays(c, 1, inp))
        m["x1in"] = np.asarray(rA[c]["x1T"])
        m["aff_own"] = np.asarray(rA[c]["aff"])
        m["aff_all"] = aff_all
        maps.append(m)
    del rA
    rB = run_stage("B", maps)
    aff_all = np.ascontiguousarray(np.concatenate([np.asarray(rB[c]["aff"]) for c in range(NCORES)], axis=0))
    maps = []
    for c in range(NCORES):
        pT = np.ascontiguousarray(gather_tokens(c, pp[1], ps[1]).T)
        m = ffn_input_arrays(c, 1, inp, pT)
        m["x1in"] = np.asarray(rB[c]["x1T"])
        m["aff_own"] = np.asarray(rB[c]["aff"])
        m["aff_all"] = aff_all
        maps.append(m)
    del rB
    rC = run_stage("C", maps)
    y_prompt = np.zeros((4, 8192, D), np.float32)
    y_sample = np.zeros((16, 4096, D), np.float32)
    for c in range(NCORES):
        y = np.asarray(rC[c]["yT"]).T
        for u, (kind, i, h) in enumerate(core_units(c)):
            blk = y[u * UNIT:(u + 1) * UNIT]
            if kind == "p":
                y_prompt[i, h * UNIT:(h + 1) * UNIT] = blk
            else:
                y_sample[i] = blk
    return (y_prompt, y_sample)
```

```python
from contextlib import contextmanager, ExitStack
import math
import numpy as np
import concourse.bass as bass
import concourse.mybir as mybir
from concourse.bass_utils import run_bass_kernel_spmd

F32 = mybir.dt.float32
BF16 = mybir.dt.bfloat16
AF = mybir.ActivationFunctionType
ALU = mybir.AluOpType
AX = mybir.AxisListType

NCORES = 8
T = 12288
UNIT = 4096
NT = T // 128
D = 1024
DEPTH = 2
ALPHA = (2 * DEPTH) ** 0.25
NEG = -30000.0
FM_ROWS = 3200
D_IN = 3616
RW_DECAY = math.exp(-0.5)


class Buf:
    __slots__ = ("name", "w", "r", "dsem", "dcnt", "t", "psum")

    def __init__(self, t, name):
        self.t = t
        self.psum = False
        self.name = name
        self.w = None
        self.r = {}
        self.dsem = None
        self.dcnt = 0

    def __getitem__(self, idx):
        return self.t[idx]


class Eng:
    def __init__(self, h, sem, name):
        self.h = h
        self.sem = sem
        self.n = 0
        self.seen = {}
        self.name = name


class KB:
    def __init__(self, nc):
        self.nc = nc
        self.es = ExitStack()
        self.engs = {}
        for nm, h in (("pe", nc.tensor), ("act", nc.scalar), ("dve", nc.vector),
                      ("pool", nc.gpsimd), ("sp", nc.sync)):
            self.engs[nm] = Eng(h, nc.alloc_semaphore(name="es_" + nm), nm)
        self.bufs = []
        self.dsems = []
        self.ninst = 0
        self.rr = 0

    def sb(self, name, shape, dtype=F32, stack=None):
        self.uid = getattr(self, "uid", 0) + 1
        name = "%s_%d" % (name, self.uid)
        t = (stack or self.es).enter_context(self.nc.sbuf_tensor(name, list(shape), dtype))
        b = Buf(t, name)
        self.bufs.append(b)
        return b

    def ps(self, name, shape, dtype=F32, stack=None):
        self.uid = getattr(self, "uid", 0) + 1
        name = "%s_%d" % (name, self.uid)
        t = (stack or self.es).enter_context(self.nc.psum_tensor(name, list(shape), dtype))
        b = Buf(t, name)
        b.psum = True
        self.bufs.append(b)
        return b

    def drop(self, bufs):
        ids = set(id(b) for b in bufs)
        self.bufs = [b for b in self.bufs if id(b) not in ids]

    def track(self, name):
        b = Buf(None, name)
        self.bufs.append(b)
        return b

    def _waits(self, e, reads, writes, skip_self=False):
        need = {}

        def req(ev):
            if ev is None:
                return
            sem, val = ev
            k = id(sem)
            if k not in need or need[k][1] < val:
                need[k] = (sem, val)

        for b in reads:
            req(b.w)
            if b.psum:
                for ev in b.r.values():
                    if ev[0] is not e.sem:
                        req(ev)
        for b in writes:
            req(b.w)
            for ev in b.r.values():
                req(ev)
        for k, (sem, val) in need.items():
            if skip_self and sem is e.sem:
                continue
            if e.seen.get(k, 0) >= val:
                continue
            e.h.wait_ge(sem, val)
            e.seen[k] = val

    def op(self, eng, fn, reads=(), writes=()):
        e = self.engs[eng]
        self._waits(e, reads, writes, skip_self=(eng == "pe"))
        ins = fn(e.h)
        e.n += 1
        ins.then_inc(e.sem, 1)
        self.ninst += 1
        ev = (e.sem, e.n)
        for b in reads:
            b.r[id(ev[0])] = ev
        for b in writes:
            b.w = ev
            b.r = {}
        return ins

    def dma(self, q, out_ap, in_ap, reads=(), writes=(), owner=None, **kw):
        e = self.engs[q]
        self._waits(e, reads, writes)
        owner = owner or (list(writes) + list(reads))[0]
        if owner.dsem is None:
            self.semuid = getattr(self, "semuid", 0) + 1
            owner.dsem = self.nc.alloc_semaphore(name="ds_%d" % self.semuid)
            self.dsems.append(owner.dsem)
        owner.dcnt += 16
        e.h.dma_start(out=out_ap, in_=in_ap, **kw).then_inc(owner.dsem, 16)
        self.ninst += 1
        ev = (owner.dsem, owner.dcnt)
        for b in reads:
            b.r[id(ev[0])] = ev
        for b in writes:
            b.w = ev
            b.r = {}

    def sync_all(self):
        sp = self.engs["sp"]
        for b in self.bufs:
            if b.dsem is not None and b.dcnt > 0:
                if sp.seen.get(id(b.dsem), 0) < b.dcnt:
                    sp.h.wait_ge(b.dsem, b.dcnt)
        self.nc.all_engine_barrier()
        nums = [e.sem.num for e in self.engs.values()] + [s.num for s in self.dsems]
        sp.h.sem_clear(range(min(nums), max(nums) + 1))
        self.nc.all_engine_barrier()
        for e in self.engs.values():
            e.n = 0
            e.seen = {}
        for b in self.bufs:
            b.w = None
            b.r = {}
            b.dcnt = 0

    @contextmanager
    def loop(self, n):
        self.sync_all()
        with self.nc.Fori(0, n) as i:
            yield i
            self.sync_all()

    def uloop(self, n, every=1):
        self.sync_all()
        for i in range(n):
            yield i
            if (i + 1) % every == 0:
                self.sync_all()

    @contextmanager
    def scope(self):
        st = ExitStack()
        n0 = len(self.bufs)
        try:
            yield st
        finally:
            self.sync_all()
            for b in self.bufs[n0:]:
                if b.dsem is not None:
                    self.dsems = [s_ for s_ in self.dsems if s_ is not b.dsem]
                    self.nc.release_semaphore(b.dsem)
                    b.dsem = None
            self.bufs = self.bufs[:n0]
            st.close()

    def mm(self, out, lhsT, rhs, start=True, stop=True, reads=(), writes=()):
        return self.op("pe", lambda h: h.matmul(out, lhsT, rhs, start=start, stop=stop),
                       reads=reads, writes=writes)

    def tr(self, out, in_, ident, reads=(), writes=()):
        return self.op("pe", lambda h: h.transpose(out, in_, ident), reads=reads, writes=writes)

    def act(self, out, in_, func, reads=(), writes=(), bias=None, scale=None):
        kw = {}
        if bias is not None:
            kw["bias"] = bias
        if scale is not None:
            kw["scale"] = scale
        return self.op("act", lambda h: h.activation(out, in_, func, **kw), reads=reads, writes=writes)

    def copy(self, eng, out, in_, reads=(), writes=()):
        if eng == "act":
            return self.op("act", lambda h: h.copy(out, in_), reads=reads, writes=writes)
        return self.op(eng, lambda h: h.tensor_copy(out, in_), reads=reads, writes=writes)

    def copy_rr(self, out, in_, reads=(), writes=(), engs=("act", "dve")):
        self.rr += 1
        return self.copy(engs[self.rr % len(engs)], out, in_, reads=reads, writes=writes)

    def tt(self, eng, out, in0, in1, op, reads=(), writes=()):
        return self.op(eng, lambda h: h.tensor_tensor(out, in0, in1, op), reads=reads, writes=writes)

    def ts(self, eng, out, in0, s1, s2, op0, op1=None, reads=(), writes=()):
        if op1 is None:
            return self.op(eng, lambda h: h.tensor_scalar(out, in0, s1, None, op0), reads=reads, writes=writes)
        return self.op(eng, lambda h: h.tensor_scalar(out, in0, s1, s2, op0, op1), reads=reads, writes=writes)

    def stt(self, eng, out, in0, scalar, in1, op0, op1, reads=(), writes=()):
        return self.op(eng, lambda h: h.scalar_tensor_tensor(out, in0, scalar, in1, op0, op1),
                       reads=reads, writes=writes)

    def memset(self, eng, ap, val, writes=()):
        return self.op(eng, lambda h: h.memset(ap, val), writes=writes)


A_COLS, B_COLS, C_COLS, D_COLS = 416, 768, 1280, 1152
B0, C0, D0 = 416, 416 + 768, 416 + 768 + 1280


def fm_col_index():
    idx = []
    idx += list(range(0, 256))
    idx += list(range(256, 384))
    idx += list(range(384, 416))
    idx += list(range(400, 416)) + list(range(384, 400))
    idx += list(range(384, 416)) + list(range(384, 416))
    idx += list(range(B0, B0 + 256))
    idx += list(range(B0 + 256, B0 + 512))
    idx += list(range(C0, C0 + 256))
    idx += list(range(C0 + 256, C0 + 512))
    idx += list(range(C0 + 512, C0 + 768))
    idx += list(range(C0 + 1024, C0 + 1280))
    idx += list(range(D0, D0 + 1152))
    assert len(idx) == FM_ROWS
    return np.array(idx)


def tm_col_index():
    return np.array(list(range(B0 + 512, B0 + 768)) + list(range(C0 + 768, C0 + 1024)))


R_CQ, R_CKV, R_KR, R_KRS = 0, 256, 384, 416
R_NQ, R_NK = 512, 768
R_HQ, R_HF, R_HB, R_HG = 1024, 1280, 1536, 1792
R_RW = 2048


def core_units(c):
    if c < 4:
        return [("p", c, 0), ("p", c, 1), ("s", c, 0)]
    b = 4 + 3 * (c - 4)
    return [("s", b, 0), ("s", b + 1, 0), ("s", b + 2, 0)]


def gather_tokens(c, xp, xs):
    parts = []
    for kind, i, h in core_units(c):
        if kind == "p":
            parts.append(xp[i, h * UNIT:(h + 1) * UNIT])
        else:
            parts.append(xs[i])
    return np.concatenate(parts, axis=0)


def rope_tables(c):
    pos = np.zeros(T, np.float32)
    for u, (kind, i, h) in enumerate(core_units(c)):
        pos[u * UNIT:(u + 1) * UNIT] = np.arange(UNIT, dtype=np.float32) + (UNIT * h)
    inv_freq = (10000.0 ** (-np.arange(0, 32, 2, dtype=np.float32) / 32)).astype(np.float32)
    ang = (pos[:, None] * inv_freq[None, :]).astype(np.float32)
    cos, sin = np.cos(ang).astype(np.float32).T, np.sin(ang).astype(np.float32).T
    CQ = np.concatenate([np.ones((64, T), np.float32), cos, cos], 0)
    SQ = np.concatenate([np.zeros((64, T), np.float32), -sin, sin], 0)
    return np.ascontiguousarray(CQ), np.ascontiguousarray(SQ)


NA_NKT = 7


def na_tile_plan():
    plan = []
    for qt in range(NT):
        if qt < 64:
            ks = min(max(qt - 3, 0), 64 - NA_NKT)
            if qt < 3:
                tid = qt
            elif 29 <= qt <= 34:
                tid = 4 + (qt - 29)
            elif qt >= 61:
                tid = 10 + (qt - 61)
            else:
                tid = 3
        else:
            rb = qt - 64
            ks = 64 + min(max(rb - 3, 0), 32 - NA_NKT)
            if rb < 3:
                tid = rb
            elif rb >= 29:
                tid = 13 + (rb - 29)
            else:
                tid = 3
        plan.append((ks, tid))
    return plan


N_NA_TAB = 16


def na_tables(c, bias):
    linked = c < 4
    plan = na_tile_plan()
    tabs = {}
    u_ = np.arange(64)
    for qt, (ks, tid) in enumerate(plan):
        if qt < 64:
            if linked:
                rows, qrow0, qunit = 128, 2 * qt, 0
            else:
                rows, qrow0, qunit = 64, 2 * (qt % 32), qt // 32
        else:
            rows, qrow0, qunit = 64, 2 * (qt - 64), 2
        tab = np.full((NA_NKT, 128, 4, 128), NEG, np.float32)
        for m in range(NA_NKT):
            kt = ks + m
            if qt < 64:
                if linked:
                    krow0, kunit = 2 * kt, 0
                else:
                    krow0, kunit = 2 * (kt % 32), kt // 32
            else:
                krow0, kunit = 2 * (kt - 64), 2
            if kunit != qunit:
                continue
            for i in range(2):
                qrow = qrow0 + i
                rstart = min(max(qrow - 4, 0), rows - 8)
                for j in range(2):
                    krow = krow0 + j
                    if not (rstart <= krow < rstart + 8):
                        continue
                    drow = min(max(krow - qrow + 7, 0), 14)
                    cstart = np.clip(u_ - 8, 0, 48)
                    kc = np.arange(64)[:, None]
                    ok = (kc >= cstart[None, :]) & (kc < cstart[None, :] + 16)
                    dcol = np.clip(kc - u_[None, :] + 15, 0, 30)
                    vals = bias[:, drow, :][:, dcol]
                    blk = np.where(ok[None], vals, NEG)
                    tab[m, j * 64:(j + 1) * 64, :, i * 64:(i + 1) * 64] = blk.transpose(1, 0, 2)
        if tid in tabs:
            assert np.array_equal(tabs[tid], tab), ("na table mismatch", qt, tid)
        else:
            tabs[tid] = tab
    out = np.zeros((N_NA_TAB, 128, NA_NKT, 4, 128), np.float32)
    for tid, tab in tabs.items():
        out[tid] = tab.transpose(1, 0, 2, 3)
    return out


PV_SLOTS = [("gq", 2), ("gkv", 1), ("hglb", 8), ("hgn", 2), ("mu", 18), ("w0", 4), ("a0", 4),
            ("kk", 2), ("ka", 2), ("rk", 2), ("lnw", 2), ("lnb", 2), ("ln1g", 8), ("ln1b", 8),
            ("ln2g", 8), ("ln2b", 8), ("link", 1), ("linkbias", 1), ("isP", 3), ("mugd", 2)]
PV_OFF = {}
_o = 0
for _n, _w in PV_SLOTS:
    PV_OFF[_n] = (_o, _w)
    _o += _w
NPV = _o


def _cols(v):
    v = np.asarray(v, np.float32).reshape(-1, 128)
    return v.T


def pack_pvec(c, l, inp):
    pv = np.zeros((128, NPV), np.float32)

    def put(name, arr):
        o, w = PV_OFF[name]
        arr = np.asarray(arr, np.float32)
        assert arr.shape == (128, w), (name, arr.shape)
        pv[:, o:o + w] = arr

    put("gq", _cols(inp["mla_gq"][l]))
    put("gkv", _cols(inp["mla_gkv"][l]))
    put("hglb", np.concatenate([_cols(inp["hg_lb"][ll, z]) for ll in range(2) for z in range(2)], 1))
    put("hgn", _cols(inp["hg_gnorm"][l]))
    put("mu", np.concatenate([_cols(inp["rw_mu"][l, j]) for j in range(2)], 1))
    put("w0", np.concatenate([_cols(inp["rw_w0"][l, z]) for z in range(2)], 1))
    put("a0", np.concatenate([_cols(inp["rw_a0"][l, z]) for z in range(2)], 1))
    put("kk", _cols(inp["rw_kk"][l]))
    put("ka", _cols(inp["rw_ka"][l]))
    put("rk", _cols(inp["rw_rk"][l]))
    put("lnw", _cols(inp["rw_ln_w"][l]))
    put("lnb", _cols(inp["rw_ln_b"][l]))
    put("ln1g", _cols(inp["ln1_g"][l]))
    put("ln1b", _cols(inp["ln1_b"][l]))
    put("ln2g", _cols(inp["ln2_g"][l]))
    put("ln2b", _cols(inp["ln2_b"][l]))
    linked = c < 4
    put("link", np.full((128, 1), 1.0 if linked else 0.0, np.float32))
    put("linkbias", np.full((128, 1), 0.0 if linked else NEG, np.float32))
    isp = np.zeros((128, 3), np.float32)
    if linked:
        isp[:, 0:2] = 1.0
    put("isP", isp)
    put("mugd", np.stack([inp["rw_mu"][l, 0, 1024:1152], inp["rw_mu"][l, 1, 1024:1152]], 1))
    return pv


class Ctx:
    pass


def consts(k, cx):
    nc = k.nc
    cx.identf = k.sb("identf", [128, 128], F32)
    cx.ident = k.sb("ident", [128, 128], BF16)
    cx.ones = k.sb("ones", [128, 128], F32)
    cx.blk = k.sb("blk", [128, 128], F32)
    cx.blkb = k.sb("blkb", [128, 128], BF16)
    k.memset("pool", cx.identf[:], 0.0, writes=[cx.identf])
    k.op("pool", lambda h: h.affine_select(cx.identf[:], cx.identf[:], [[-1, 128]], ALU.not_equal, 1.0,
                                           base=0, channel_multiplier=1), reads=[cx.identf], writes=[cx.identf])
    k.copy("dve", cx.ident[:], cx.identf[:], reads=[cx.identf], writes=[cx.ident])
    k.memset("pool", cx.ones[:], 1.0, writes=[cx.ones])
    k.memset("pool", cx.blk[:], 0.0, writes=[cx.blk])
    k.memset("pool", cx.blk[0:64, 0:64], 1.0, writes=[cx.blk])
    k.memset("pool", cx.blk[64:128, 64:128], 1.0, writes=[cx.blk])
    k.copy("dve", cx.blkb[:], cx.blk[:], reads=[cx.blk], writes=[cx.blkb])
    cx.one_col = k.sb("one_col", [128, 1], F32)
    k.memset("pool", cx.one_col[:], 1.0, writes=[cx.one_col])
    cx.eps6 = k.sb("eps6", [128, 1], F32)
    k.memset("pool", cx.eps6[:], 1e-6, writes=[cx.eps6])
    cx.of_tok = k.track("of_tok")
    cx.eps5 = k.sb("eps5", [128, 1], F32)
    k.memset("pool", cx.eps5[:], 1e-5, writes=[cx.eps5])
    for nm, cmp, sgn in (("m_lt", ALU.is_gt, 1), ("m_le", ALU.is_ge, 1), ("m_gt", ALU.is_gt, -1), ("m_ge", ALU.is_ge, -1)):
        mf = k.sb(nm + "f", [128, 128], F32)
        mb = k.sb(nm, [128, 128], BF16)
        k.memset("pool", mf[:], 1.0, writes=[mf])
        k.op("pool", lambda h, mf=mf, cmp=cmp, sgn=sgn: h.affine_select(mf[:], mf[:], [[sgn, 128]], cmp, 0.0,
                                                                       base=0, channel_multiplier=-sgn),
             reads=[mf], writes=[mf])
        k.copy("dve", mb[:], mf[:], reads=[mf], writes=[mb])
        setattr(cx, nm, mb)
        setattr(cx, nm + "f", mf)
    cx.m_le64 = cx.m_lef
    cx.m_ge64 = cx.m_gef


def load_cast(k, dst, dst_ap, src_ap, shape, stage, q="sp"):
    k.dma(q, stage_ap(stage, shape), src_ap, writes=[stage])
    k.copy_rr(dst_ap, stage_ap(stage, shape), reads=[stage], writes=[dst])


def stage_ap(stage, shape):
    if len(shape) == 2:
        return stage[0:shape[0], 0:shape[1]]
    return stage[0:shape[0], 0:shape[1], 0:shape[2]]


def phase_proj(k, cx, xT, wfm, wtm, zT, vi):
    with k.scope() as st:
        wb = k.sb("p1_wb", [128, 8, FM_ROWS], BF16, st)
        wtb = k.sb("p1_wtb", [128, 8, 512], BF16, st)
        stg = [k.sb("p1_stg%d" % i, [128, 8, 640], F32, st) for i in range(2)]
        wv = wfm.rearrange("(c p) n -> p c n", p=128)
        for j in range(FM_ROWS // 640):
            s = stg[j % 2]
            k.dma("sp", s[:], wv[:, :, j * 640:(j + 1) * 640], writes=[s])
            for c in range(8):
                k.copy_rr(wb[:, c, j * 640:(j + 1) * 640], s[:, c, :], reads=[s], writes=[wb])
        s = stg[1]
        k.dma("sp", s[:, :, 0:512], wtm.rearrange("(c p) n -> p c n", p=128), writes=[s])
        for c in range(8):
            k.copy_rr(wtb[:, c, :], s[:, c, 0:512], reads=[s], writes=[wtb])
        xf = k.sb("p1_xf", [128, 8, 512], F32, st)
        xb = k.sb("p1_xb", [128, 8, 512], BF16, st)
        pss = [k.ps("p1_ps%d" % i, [128, 512], F32, st) for i in range(4)]
        zo = [k.sb("p1_zo%d" % i, [128, 512], F32, st) for i in range(4)]
        xv = xT.rearrange("(c p) t -> p c t", p=128)
        import os
        MODE = int(os.environ.get("P1_MODE", "0"))
        if MODE == 1:
            return
        for i in k.uloop(T // 512, every=4):
            k.dma("sp", xf[:], xv[:, :, bass.ts(i, 512)], writes=[xf])
            for c in range(8):
                k.copy_rr(xb[:, c, :], xf[:, c, :], reads=[xf], writes=[xb], engs=("act", "dve", "pool"))
            for mt in range(FM_ROWS // 128):
                ps = pss[mt % 4]
                z = zo[mt % 4]
                for c in range(8):
                    k.mm(ps[:], wb[:, c, mt * 128:(mt + 1) * 128], xb[:, c, :], start=(c == 0), stop=(c == 7),
                         reads=[wb, xb], writes=[ps])
                k.copy_rr(z[:], ps[:], reads=[ps], writes=[z])
                k.dma("sp", zT(mt * 128, (mt + 1) * 128)[:, bass.ts(i, 512)], z[:], reads=[z])
            for tt in range(4 if MODE != 3 else 0):
                ps = pss[tt % 4]
                z = zo[tt % 4]
                for c in range(8):
                    k.mm(ps[:], xb[:, c, tt * 128:(tt + 1) * 128], wtb[:, c, :], start=(c == 0), stop=(c == 7),
                         reads=[wtb, xb], writes=[ps])
                k.copy_rr(z[:], ps[:], reads=[ps], writes=[z])
                k.dma("sp", vi[bass.ds(i * 512 + tt * 128, 128), :], z[:], reads=[z])


def declare_mixer_inputs(nc, cx, sfx=""):
    def din(name, shape, dt=F32):
        return nc.dram_tensor(name + sfx, list(shape), dt, kind="ExternalInput").ap()
    cx.wfm = din("wfm", [D, FM_ROWS])
    cx.wtm = din("wtm", [D, 512])
    cx.CQ = din("CQ", [96, T])
    cx.SQ = din("SQ", [96, T])
    cx.wuq = din("wuq", [256, 384])
    cx.wuqs = din("wuqs", [256, 384])
    cx.wuk = din("wuk", [128, 256])
    cx.wuv = din("wuv", [128, 256])
    cx.natab = din("natab", [N_NA_TAB, 128, NA_NKT, 4, 128])
    cx.wup = din("wup", [128, 256])
    cx.aup = din("aup", [128, 256])
    cx.gup = din("gup", [128, 256])
    cx.wout = din("wout", [D, D])
    cx.router = din("router", [D, 16])
    cx.pvec = din("pvec", [128, NPV])
    cx.pvec64 = din("pvec64", [64, NPV64])


def mixer_input_arrays(c, l, inp):
    w_in = inp["w_in"][l]
    wuq = inp["mla_wuq"][l]
    sw = []
    for h in range(4):
        b = h * 96
        sw += list(range(b, b + 64)) + list(range(b + 80, b + 96)) + list(range(b + 64, b + 80))
    CQ, SQ = rope_tables(c)
    return {
        "wfm": np.ascontiguousarray(w_in[:, fm_col_index()]),
        "wtm": np.ascontiguousarray(w_in[:, tm_col_index()]),
        "CQ": CQ, "SQ": SQ,
        "wuq": np.ascontiguousarray(wuq),
        "wuqs": np.ascontiguousarray(wuq[:, np.array(sw)]),
        "wuk": np.ascontiguousarray(inp["mla_wuk"][l]),
        "wuv": np.ascontiguousarray(inp["mla_wuv"][l]),
        "natab": na_tables(c, inp["na_bias"][l]),
        "wup": np.ascontiguousarray(inp["rw_w_up"][l].reshape(128, 256)),
        "aup": np.ascontiguousarray(inp["rw_a_up"][l].reshape(128, 256)),
        "gup": np.ascontiguousarray(inp["rw_g_up"][l]),
        "wout": np.ascontiguousarray(inp["w_out"][l]),
        "router": np.ascontiguousarray(inp["moe_router"][l]),
        "pvec": pack_pvec(c, l, inp),
        "pvec64": pack_pv64(l, inp),
    }


def build_stage_a(upto="all", debug=(), dbg={}, mixers="ABCD", layer=0):
    nc = bass.Bass("TRN2", target_bir_lowering=False)
    cx = Ctx()
    k = KB(nc)
    cx.xT = nc.dram_tensor("xT", [D, T], F32, kind="ExternalInput").ap()
    declare_mixer_inputs(nc, cx)

    def scratch(name, shape, dt=F32):
        kind = "ExternalOutput" if name in debug else "Internal"
        return nc.dram_tensor(name, list(shape), dt, kind=kind).ap()
    zgrp = [(0, 512, scratch("zA", [512, T])), (512, 1024, scratch("zB", [512, T])),
            (1024, 2048, scratch("zC", [1024, T])), (2048, 3200, scratch("zD", [1152, T]))]

    def zT(r0, r1):
        for a, b, ap in zgrp:
            if a <= r0 and r1 <= b:
                return ap[r0 - a:r1 - a, :]
        raise ValueError((r0, r1))
    cx.zT = zT
    cx.vi = scratch("vi", [T, 512])
    cx.oT = scratch("oT", [D, T], BF16)
    cx.x1T = nc.dram_tensor("x1T", [D, T], F32, kind="ExternalOutput").ap()
    cx.aff = nc.dram_tensor("aff", [T, 16], F32, kind="ExternalOutput").ap()
    consts(k, cx)
    cx.pv = k.sb("pv", [128, NPV], F32)
    k.dma("sp", cx.pv[:], cx.pvec, writes=[cx.pv])
    cx.pv64 = k.sb("pv64", [64, NPV64], F32)
    k.dma("sp", cx.pv64[:], cx.pvec64, writes=[cx.pv64])
    phase_proj(k, cx, cx.xT, cx.wfm, cx.wtm, cx.zT, cx.vi)
    def finish():
        k.sync_all()
        dt = k.track("dbgt")
        for name, (src, shape, dtp) in dbg.items():
            o = nc.dram_tensor("dbg_" + name, list(shape), dtp, kind="ExternalOutput").ap()
            k.dma("sp", o, src(cx), writes=[dt], owner=dt)
        k.sync_all()
        return nc, k
    if upto == "P1":
        return finish()
    if "A" in mixers:
        phase_mla(k, cx)
    if upto == "P2":
        return finish()
    if "B" in mixers:
        phase_na(k, cx)
    if upto == "P3":
        return finish()
    if "C" in mixers:
        phase_hgrn(k, cx, layer)
    if upto == "P4":
        return finish()
    if "D" in mixers:
        phase_rwkv(k, cx)
    if upto == "P5":
        return finish()
    phase_outproj(k, cx, cx.xT)
    return finish()


def pvs(cx, name, j=0, n=1):
    o, w = PV_OFF[name]
    return cx.pv[:, o + j:o + j + n]


def phase_mla(k, cx):
    nc = k.nc
    QT = nc.dram_tensor("mla_QT" + getattr(cx, "sfx", ""), [4, 96, T], BF16).ap()
    KT = nc.dram_tensor("mla_KT" + getattr(cx, "sfx", ""), [4, 96, T], BF16).ap()
    VA = nc.dram_tensor("mla_VA" + getattr(cx, "sfx", ""), [T, 4, 65], BF16).ap()
    zT = cx.zT
    with k.scope() as st:
        stg = k.sb("m_stg", [128, 2, 384], F32, st)
        wuq = k.sb("m_wuq", [128, 2, 384], BF16, st)
        wuqs = k.sb("m_wuqs", [128, 2, 384], BF16, st)
        wuk = k.sb("m_wuk", [128, 256], BF16, st)
        wuv = k.sb("m_wuv", [128, 256], BF16, st)
        load_cast(k, wuq, wuq[:], cx.wuq.rearrange("(c p) n -> p c n", p=128), [128, 2, 384], stg)
        load_cast(k, wuqs, wuqs[:], cx.wuqs.rearrange("(c p) n -> p c n", p=128), [128, 2, 384], stg)
        k.dma("sp", stg[:, 0, 0:256], cx.wuk, writes=[stg])
        k.copy("dve", wuk[:], stg[:, 0, 0:256], reads=[stg], writes=[wuk])
        k.dma("sp", stg[:, 1, 0:256], cx.wuv, writes=[stg])
        k.copy("dve", wuv[:], stg[:, 1, 0:256], reads=[stg], writes=[wuv])
        eps = k.sb("m_eps", [128, 1], F32, st)
        k.memset("pool", eps[:], 1e-6, writes=[eps])
        cq = k.sb("m_cq", [128, 2, 512], F32, st)
        ckv = k.sb("m_ckv", [128, 512], F32, st)
        krt = k.sb("m_krt", [128, 512], F32, st)
        krs = k.sb("m_krs", [128, 512], F32, st)
        cqt = k.sb("m_cqt", [96, 512], F32, st)
        sqt = k.sb("m_sqt", [96, 512], F32, st)
        sq = k.sb("m_sq", [128, 2, 512], F32, st)
        rstd = k.sb("m_rstd", [128, 512], F32, st)
        cqn = k.sb("m_cqn", [128, 2, 512], BF16, st)
        ckvn = k.sb("m_ckvn", [128, 512], BF16, st)
        t1 = k.sb("m_t1", [96, 512], F32, st)
        t2 = k.sb("m_t2", [96, 512], F32, st)
        qo = k.sb("m_qo", [96, 4, 512], BF16, st)
        ko = k.sb("m_ko", [96, 4, 512], BF16, st)
        vo = k.sb("m_vo", [128, 4, 4, 65], BF16, st)
        p_ms = k.ps("m_pms", [128, 512], F32, st)
        p_qa = k.ps("m_pqa", [96, 512], F32, st)
        p_qb = k.ps("m_pqb", [96, 512], F32, st)
        p_k = k.ps("m_pk", [64, 512], F32, st)
        p_v = k.ps("m_pv", [128, 256], F32, st)
        k.memset("pool", vo[:], 1.0, writes=[vo])
        for i in k.uloop(T // 512, every=4):
            tsl = bass.ts(i, 512)
            k.dma("sp", cq[:], zT(R_CQ, R_CQ + 256).rearrange("(c p) t -> p c t", p=128)[:, :, tsl], writes=[cq])
            k.dma("sp", ckv[:], zT(R_CKV, R_CKV + 128)[:, tsl], writes=[ckv])
            k.dma("sp", krt[64:96, :], zT(R_KR, R_KR + 32)[:, tsl], writes=[krt])
            k.dma("sp", krs[64:96, :], zT(R_KRS, R_KRS + 32)[:, tsl], writes=[krs])
            k.dma("sp", cqt[:], cx.CQ[:, tsl], writes=[cqt])
            k.dma("sp", sqt[:], cx.SQ[:, tsl], writes=[sqt])
            k.act(sq[:], cq[:], AF.Square, reads=[cq], writes=[sq])
            for c in range(2):
                k.mm(p_ms[:], cx.ones[:], sq[:, c, :], start=(c == 0), stop=(c == 1), reads=[cx.ones, sq], writes=[p_ms])
            k.act(rstd[:], p_ms[:], AF.Sqrt, bias=eps[:], scale=1.0 / 256, reads=[p_ms, eps], writes=[rstd])
            k.op("dve", lambda h: h.reciprocal(rstd[:], rstd[:]), reads=[rstd], writes=[rstd])
            for c in range(2):
                k.stt("dve", cqn[:, c, :], cq[:, c, :], pvs(cx, "gq", c), rstd[:], ALU.mult, ALU.mult,
                      reads=[cq, rstd, cx.pv], writes=[cqn])
            for h in range(4):
                for c in range(2):
                    k.mm(p_qa[:], wuq[:, c, h * 96:(h + 1) * 96], cqn[:, c, :], start=(c == 0), stop=(c == 1),
                         reads=[wuq, cqn], writes=[p_qa])
                for c in range(2):
                    k.mm(p_qb[:], wuqs[:, c, h * 96:(h + 1) * 96], cqn[:, c, :], start=(c == 0), stop=(c == 1),
                         reads=[wuqs, cqn], writes=[p_qb])
                k.tt("dve", t1[:], p_qa[:], cqt[:], ALU.mult, reads=[p_qa, cqt], writes=[t1])
                k.tt("dve", t2[:], p_qb[:], sqt[:], ALU.mult, reads=[p_qb, sqt], writes=[t2])
                k.tt("pool", qo[:, h, :], t1[:], t2[:], ALU.add, reads=[t1, t2], writes=[qo])
            k.dma("sp", QT.rearrange("h r t -> r h t")[:, :, tsl], qo[:], reads=[qo])
            k.act(sq[:, 0, :], ckv[:], AF.Square, reads=[ckv], writes=[sq])
            k.mm(p_ms[:], cx.ones[:], sq[:, 0, :], reads=[cx.ones, sq], writes=[p_ms])
            k.act(rstd[:], p_ms[:], AF.Sqrt, bias=eps[:], scale=1.0 / 128, reads=[p_ms, eps], writes=[rstd])
            k.op("dve", lambda h: h.reciprocal(rstd[:], rstd[:]), reads=[rstd], writes=[rstd])
            k.stt("dve", ckvn[:], ckv[:], pvs(cx, "gkv"), rstd[:], ALU.mult, ALU.mult,
                  reads=[ckv, rstd, cx.pv], writes=[ckvn])
            k.tt("dve", t1[64:96, :], krt[64:96, :], cqt[64:96, :], ALU.mult, reads=[krt, cqt], writes=[t1])
            k.tt("dve", t2[64:96, :], krs[64:96, :], sqt[64:96, :], ALU.mult, reads=[krs, sqt], writes=[t2])
            for h in range(4):
                k.mm(p_k[:], wuk[:, h * 64:(h + 1) * 64], ckvn[:], reads=[wuk, ckvn], writes=[p_k])
                k.copy("act", ko[0:64, h, :], p_k[:], reads=[p_k], writes=[ko])
                k.tt("pool", ko[64:96, h, :], t1[64:96, :], t2[64:96, :], ALU.add, reads=[t1, t2], writes=[ko])
            k.dma("sp", KT.rearrange("h r t -> r h t")[:, :, tsl], ko[:], reads=[ko])
            for tt in range(4):
                k.mm(p_v[:], ckvn[:, tt * 128:(tt + 1) * 128], wuv[:], reads=[ckvn, wuv], writes=[p_v])
                k.copy("dve", vo[:, tt, :, 0:64], p_v[:].rearrange("p (h d) -> p h d", h=4), reads=[p_v], writes=[vo])
            k.dma("sp", VA.rearrange("(n p) h d -> p n h d", p=128)[:, bass.ts(i, 4), :, :], vo[:], reads=[vo])
    scale = 96 ** -0.5
    with k.scope() as st:
        kts = k.sb("a_kts", [96, 4, 2 * UNIT], BF16, st)
        va = k.sb("a_va", [128, 64, 4, 65], BF16, st)
        qs = [k.sb("a_q%d" % j, [96, 4, 512], BF16, st) for j in range(2)]
        pb = [k.sb("a_p%d" % j, [128, 512], BF16, st) for j in range(3)]
        osb = [k.sb("a_osb%d" % j, [65, 512], F32, st) for j in range(2)]
        rec = [k.sb("a_rec%d" % j, [64, 512], F32, st) for j in range(2)]
        ob = [k.sb("a_ob%d" % j, [64, 512], BF16, st) for j in range(2)]
        sel = k.sb("a_sel", [65, 64], F32, st)
        k.memset("pool", sel[:], 0.0, writes=[sel])
        k.memset("pool", sel[64:65, :], 1.0, writes=[sel])
        p_s = [k.ps("a_ps%d" % j, [128, 512], F32, st) for j in range(3)]
        p_o = [k.ps("a_po%d" % j, [65, 512], F32, st) for j in range(2)]
        p_r = k.ps("a_pr", [64, 512], F32, st)
        KTv = KT.rearrange("h r t -> r h t")
        VAv = VA.rearrange("(n p) h d -> p n h d", p=128)
        cnt = 0
        for grp in ((0, 1), (2,)):
            nku = len(grp)
            t0 = grp[0] * UNIT
            for h in range(4):
                k.dma("sp", kts[:, h, 0:nku * UNIT], KTv[:, h, t0:t0 + nku * UNIT], writes=[kts])
            for u_ in range(nku):
                k.dma("sp", va[:, u_ * 32:(u_ + 1) * 32, :, :], VAv[:, (t0 // 128) + u_ * 32:(t0 // 128) + (u_ + 1) * 32, :, :],
                      writes=[va])
            for uq in grp:
                for qb in range(UNIT // 512):
                    q = qs[cnt % 2]
                    cnt += 1
                    tq = uq * UNIT + qb * 512
                    k.dma("sp", q[:], QT.rearrange("h r t -> r h t")[:, :, tq:tq + 512], writes=[q])
                    for h in range(4):
                        po = p_o[h % 2]
                        nk = nku * 32
                        for kt in range(nk):
                            ps_ = p_s[kt % 3]
                            pp = pb[kt % 3]
                            k.mm(ps_[:], kts[:, h, kt * 128:(kt + 1) * 128], q[:, h, :], reads=[kts, q], writes=[ps_])
                            same = (grp[kt // 32] == uq)
                            if same:
                                k.act(pp[:], ps_[:], AF.Exp, scale=scale, reads=[ps_], writes=[pp])
                            else:
                                k.act(pp[:], ps_[:], AF.Exp, scale=scale, bias=pvs(cx, "linkbias"),
                                      reads=[ps_, cx.pv], writes=[pp])
                            k.mm(po[:], va[:, kt, h, :], pp[:], start=(kt == 0), stop=(kt == nk - 1),
                                 reads=[va, pp], writes=[po])
                        o_ = osb[h % 2]
                        k.copy("dve", o_[:], po[:], reads=[po], writes=[o_])
                        k.mm(p_r[:], sel[:], o_[:], reads=[sel, o_], writes=[p_r])
                        r_ = rec[h % 2]
                        k.op("dve", lambda hh, r_=r_: hh.reciprocal(r_[:], p_r[:]), reads=[p_r], writes=[r_])
                        b_ = ob[h % 2]
                        k.tt("pool", b_[:], o_[0:64, :], r_[:], ALU.mult, reads=[o_, r_], writes=[b_])
                        k.dma("sp", cx.oT[h * 64:(h + 1) * 64, tq:tq + 512], b_[:], reads=[b_])


def phase_na(k, cx):
    nc = k.nc
    NQ = nc.dram_tensor("na_Q" + getattr(cx, "sfx", ""), [256, T], BF16).ap()
    NK = nc.dram_tensor("na_K" + getattr(cx, "sfx", ""), [256, T], BF16).ap()
    NVA = nc.dram_tensor("na_VA" + getattr(cx, "sfx", ""), [T, 4, 65], BF16).ap()
    zT = cx.zT
    with k.scope() as st:
        qf = k.sb("n_qf", [128, 4, 512], F32, st)
        qb = k.sb("n_qb", [128, 4, 512], BF16, st)
        vf = k.sb("n_vf", [128, 4, 256], F32, st)
        vo = k.sb("n_vo", [128, 4, 4, 65], BF16, st)
        k.memset("pool", vo[:], 1.0, writes=[vo])
        for i in k.uloop(T // 512, every=4):
            tsl = bass.ts(i, 512)
            k.dma("sp", qf[:, 0:2, :], zT(R_NQ, R_NQ + 256).rearrange("(c p) t -> p c t", p=128)[:, :, tsl], writes=[qf])
            k.dma("sp", qf[:, 2:4, :], zT(R_NK, R_NK + 256).rearrange("(c p) t -> p c t", p=128)[:, :, tsl], writes=[qf])
            k.dma("sp", vf[:], cx.vi.rearrange("(n p) c -> p n c", p=128)[:, bass.ts(i, 4), 0:256], writes=[vf])
            k.copy("act", qb[:, 0:2, :], qf[:, 0:2, :], reads=[qf], writes=[qb])
            k.copy("dve", qb[:, 2:4, :], qf[:, 2:4, :], reads=[qf], writes=[qb])
            for tt in range(4):
                k.copy("pool", vo[:, tt, :, 0:64], vf[:, tt, :].rearrange("p (h d) -> p h d", h=4), reads=[vf], writes=[vo])
            k.dma("sp", NQ.rearrange("(c p) t -> p c t", p=128)[:, :, tsl], qb[:, 0:2, :], reads=[qb])
            k.dma("sp", NK.rearrange("(c p) t -> p c t", p=128)[:, :, tsl], qb[:, 2:4, :], reads=[qb])
            k.dma("sp", NVA.rearrange("(n p) h d -> p n h d", p=128)[:, bass.ts(i, 4), :, :], vo[:], reads=[vo])
    plan = na_tile_plan()
    import os
    NAM = int(os.environ.get("NA_MODE", "0"))
    if NAM == 1:
        return
    with k.scope() as st:
        nq = k.sb("n_q", [64, 4, UNIT], BF16, st)
        nk_ = k.sb("n_k", [64, 4, 40 * 128], BF16, st)
        va = k.sb("n_va", [128, 40, 4, 65], BF16, st)
        tab = k.sb("n_tab", [128, NA_NKT, 4, 128], F32, st)
        tt_ = [k.sb("n_t%d" % j, [128, 4, 128], F32, st) for j in range(2)]
        pp = [k.sb("n_p%d" % j, [128, 4, 128], BF16, st) for j in range(2 * NA_NKT)]
        osb = [k.sb("n_osb%d" % j, [65, 4, 128], F32, st) for j in range(2)]
        rec = [k.sb("n_rec%d" % j, [64, 4, 128], F32, st) for j in range(2)]
        ob = [k.sb("n_ob%d" % j, [64, 4, 128], BF16, st) for j in range(2)]
        sel = k.sb("n_sel", [65, 64], F32, st)
        k.memset("pool", sel[:], 0.0, writes=[sel])
        k.memset("pool", sel[64:65, :], 1.0, writes=[sel])
        p_s = [k.ps("n_ps%d" % j, [128, 4, 128], F32, st) for j in range(3)]
        p_o = [k.ps("n_po%d" % j, [65, 4, 128], F32, st) for j in range(2)]
        p_r = k.ps("n_pr", [64, 4, 128], F32, st)
        cur_tid = -1
        for un in range(3):
            q0 = un * 32
            g0 = min(plan[qt][0] for qt in range(q0, q0 + 32))
            g1 = max(plan[qt][0] for qt in range(q0, q0 + 32)) + NA_NKT
            gn = g1 - g0
            assert gn <= 40
            k.dma("sp", nq[:, :, :], NQ.rearrange("(h d) t -> d h t", d=64)[:, :, q0 * 128:(q0 + 32) * 128], writes=[nq])
            k.dma("sp", nk_[:, :, 0:gn * 128], NK.rearrange("(h d) t -> d h t", d=64)[:, :, g0 * 128:g1 * 128], writes=[nk_])
            k.dma("sp", va[:, 0:gn, :, :], NVA.rearrange("(n p) h d -> p n h d", p=128)[:, g0:g1, :, :], writes=[va])
            for qt in range(q0, q0 + 32):
                ks, tid = plan[qt]
                if tid != cur_tid:
                    k.dma("sp", tab[:], cx.natab[tid], writes=[tab])
                    cur_tid = tid
                ql = (qt - q0) * 128
                po = p_o[qt % 2]
                if NAM == 4:
                    continue
                for m in range(NA_NKT):
                    kl = (ks + m - g0) * 128
                    ps_ = p_s[m % 3]
                    for h in range(4):
                        k.mm(ps_[:, h, :], nk_[:, h, kl:kl + 128], nq[:, h, ql:ql + 128],
                             reads=[nk_, nq], writes=[ps_])
                    t_ = tt_[m % 2]
                    k.stt("dve", t_[:], ps_[:], 0.125, tab[:, m, :, :], ALU.mult, ALU.add, reads=[ps_, tab], writes=[t_])
                    p_ = pp[(qt % 2) * NA_NKT + m]
                    k.act(p_[:], t_[:], AF.Exp, reads=[t_], writes=[p_])
                for h in range(4 if NAM != 5 else 0):
                    for m in range(NA_NKT):
                        p_ = pp[(qt % 2) * NA_NKT + m]
                        k.mm(po[:, h, :], va[:, ks + m - g0, h, :], p_[:, h, :], start=(m == 0), stop=(m == NA_NKT - 1),
                             reads=[va, p_], writes=[po])
                if NAM == 5:
                    continue
                o_ = osb[qt % 2]
                k.copy("act", o_[:], po[:], reads=[po], writes=[o_])
                k.mm(p_r[:].rearrange("p h q -> p (h q)"), sel[:], o_[:].rearrange("p h q -> p (h q)"), reads=[sel, o_], writes=[p_r])
                r_ = rec[qt % 2]
                k.op("dve", lambda hh, r_=r_: hh.reciprocal(r_[:], p_r[:]), reads=[p_r], writes=[r_])
                b_ = ob[qt % 2]
                k.tt("pool", b_[:], o_[0:64, :, :], r_[:], ALU.mult, reads=[o_, r_], writes=[b_])
                if NAM == 2:
                    continue
                if NAM == 3:
                    for h in range(4):
                        k.dma("sp", cx.oT[256 + h * 64:256 + (h + 1) * 64, qt * 128:(qt + 1) * 128], b_[:, h, :], reads=[b_])
                    continue
                k.dma("sp", cx.oT[256:512, qt * 128:(qt + 1) * 128].rearrange("(h d) t -> d h t", h=4), b_[:], reads=[b_])


PV64_SLOTS = [("hglb", 16), ("hgn", 4), ("rmu", 32), ("w0", 8), ("a0", 8), ("kk", 4), ("ka", 4), ("rk", 4),
              ("lnw", 4), ("lnb", 4)]
PV64_OFF = {}
_o = 0
for _n, _w in PV64_SLOTS:
    PV64_OFF[_n] = (_o, _w)
    _o += _w
NPV64 = _o


def _hcols(v):
    return np.asarray(v, np.float32).reshape(-1, 64).T


def pack_pv64(l, inp):
    pv = np.zeros((64, NPV64), np.float32)

    def put(name, arr):
        o, w = PV64_OFF[name]
        assert arr.shape == (64, w), (name, arr.shape)
        pv[:, o:o + w] = arr
    put("hglb", np.concatenate([_hcols(inp["hg_lb"][ll, z]) for ll in range(2) for z in range(2)], 1))
    put("hgn", _hcols(inp["hg_gnorm"][l]))
    mu = inp["rw_mu"][l]
    put("rmu", np.concatenate([_hcols(mu[j, 0:1024]) for j in range(2)], 1))
    put("w0", np.concatenate([_hcols(inp["rw_w0"][l, z]) for z in range(2)], 1))
    put("a0", np.concatenate([_hcols(inp["rw_a0"][l, z]) for z in range(2)], 1))
    for nm, key in (("kk", "rw_kk"), ("ka", "rw_ka"), ("rk", "rw_rk"), ("lnw", "rw_ln_w"), ("lnb", "rw_ln_b")):
        put(nm, _hcols(inp[key][l]))
    return pv


def p64(cx, name, j=0, n=1):
    o, w = PV64_OFF[name]
    return cx.pv64[:, o + j:o + j + n]


def bc(ap, shape):
    return ap.to_broadcast(list(shape))


def phase_hgrn(k, cx, layer):
    nc = k.nc
    OF = nc.dram_tensor("hg_of" + getattr(cx, "sfx", ""), [256, T], F32).ap()
    zT = cx.zT
    hd = lambda r0: zT(r0, r0 + 256).rearrange("(h d) t -> d h t", d=64)
    vview = cx.vi.rearrange("(n s) c -> s n c", s=64)
    with k.scope() as st:
        rst = k.sb("h_rst", [64, 4, 128], F32, st)
        k.memset("pool", rst[:], 1.0, writes=[rst])
        k.memset("pool", rst[:, :, 0:1], 0.0, writes=[rst])
        k.memset("pool", rst[:, :, 64:65], 0.0, writes=[rst])
        lb = k.sb("h_lb", [64, 2, 4], F32, st)
        oml = k.sb("h_oml", [64, 2, 4], F32, st)
        if layer == 0:
            k.memset("pool", lb[:], 0.0, writes=[lb])
        else:
            o, _ = PV64_OFF["hglb"]
            k.tt("dve", lb[:].rearrange("p z h -> p (z h)"), cx.pv64[:, o + 8:o + 16], cx.pv64[:, o:o + 8], ALU.subtract,
                 reads=[cx.pv64], writes=[lb])
            k.act(lb[:], lb[:], AF.Sigmoid, reads=[lb], writes=[lb])
        k.ts("dve", oml[:], lb[:], -1.0, 1.0, ALU.mult, ALU.add, reads=[lb], writes=[oml])
        ones64 = k.sb("h_ones", [64, 64], F32, st)
        k.memset("pool", ones64[:], 1.0, writes=[ones64])
        S = [k.sb("h_S%d" % z, [64, 4, 64], F32, st) for z in range(2)]
        Sin = [k.sb("h_Sin%d" % z, [64, 4, 64], BF16, st) for z in range(2)]
        NB = 2
        def mk(name, shape, dt=F32):
            return [k.sb("h_%s%d" % (name, j), shape, dt, st) for j in range(NB)]
        qf, pf, gf = mk("qf", [64, 4, 128]), mk("pf", [64, 4, 128]), mk("gf", [64, 4, 128])
        vf = mk("vf", [64, 2, 256])
        vb = mk("vb", [64, 2, 4, 64], BF16)
        e_, A_, B_, gl, r_, kk_ = (mk(n, [64, 4, 128]) for n in ("e", "A", "B", "gl", "r", "kk"))
        b_, arg, E1 = mk("b", [64, 4, 128]), mk("arg", [64, 4, 128]), mk("E1", [64, 4, 128])
        qt, kt = mk("qt", [64, 4, 128], BF16), mk("kt", [64, 4, 128], BF16)
        cc = mk("cc", [64, 4, 2, 3])
        ec = mk("ec", [64, 4, 2, 3])
        ktT = mk("ktT", [64, 8, 64], BF16)
        attm = mk("attm", [64, 8, 64], BF16)
        tmp = mk("tmp", [64, 4, 64])
        osb = mk("osb", [64, 4, 128])
        ofw = mk("ofw", [64, 4, 128])
        sq = mk("sq", [64, 4, 128])
        rs = mk("rs", [64, 4, 128])
        ob = mk("ob", [64, 4, 128], BF16)
        p_att = [k.ps("h_patt%d" % j, [64, 8, 64], F32, st) for j in range(2)]
        p_tr = k.ps("h_ptr", [64, 8, 64], BF16, st)
        p_o = [k.ps("h_po%d" % j, [64, 8, 64], F32, st) for j in range(2)]
        p_kv = k.ps("h_pkv", [64, 4, 64], F32, st)
        p_n = k.ps("h_pn", [64, 4, 128], F32, st)
        for z in range(2):
            k.memset("pool", S[z][:], 0.0, writes=[S[z]])
        def tile_body(step, z, j, it):
            if True:
                ti = step if z == 0 else NT - 1 - step
                t0 = ti * 128
                if ti % 32 == (0 if z == 0 else 31) and step > 0:
                    first_of = ti // 32
                    if (z == 0 and first_of == 1) or (z == 1 and first_of == 0):
                        k.ts("dve", S[z][:], S[z][:], pvs(cx, "link")[0:64, :], None, ALU.mult, reads=[S[z], cx.pv], writes=[S[z]])
                    else:
                        k.memset("pool", S[z][:], 0.0, writes=[S[z]])
                k.dma("sp", qf[j][:], hd(R_HQ)[:, :, t0:t0 + 128], writes=[qf[j]])
                k.dma("sp", pf[j][:], hd(R_HF if z == 0 else R_HB)[:, :, t0:t0 + 128], writes=[pf[j]])
                k.dma("sp", vf[j][:], vview[:, 2 * ti:2 * ti + 2, 256:512], writes=[vf[j]])
                k.copy("pool", vb[j][:].rearrange("s c h d -> s c (h d)"), vf[j][:], reads=[vf[j]], writes=[vb[j]])
                lbz = bc(lb[:, z, :].rearrange("p (h o) -> p h o", o=1), [64, 4, 128])
                omz = bc(oml[:, z, :].rearrange("p (h o) -> p h o", o=1), [64, 4, 128])
                yield
                k.act(e_[j][:], pf[j][:], AF.Exp, scale=-1.0, reads=[pf[j]], writes=[e_[j]])
                k.act(A_[j][:], e_[j][:], AF.Ln, bias=cx.one_col[0:64, :], reads=[e_[j], cx.one_col], writes=[A_[j]])
                k.tt("dve", B_[j][:], e_[j][:], lbz, ALU.mult, reads=[e_[j], lb], writes=[B_[j]])
                k.act(B_[j][:], B_[j][:], AF.Ln, bias=cx.one_col[0:64, :], reads=[B_[j], cx.one_col], writes=[B_[j]])
                k.tt("dve", gl[j][:], B_[j][:], A_[j][:], ALU.subtract, reads=[B_[j], A_[j]], writes=[gl[j]])
                k.act(r_[j][:], A_[j][:], AF.Exp, scale=-1.0, reads=[A_[j]], writes=[r_[j]])
                k.tt("pool", kk_[j][:], e_[j][:], r_[j][:], ALU.mult, reads=[e_[j], r_[j]], writes=[kk_[j]])
                k.tt("pool", kk_[j][:], kk_[j][:], omz, ALU.mult, reads=[kk_[j], oml], writes=[kk_[j]])
                yield
                flat = lambda t_: t_[:].rearrange("p h t -> p (h t)")
                k.op("dve", lambda hh, j=j: hh.tensor_tensor_scan(flat(b_[j]), flat(rst), flat(gl[j]), 0.0, ALU.mult, ALU.add),
                     reads=[rst, gl[j]], writes=[b_[j]])
                b4 = b_[j][:].rearrange("p h (c t) -> p h c t", c=2)
                if z == 1:
                    k.tt("dve", arg[j][:], gl[j][:], b_[j][:], ALU.subtract, reads=[gl[j], b_[j]], writes=[arg[j]])
                    k.copy("pool", cc[j][:, :, :, 1:2], b4[:, :, :, 63:64], reads=[b_[j]], writes=[cc[j]])
                    k.tt("dve", b4, arg[j][:].rearrange("p h (c t) -> p h c t", c=2), bc(cc[j][:, :, :, 1:2], [64, 4, 2, 64]), ALU.add,
                         reads=[arg[j], cc[j]], writes=[b_[j]])
                    k.copy("pool", cc[j][:, :, :, 0:1], b4[:, :, :, 32:33], reads=[b_[j]], writes=[cc[j]])
                else:
                    k.copy("pool", cc[j][:, :, :, 1:2], b4[:, :, :, 63:64], reads=[b_[j]], writes=[cc[j]])
                    k.copy("pool", cc[j][:, :, :, 0:1], b4[:, :, :, 31:32], reads=[b_[j]], writes=[cc[j]])
                k.tt("pool", cc[j][:, :, :, 2:3], cc[j][:, :, :, 1:2], cc[j][:, :, :, 0:1], ALU.subtract, reads=[cc[j]], writes=[cc[j]])
                k.act(ec[j][:], cc[j][:], AF.Exp, reads=[cc[j]], writes=[ec[j]])
                k.tt("dve", arg[j][:].rearrange("p h (c t) -> p h c t", c=2), b4, bc(cc[j][:, :, :, 0:1], [64, 4, 2, 64]), ALU.subtract,
                     reads=[b_[j], cc[j]], writes=[arg[j]])
                k.act(E1[j][:], arg[j][:], AF.Exp, reads=[arg[j]], writes=[E1[j]])
                k.tt("dve", qt[j][:], qf[j][:], E1[j][:], ALU.mult, reads=[qf[j], E1[j]], writes=[qt[j]])
                k.act(E1[j][:], arg[j][:], AF.Exp, scale=-1.0, reads=[arg[j]], writes=[E1[j]])
                k.tt("dve", kt[j][:], kk_[j][:], E1[j][:], ALU.mult, reads=[kk_[j], E1[j]], writes=[kt[j]])
                yield
                pa = p_att[it % 2]
                for h in range(4):
                    for c in range(2):
                        sl = slice(c * 64, (c + 1) * 64)
                        k.mm(pa[:, h * 2 + c, :], kt[j][:, h, sl], qt[j][:, h, sl], reads=[kt[j], qt[j]], writes=[pa])
                msk = cx.m_le64 if z == 0 else cx.m_ge64
                k.tt("dve", attm[j][:], pa[:], bc(msk[0:64, 0:64].rearrange("p (o t) -> p o t", o=1), [64, 8, 64]), ALU.mult,
                     reads=[pa, msk], writes=[attm[j]])
                for h in range(4):
                    for c in range(2):
                        sl = slice(c * 64, (c + 1) * 64)
                        k.tr(p_tr[:, h * 2 + c, :], kt[j][:, h, sl], cx.ident[0:64, 0:64], reads=[kt[j], cx.ident], writes=[p_tr])
                k.copy("act", ktT[j][:], p_tr[:], reads=[p_tr], writes=[ktT[j]])
                yield
                po = p_o[it % 2]
                for c in ((0, 1) if z == 0 else (1, 0)):
                    sl = slice(c * 64, (c + 1) * 64)
                    k.tt("dve", Sin[z][:], S[z][:], bc(ec[j][:, :, c, 0:1], [64, 4, 64]), ALU.mult, reads=[S[z], ec[j]], writes=[Sin[z]])
                    for h in range(4):
                        k.mm(po[:, h * 2 + c, :], vb[j][:, c, h, :], attm[j][:, h * 2 + c, :], start=True, stop=False,
                             reads=[vb[j], attm[j]], writes=[po])
                        k.mm(po[:, h * 2 + c, :], Sin[z][:, h, :], qt[j][:, h, sl], start=False, stop=True,
                             reads=[Sin[z], qt[j]], writes=[po])
                    for h in range(4):
                        k.mm(p_kv[:, h, :], ktT[j][:, h * 2 + c, :], vb[j][:, c, h, :], reads=[ktT[j], vb[j]], writes=[p_kv])
                    k.tt("dve", tmp[j][:], p_kv[:], bc(ec[j][:, :, c, 2:3], [64, 4, 64]), ALU.mult, reads=[p_kv, ec[j]], writes=[tmp[j]])
                    k.tt("dve", S[z][:], S[z][:], bc(ec[j][:, :, c, 1:2], [64, 4, 64]), ALU.mult, reads=[S[z], ec[j]], writes=[S[z]])
                    k.tt("dve", S[z][:], S[z][:], tmp[j][:], ALU.add, reads=[S[z], tmp[j]], writes=[S[z]])
                    yield
                k.copy("act", osb[j][:].rearrange("p h (c t) -> p (h c) t", c=2), po[:], reads=[po], writes=[osb[j]])
                ofv = OF.rearrange("(h d) t -> d h t", d=64)[:, :, t0:t0 + 128]
                if step < NT // 2:
                    k.dma("sp", ofv, osb[j][:], reads=[osb[j]], writes=[cx.of_tok], owner=osb[j])
                else:
                    k.dma("sp", ofw[j][:], ofv, reads=[cx.of_tok], writes=[ofw[j]])
                    k.tt("dve", osb[j][:], osb[j][:], ofw[j][:], ALU.add, reads=[osb[j], ofw[j]], writes=[osb[j]])
                    k.act(sq[j][:], osb[j][:], AF.Square, reads=[osb[j]], writes=[sq[j]])
                    k.mm(p_n[:].rearrange("p h t -> p (h t)"), ones64[:], sq[j][:].rearrange("p h t -> p (h t)"),
                         reads=[ones64, sq[j]], writes=[p_n])
                    k.act(rs[j][:], p_n[:], AF.Sqrt, bias=cx.eps6[0:64, :], scale=1.0 / 64, reads=[p_n, cx.eps6], writes=[rs[j]])
                    k.op("dve", lambda hh, j=j: hh.reciprocal(rs[j][:], rs[j][:]), reads=[rs[j]], writes=[rs[j]])
                    k.tt("dve", osb[j][:], osb[j][:], rs[j][:], ALU.mult, reads=[osb[j], rs[j]], writes=[osb[j]])
                    k.tt("pool", osb[j][:], osb[j][:], bc(p64(cx, "hgn", 0, 4).rearrange("p (h o) -> p h o", o=1), [64, 4, 128]), ALU.mult,
                         reads=[osb[j], cx.pv64], writes=[osb[j]])
                    k.dma("sp", gf[j][:], hd(R_HG)[:, :, t0:t0 + 128], writes=[gf[j]])
                    k.act(sq[j][:], gf[j][:], AF.Exp, scale=-1.0, reads=[gf[j]], writes=[sq[j]])
                    k.ts("dve", sq[j][:], sq[j][:], 1.0, None, ALU.add, reads=[sq[j]], writes=[sq[j]])
                    k.op("dve", lambda hh, j=j: hh.reciprocal(sq[j][:], sq[j][:]), reads=[sq[j]], writes=[sq[j]])
                    k.tt("pool", sq[j][:], sq[j][:], gf[j][:], ALU.mult, reads=[sq[j], gf[j]], writes=[sq[j]])
                    k.tt("dve", ob[j][:], osb[j][:], sq[j][:], ALU.mult, reads=[osb[j], sq[j]], writes=[ob[j]])
                    k.dma("sp", cx.oT[512:768, t0:t0 + 128].rearrange("(h d) t -> d h t", d=64), ob[j][:], reads=[ob[j]])

        it = 0
        for step in range(NT):
            if step % 16 == 0 and step > 0:
                k.sync_all()
            gens = []
            for z in range(2):
                it += 1
                gens.append(tile_body(step, z, (it - 1) % NB, it))
            while gens:
                for g_ in list(gens):
                    try:
                        next(g_)
                    except StopIteration:
                        gens.remove(g_)


def phase_rwkv(k, cx):
    import os
    RWM = int(os.environ.get("RW_MODE", "0"))
    nc = k.nc
    OF = nc.dram_tensor("rw_of" + getattr(cx, "sfx", ""), [256, T], F32).ap()
    zT = cx.zT
    hd = lambda r0: zT(r0, r0 + 256).rearrange("(h d) t -> d h t", d=64)
    with k.scope() as st:
        NB = 2
        def mk(name, shape, dt=F32, nb=NB):
            return [k.sb("r_%s%d" % (name, j), shape, dt, st) for j in range(nb)]
        def one(name, shape, dt=F32):
            return k.sb("r_" + name, shape, dt, st)
        stg = one("stg", [128, 256], F32)
        wup = one("wup", [64, 2, 256], BF16)
        aup = one("aup", [64, 2, 256], BF16)
        gup = one("gup", [128, 256], BF16)
        for z in range(2):
            k.dma("sp", stg[0:64, :], cx.wup[z * 64:(z + 1) * 64, :], writes=[stg])
            k.copy("dve", wup[:, z, :], stg[0:64, :], reads=[stg], writes=[wup])
            k.dma("sp", stg[0:64, :], cx.aup[z * 64:(z + 1) * 64, :], writes=[stg])
            k.copy("dve", aup[:, z, :], stg[0:64, :], reads=[stg], writes=[aup])
        k.dma("sp", stg[:], cx.gup, writes=[stg])
        k.copy("dve", gup[:], stg[:], reads=[stg], writes=[gup])
        rst = one("rst", [64, 4, 128], F32)
        k.memset("pool", rst[:], 1.0, writes=[rst])
        k.memset("pool", rst[:, :, 0:1], 0.0, writes=[rst])
        ones64 = one("ones", [64, 64], F32)
        k.memset("pool", ones64[:], 1.0, writes=[ones64])
        omka = one("omka", [64, 4], F32)
        k.ts("dve", omka[:], p64(cx, "ka", 0, 4), -1.0, 1.0, ALU.mult, ALU.add, reads=[cx.pv64], writes=[omka])
        identb = one("identb", [128, 4, 128], BF16)
        for h in range(4):
            k.copy("dve", identb[:, h, :], cx.identf[:], reads=[cx.identf], writes=[identb])
        epsln = one("epsln", [64, 1], F32)
        k.memset("pool", epsln[:], 64e-5, writes=[epsln])
        H = [one("H%d" % z, [64, 4, 64], F32) for z in range(2)]
        Hb = [one("Hb%d" % z, [64, 4, 64], BF16) for z in range(2)]
        for z in range(2):
            k.memset("pool", H[z][:], 0.0, writes=[H[z]])
            k.memset("pool", Hb[z][:], 0.0, writes=[Hb[z]])
        X = {n: mk("x" + n, [64, 4, 130]) for n in ("r", "k", "v", "l")}
        xg = mk("xg", [128, 130])
        F = {n: mk("f" + n, [64, 4, 128]) for n in ("r", "k", "v", "l")}
        d0, d1 = mk("d0", [64, 4, 128]), mk("d1", [64, 4, 128])
        fg = mk("fg", [128, 128])
        sgb = mk("sgb", [128, 128], BF16)
        lin = mk("lin", [64, 2, 128], BF16)
        lw, av = mk("lw", [64, 4, 128]), mk("av", [64, 4, 128])
        t1, t2 = mk("t1", [64, 4, 128]), mk("t2", [64, 4, 128])
        kkn, ktt, bb = mk("kkn", [64, 4, 128]), mk("ktt", [64, 4, 128]), mk("bb", [64, 4, 128])
        cl = mk("cl", [64, 4, 128])
        Ep, Em, Ek, El = (mk(n, [64, 4, 128]) for n in ("Ep", "Em", "Ek", "El"))
        gL = mk("gL", [64, 4, 1])
        RgT, KKgT, BiT, KTiT, XT, YT, VT = (mk(n, [64, 4, 128], BF16) for n in ("RgT", "KKgT", "BiT", "KTiT", "XT", "YT", "VT"))
        ZB = []
        for zz in range(2):
            ZB.append((
                [one("P%d_%d" % (i, zz), [128, 4, 128], BF16) for i in range(2)],
                [one("PT%d_%d" % (i, zz), [128, 4, 128], BF16) for i in range(2)],
                [one("TT%d_%d" % (i, zz), [128, 4, 128], BF16) for i in range(2)],
                one("AktT%d" % zz, [128, 4, 128], BF16), one("BbT%d" % zz, [128, 4, 128], BF16),
                one("BktT%d" % zz, [128, 4, 128], BF16), one("W2%d" % zz, [128, 4, 128], BF16),
                one("VTM%d" % zz, [128, 4, 64], BF16), one("XTM%d" % zz, [128, 4, 64], BF16),
                one("YTM%d" % zz, [128, 4, 64], BF16), one("TKT%d" % zz, [128, 4, 128], BF16),
                one("nTAV%d" % zz, [128, 4, 64], BF16), one("McT%d" % zz, [64, 4, 64], BF16),
                one("RhT%d" % zz, [64, 4, 128], BF16)))
        osb, ofw, o2 = mk("osb", [64, 4, 128]), mk("ofw", [64, 4, 128]), mk("o2", [64, 4, 128])
        ob = mk("ob", [64, 4, 128], BF16)
        pA = [k.ps("r_pA%d" % j, [128, 4, 128], F32, st) for j in range(3)]
        p_t = k.ps("r_pt", [128, 4, 64], BF16, st)
        p_u = k.ps("r_pu", [128, 4, 128], F32, st)
        p_s = k.ps("r_ps", [64, 4, 64], F32, st)
        p_o = [k.ps("r_po%d" % j, [64, 4, 128], F32, st) for j in range(2)]
        pac = [0]

        def nextA():
            pac[0] += 1
            return pA[pac[0] % 3]
        mu = lambda jmu, off: bc(p64(cx, "rmu", jmu * 16 + off, 4).rearrange("p (h o) -> p h o", o=1), [64, 4, 128])
        b4 = lambda name: bc(p64(cx, name, 0, 4).rearrange("p (h o) -> p h o", o=1), [64, 4, 128])
        def tile_body(step, z, j):
            Pm, PmT, TT, AktT, BbT, BktT, W2, VTM, XTM, YTM, TKT, nTAV, McT, RhT = ZB[z]
            if True:
                ti = step if z == 0 else NT - 1 - step
                t0 = ti * 128
                second = step >= NT // 2
                if ti % 32 == (0 if z == 0 else 31) and step > 0:
                    first_of = ti // 32
                    if (z == 0 and first_of == 1) or (z == 1 and first_of == 0):
                        k.ts("dve", H[z][:], H[z][:], pvs(cx, "link")[0:64, :], None, ALU.mult, reads=[H[z], cx.pv], writes=[H[z]])
                        k.copy("dve", Hb[z][:], H[z][:], reads=[H[z]], writes=[Hb[z]])
                    else:
                        k.memset("pool", H[z][:], 0.0, writes=[H[z]])
                        k.memset("pool", Hb[z][:], 0.0, writes=[Hb[z]])
                lo, hi = max(t0 - 1, 0), min(t0 + 129, T)
                c0, c1 = lo - (t0 - 1), hi - (t0 - 1)
                for n, r0 in (("r", R_RW), ("k", R_RW + 256), ("v", R_RW + 512), ("l", R_RW + 768)):
                    k.dma("sp", X[n][j][:, :, c0:c1], hd(r0)[:, :, lo:hi], writes=[X[n][j]])
                k.dma("sp", xg[j][:, c0:c1], zT(R_RW + 1024, R_RW + 1152)[:, lo:hi], writes=[xg[j]])
                allx = [X[n][j] for n in ("r", "k", "v", "l")] + [xg[j]]
                for side, col in ((0, 0), (1, 129)):
                    at_edge = (ti % 32 == 0) if side == 0 else (ti % 32 == 31)
                    if not at_edge:
                        continue
                    linked = (ti == 32 and side == 0) or (ti == 31 and side == 1)
                    for xb_ in allx:
                        sl = xb_[:, :, col:col + 1] if xb_ is not xg[j] else xb_[:, col:col + 1]
                        npart = 64 if xb_ is not xg[j] else 128
                        if linked:
                            k.ts("dve", sl, sl, pvs(cx, "link")[0:npart, :], None, ALU.mult, reads=[xb_, cx.pv], writes=[xb_])
                        else:
                            k.memset("pool", sl, 0.0, writes=[xb_])
                yield
                for qi, n in enumerate(("r", "k", "v", "l")):
                    x_ = X[n][j]
                    e1, e2 = ("dve", "pool") if qi % 2 == 0 else ("pool", "dve")
                    k.tt(e1, d0[j][:], x_[:, :, 0:128], x_[:, :, 1:129], ALU.subtract, reads=[x_], writes=[d0[j]])
                    k.tt(e1, d0[j][:], d0[j][:], mu(0, qi * 4), ALU.mult, reads=[d0[j], cx.pv64], writes=[d0[j]])
                    k.tt(e2, d1[j][:], x_[:, :, 2:130], x_[:, :, 1:129], ALU.subtract, reads=[x_], writes=[d1[j]])
                    k.tt(e2, d1[j][:], d1[j][:], mu(1, qi * 4), ALU.mult, reads=[d1[j], cx.pv64], writes=[d1[j]])
                    k.tt(e1, d0[j][:], d0[j][:], d1[j][:], ALU.add, reads=[d0[j], d1[j]], writes=[d0[j]])
                    k.tt(e1, F[n][j][:], d0[j][:], x_[:, :, 1:129], ALU.add, reads=[d0[j], x_], writes=[F[n][j]])
                if RWM == 1:
                    return
                rf, kf, vf, lf = F["r"][j], F["k"][j], F["v"][j], F["l"][j]
                yield
                k.act(t1[j][:, 0, :], lf[:, z, :], AF.Exp, scale=-2.0, reads=[lf], writes=[t1[j]])
                k.ts("dve", t1[j][:, 0, :], t1[j][:, 0, :], 1.0, None, ALU.add, reads=[t1[j]], writes=[t1[j]])
                k.op("dve", lambda hh, j=j: hh.reciprocal(t1[j][:, 0, :], t1[j][:, 0, :]), reads=[t1[j]], writes=[t1[j]])
                k.ts("dve", lin[j][:, 0, :], t1[j][:, 0, :], 2.0, -1.0, ALU.mult, ALU.add, reads=[t1[j]], writes=[lin[j]])
                k.copy("pool", lin[j][:, 1, :], lf[:, 2 + z, :], reads=[lf], writes=[lin[j]])
                pw = nextA()
                for h in range(4):
                    k.mm(pw[0:64, h, :], wup[:, z, h * 64:(h + 1) * 64], lin[j][:, 0, :], reads=[wup, lin[j]], writes=[pw])
                k.tt("dve", t1[j][:], pw[0:64, :, :], bc(p64(cx, "w0", z * 4, 4).rearrange("p (h o) -> p h o", o=1), [64, 4, 128]), ALU.add,
                     reads=[pw, cx.pv64], writes=[t1[j]])
                pa_ = nextA()
                for h in range(4):
                    k.mm(pa_[0:64, h, :], aup[:, z, h * 64:(h + 1) * 64], lin[j][:, 1, :], reads=[aup, lin[j]], writes=[pa_])
                k.tt("dve", t2[j][:], pa_[0:64, :, :], bc(p64(cx, "a0", z * 4, 4).rearrange("p (h o) -> p h o", o=1), [64, 4, 128]), ALU.add,
                     reads=[pa_, cx.pv64], writes=[t2[j]])
                for src, dst, mulc in ((t1[j], lw[j], -RW_DECAY), (t2[j], av[j], 1.0)):
                    k.act(src[:], src[:], AF.Exp, scale=-1.0, reads=[src], writes=[src])
                    k.ts("dve", src[:], src[:], 1.0, None, ALU.add, reads=[src], writes=[src])
                    k.op("dve", lambda hh, src=src: hh.reciprocal(src[:], src[:]), reads=[src], writes=[src])
                    k.ts("pool", dst[:], src[:], mulc, None, ALU.mult, reads=[src], writes=[dst])
                if RWM == 2:
                    return
                yield
                k.tt("pool", t1[j][:], kf[:], b4("kk"), ALU.mult, reads=[kf, cx.pv64], writes=[t1[j]])
                k.tt("pool", t2[j][:], t1[j][:], t1[j][:], ALU.mult, reads=[t1[j]], writes=[t2[j]])
                pn = nextA()
                k.mm(pn[0:64, :, :].rearrange("p h t -> p (h t)"), ones64[:], t2[j][:].rearrange("p h t -> p (h t)"),
                     reads=[ones64, t2[j]], writes=[pn])
                k.ts("dve", t2[j][:], pn[0:64, :, :], 1e-24, None, ALU.max, reads=[pn], writes=[t2[j]])
                k.act(t2[j][:], t2[j][:], AF.Ln, reads=[t2[j]], writes=[t2[j]])
                k.act(t2[j][:], t2[j][:], AF.Exp, scale=-0.5, reads=[t2[j]], writes=[t2[j]])
                k.tt("dve", kkn[j][:], t1[j][:], t2[j][:], ALU.mult, reads=[t1[j], t2[j]], writes=[kkn[j]])
                k.tt("pool", t1[j][:], av[j][:], b4("ka"), ALU.mult, reads=[av[j], cx.pv64], writes=[t1[j]])
                k.tt("pool", t1[j][:], t1[j][:], bc(omka[:, :].rearrange("p (h o) -> p h o", o=1), [64, 4, 128]), ALU.add,
                     reads=[t1[j], omka], writes=[t1[j]])
                k.tt("pool", ktt[j][:], t1[j][:], kf[:], ALU.mult, reads=[t1[j], kf], writes=[ktt[j]])
                k.tt("dve", bb[j][:], kkn[j][:], av[j][:], ALU.mult, reads=[kkn[j], av[j]], writes=[bb[j]])
                yield
                flat = lambda t_: t_[:].rearrange("p h t -> p (h t)")
                k.op("dve", lambda hh, j=j: hh.tensor_tensor_scan(flat(cl[j]), flat(rst), flat(lw[j]), 0.0, ALU.mult, ALU.add),
                     reads=[rst, lw[j]], writes=[cl[j]])
                k.copy("pool", gL[j][:], cl[j][:, :, 127:128], reads=[cl[j]], writes=[gL[j]])
                if z == 1:
                    k.tt("dve", cl[j][:], lw[j][:], cl[j][:], ALU.subtract, reads=[lw[j], cl[j]], writes=[cl[j]])
                    k.tt("dve", cl[j][:], cl[j][:], bc(gL[j][:], [64, 4, 128]), ALU.add, reads=[cl[j], gL[j]], writes=[cl[j]])
                k.act(Ep[j][:], cl[j][:], AF.Exp, reads=[cl[j]], writes=[Ep[j]])
                k.act(Em[j][:], cl[j][:], AF.Exp, scale=-1.0, reads=[cl[j]], writes=[Em[j]])
                k.tt("pool", t1[j][:], cl[j][:], lw[j][:], ALU.subtract, reads=[cl[j], lw[j]], writes=[t1[j]])
                k.act(Ek[j][:], t1[j][:], AF.Exp, reads=[t1[j]], writes=[Ek[j]])
                k.tt("pool", t2[j][:], bc(gL[j][:], [64, 4, 128]), cl[j][:], ALU.subtract, reads=[cl[j], gL[j]], writes=[t2[j]])
                k.act(El[j][:], t2[j][:], AF.Exp, reads=[t2[j]], writes=[El[j]])
                k.act(gL[j][:], gL[j][:], AF.Exp, reads=[gL[j]], writes=[gL[j]])
                k.tt("dve", RgT[j][:], rf[:], Ep[j][:], ALU.mult, reads=[rf, Ep[j]], writes=[RgT[j]])
                k.tt("pool", KKgT[j][:], kkn[j][:], Ek[j][:], ALU.mult, reads=[kkn[j], Ek[j]], writes=[KKgT[j]])
                k.tt("dve", BiT[j][:], bb[j][:], Em[j][:], ALU.mult, reads=[bb[j], Em[j]], writes=[BiT[j]])
                k.tt("pool", KTiT[j][:], ktt[j][:], Em[j][:], ALU.mult, reads=[ktt[j], Em[j]], writes=[KTiT[j]])
                k.tt("dve", XT[j][:], ktt[j][:], El[j][:], ALU.mult, reads=[ktt[j], El[j]], writes=[XT[j]])
                k.tt("pool", YT[j][:], bb[j][:], El[j][:], ALU.mult, reads=[bb[j], El[j]], writes=[YT[j]])
                k.copy("act", VT[j][:], vf[:], reads=[vf], writes=[VT[j]])
                if RWM == 3:
                    return
                if z == 0:
                    M1, M2, M3 = cx.m_lt, cx.m_gt, cx.m_le
                else:
                    M1, M2, M3 = cx.m_gt, cx.m_lt, cx.m_ge
                mb = lambda m_: bc(m_[:, :].rearrange("p (o t) -> p o t", o=1), [128, 4, 128])
                yield
                def pairprod(lhs, rhs, mask, out, neg=False):
                    p_ = nextA()
                    for h in range(4):
                        k.mm(p_[:, h, :], lhs[:, h, :], rhs[:, h, :], reads=[lhs, rhs], writes=[p_])
                    if neg:
                        k.stt("dve", out[:], p_[:], -1.0, mb(mask), ALU.mult, ALU.mult, reads=[p_, mask], writes=[out])
                    else:
                        k.tt("dve", out[:], p_[:], mb(mask), ALU.mult, reads=[p_, mask], writes=[out])
                pairprod(BiT[j], KKgT[j], M1, PmT[0], neg=True)
                pairprod(KKgT[j], BiT[j], M2, Pm[0], neg=True)
                pairprod(KTiT[j], KKgT[j], M1, AktT)
                pairprod(BiT[j], RgT[j], M3, BbT)
                pairprod(KTiT[j], RgT[j], M3, BktT)
                if RWM == 4:
                    return
                yield
                k.tt("pool", TT[0][:], PmT[0][:], identb[:], ALU.add, reads=[PmT[0], identb], writes=[TT[0]])
                cur = 0
                for lev in range(6):
                    nxt = 1 - cur
                    p1 = nextA()
                    for h in range(4):
                        k.mm(p1[:, h, :], PmT[cur][:, h, :], Pm[cur][:, h, :], reads=[PmT[cur], Pm[cur]], writes=[p1])
                    k.copy("act", Pm[nxt][:], p1[:], reads=[p1], writes=[Pm[nxt]])
                    if lev < 5:
                        p2 = nextA()
                        for h in range(4):
                            k.mm(p2[:, h, :], Pm[cur][:, h, :], PmT[cur][:, h, :], reads=[PmT[cur], Pm[cur]], writes=[p2])
                        k.copy("dve", PmT[nxt][:], p2[:], reads=[p2], writes=[PmT[nxt]])
                    p3 = nextA()
                    tcur, tnxt = TT[lev % 2], TT[(lev + 1) % 2]
                    for h in range(4):
                        k.mm(p3[:, h, :], Pm[nxt][:, h, :], tcur[:, h, :], reads=[Pm[nxt], tcur], writes=[p3])
                    k.tt("dve", tnxt[:], p3[:], tcur[:], ALU.add, reads=[p3, tcur], writes=[tnxt])
                    cur = nxt
                    yield
                if RWM == 5:
                    return
                TTf = TT[0]
                yield
                for src, dst, col in ((KKgT[j], W2, 0), (VT[j], VTM, None), (XT[j], XTM, None), (YT[j], YTM, None)):
                    for h in range(4):
                        k.tr(p_t[:, h, :], src[:, h, :], cx.ident[0:64, 0:64], reads=[src, cx.ident], writes=[p_t])
                    if col is None:
                        k.copy_rr(dst[:], p_t[:], reads=[p_t], writes=[dst])
                    else:
                        k.copy_rr(dst[:, :, 0:64], p_t[:], reads=[p_t], writes=[dst])
                yield
                pav = nextA()
                for h in range(4):
                    k.mm(pav[:, h, 0:64], AktT[:, h, :], VTM[:, h, :], reads=[AktT, VTM], writes=[pav])
                k.copy("act", W2[:, :, 64:128], pav[:, :, 0:64], reads=[pav], writes=[W2])
                for h in range(4):
                    k.mm(p_u[:, h, :], TTf[:, h, :], W2[:, h, :], reads=[TTf, W2], writes=[p_u])
                k.copy("act", TKT[:], p_u[:], reads=[p_u], writes=[TKT])
                k.ts("pool", nTAV[:], TKT[:, :, 64:128], -1.0, None, ALU.mult, reads=[TKT], writes=[nTAV])
                if RWM == 6:
                    return
                yield
                for h in range(4):
                    k.mm(p_s[:, h, :], TKT[:, h, 0:64], YTM[:, h, :], reads=[TKT, YTM], writes=[p_s])
                for h in range(4):
                    k.stt("dve", McT[:, h, :], cx.identf[0:64, 0:64], gL[j][:, h, :], p_s[:, h, :], ALU.mult, ALU.subtract,
                          reads=[cx.identf, gL[j], p_s], writes=[McT])
                pr_ = p_o[0]
                for h in range(4):
                    k.mm(pr_[:, h, :], TKT[:, h, 0:64], BbT[:, h, :], reads=[TKT, BbT], writes=[pr_])
                k.tt("dve", RhT[:], RgT[j][:], pr_[:], ALU.subtract, reads=[RgT[j], pr_], writes=[RhT])
                yield
                po = p_o[1]
                for h in range(4):
                    k.mm(po[:, h, :], VTM[:, h, :], BktT[:, h, :], start=True, stop=False, reads=[VTM, BktT], writes=[po])
                    k.mm(po[:, h, :], nTAV[:, h, :], BbT[:, h, :], start=False, stop=False, reads=[nTAV, BbT], writes=[po])
                    k.mm(po[:, h, :], Hb[z][:, h, :], RhT[:, h, :], start=False, stop=True, reads=[Hb[z], RhT], writes=[po])
                k.copy("act", osb[j][:], po[:], reads=[po], writes=[osb[j]])
                yield
                for h in range(4):
                    k.mm(p_s[:, h, :], XTM[:, h, :], VTM[:, h, :], start=True, stop=False, reads=[XTM, VTM], writes=[p_s])
                    k.mm(p_s[:, h, :], YTM[:, h, :], nTAV[:, h, :], start=False, stop=False, reads=[YTM, nTAV], writes=[p_s])
                    k.mm(p_s[:, h, :], McT[:, h, :], Hb[z][:, h, :], start=False, stop=True, reads=[McT, Hb[z]], writes=[p_s])
                k.copy("dve", H[z][:], p_s[:], reads=[p_s], writes=[H[z]])
                k.copy("act", Hb[z][:], H[z][:], reads=[H[z]], writes=[Hb[z]])
                if RWM == 7:
                    return
                yield
                ofv = OF.rearrange("(h d) t -> d h t", d=64)[:, :, t0:t0 + 128]
                if not second:
                    k.dma("sp", ofv, osb[j][:], reads=[osb[j]], writes=[cx.of_tok], owner=osb[j])
                    return
                k.dma("sp", ofw[j][:], ofv, reads=[cx.of_tok], writes=[ofw[j]])
                k.tt("dve", osb[j][:], osb[j][:], ofw[j][:], ALU.add, reads=[osb[j], ofw[j]], writes=[osb[j]])
                pm_ = nextA()
                k.mm(pm_[0:64, :, :].rearrange("p h t -> p (h t)"), ones64[:], osb[j][:].rearrange("p h t -> p (h t)"),
                     reads=[ones64, osb[j]], writes=[pm_])
                k.stt("dve", o2[j][:], pm_[0:64, :, :], -1.0 / 64, osb[j][:], ALU.mult, ALU.add, reads=[pm_, osb[j]], writes=[o2[j]])
                k.tt("pool", t1[j][:], o2[j][:], o2[j][:], ALU.mult, reads=[o2[j]], writes=[t1[j]])
                pv_ = nextA()
                k.mm(pv_[0:64, :, :].rearrange("p h t -> p (h t)"), ones64[:], t1[j][:].rearrange("p h t -> p (h t)"),
                     reads=[ones64, t1[j]], writes=[pv_])
                k.ts("dve", t2[j][:], pv_[0:64, :, :], 1.0 / 64, 64e-5, ALU.mult, ALU.add, reads=[pv_], writes=[t2[j]])
                k.act(t2[j][:], t2[j][:], AF.Ln, reads=[t2[j]], writes=[t2[j]])
                k.act(t2[j][:], t2[j][:], AF.Exp, scale=-0.5, reads=[t2[j]], writes=[t2[j]])
                k.tt("dve", o2[j][:], o2[j][:], t2[j][:], ALU.mult, reads=[o2[j], t2[j]], writes=[o2[j]])
                k.tt("pool", o2[j][:], o2[j][:], b4("lnw"), ALU.mult, reads=[o2[j], cx.pv64], writes=[o2[j]])
                k.tt("pool", o2[j][:], o2[j][:], b4("lnb"), ALU.add, reads=[o2[j], cx.pv64], writes=[o2[j]])
                k.tt("dve", t1[j][:], rf[:], kf[:], ALU.mult, reads=[rf, kf], writes=[t1[j]])
                k.tt("pool", t1[j][:], t1[j][:], b4("rk"), ALU.mult, reads=[t1[j], cx.pv64], writes=[t1[j]])
                pb_ = nextA()
                k.mm(pb_[0:64, :, :].rearrange("p h t -> p (h t)"), ones64[:], t1[j][:].rearrange("p h t -> p (h t)"),
                     reads=[ones64, t1[j]], writes=[pb_])
                k.tt("dve", t1[j][:], pb_[0:64, :, :], vf[:], ALU.mult, reads=[pb_, vf], writes=[t1[j]])
                k.tt("pool", o2[j][:], o2[j][:], t1[j][:], ALU.add, reads=[o2[j], t1[j]], writes=[o2[j]])
                k.tt("dve", fg[j][:], xg[j][:, 0:128], xg[j][:, 1:129], ALU.subtract, reads=[xg[j]], writes=[fg[j]])
                k.ts("dve", fg[j][:], fg[j][:], pvs(cx, "mugd", 0), None, ALU.mult, reads=[fg[j], cx.pv], writes=[fg[j]])
                k.tt("dve", fg[j][:], fg[j][:], xg[j][:, 1:129], ALU.add, reads=[fg[j], xg[j]], writes=[fg[j]])
                fg2 = d0[j][:].rearrange("p h t -> p (h t)")
                k.tt("dve", xg[j][:, 0:128], xg[j][:, 2:130], xg[j][:, 1:129], ALU.subtract, reads=[xg[j]], writes=[xg[j]])
                k.stt("dve", fg[j][:], xg[j][:, 0:128], pvs(cx, "mugd", 1), fg[j][:], ALU.mult, ALU.add, reads=[xg[j], fg[j], cx.pv], writes=[fg[j]])
                k.act(fg[j][:], fg[j][:], AF.Exp, scale=-1.0, reads=[fg[j]], writes=[fg[j]])
                k.ts("dve", fg[j][:], fg[j][:], 1.0, None, ALU.add, reads=[fg[j]], writes=[fg[j]])
                k.op("dve", lambda hh, j=j: hh.reciprocal(fg[j][:], fg[j][:]), reads=[fg[j]], writes=[fg[j]])
                k.copy("pool", sgb[j][:], fg[j][:], reads=[fg[j]], writes=[sgb[j]])
                pg_ = nextA()
                for h in range(4):
                    k.mm(pg_[0:64, h, :], gup[:, h * 64:(h + 1) * 64], sgb[j][:], reads=[gup, sgb[j]], writes=[pg_])
                k.tt("dve", ob[j][:], o2[j][:], pg_[0:64, :, :], ALU.mult, reads=[o2[j], pg_], writes=[ob[j]])
                k.dma("sp", cx.oT[768:1024, t0:t0 + 128].rearrange("(h d) t -> d h t", d=64), ob[j][:], reads=[ob[j]])

        it = 0
        for step in range(NT):
            if step % 8 == 0 and step > 0:
                k.sync_all()
            gens = []
            for z in range(2):
                gens.append(tile_body(step, z, it % NB))
                it += 1
            while gens:
                for g_ in list(gens):
                    try:
                        next(g_)
                    except StopIteration:
                        gens.remove(g_)


def layernorm_fm(k, cx, st, hbuf, sq, out, gname, bname, p_m, p_v, tmp_m, tmp_r):
    k.act(sq[:], hbuf[:], AF.Square, reads=[hbuf], writes=[sq])
    for mc in range(8):
        k.mm(p_m[:], cx.ones[:], hbuf[:, mc, :], start=(mc == 0), stop=(mc == 7), reads=[cx.ones, hbuf], writes=[p_m])
    for mc in range(8):
        k.mm(p_v[:], cx.ones[:], sq[:, mc, :], start=(mc == 0), stop=(mc == 7), reads=[cx.ones, sq], writes=[p_v])
    k.op("act", lambda h: h.mul(tmp_m[:], p_m[:], 1.0 / D), reads=[p_m], writes=[tmp_m])
    k.tt("pool", tmp_r[:], tmp_m[:], tmp_m[:], ALU.mult, reads=[tmp_m], writes=[tmp_r])
    k.stt("dve", tmp_r[:], p_v[:], 1.0 / D, tmp_r[:], ALU.mult, ALU.subtract, reads=[p_v, tmp_r], writes=[tmp_r])
    k.act(tmp_r[:], tmp_r[:], AF.Ln, bias=cx.eps5[:], reads=[tmp_r, cx.eps5], writes=[tmp_r])
    k.act(tmp_r[:], tmp_r[:], AF.Exp, scale=-0.5, reads=[tmp_r], writes=[tmp_r])
    for mc in range(8):
        k.tt("pool", sq[:, mc, :], hbuf[:, mc, :], tmp_m[:], ALU.subtract, reads=[hbuf, tmp_m], writes=[sq])
        k.tt("pool" if mc % 2 else "dve", sq[:, mc, :], sq[:, mc, :], tmp_r[:], ALU.mult, reads=[sq, tmp_r], writes=[sq])
        k.ts("dve", out[:, mc, :], sq[:, mc, :], pvs(cx, gname, mc), pvs(cx, bname, mc), ALU.mult, ALU.add,
             reads=[sq, cx.pv], writes=[out])


def phase_outproj(k, cx, xT_res):
    with k.scope() as st:
        stg = k.sb("o_stg", [128, 8, 512], F32, st)
        wb = k.sb("o_wb", [128, 8, D], BF16, st)
        wv = cx.wout.rearrange("(c p) n -> p c n", p=128)
        for jj in range(2):
            k.dma("sp", stg[:], wv[:, :, jj * 512:(jj + 1) * 512], writes=[stg])
            for c in range(8):
                k.copy_rr(wb[:, c, jj * 512:(jj + 1) * 512], stg[:, c, :], reads=[stg], writes=[wb])
        rt = k.sb("o_rt", [128, 8, 16], F32, st)
        k.dma("sp", rt[:], cx.router.rearrange("(c p) e -> p c e", p=128), writes=[rt])
        ob = k.sb("o_ob", [128, 8, 512], BF16, st)
        xf = k.sb("o_xf", [128, 8, 512], F32, st)
        hb = k.sb("o_hb", [128, 8, 512], F32, st)
        sq = k.sb("o_sq", [128, 8, 512], F32, st)
        x1 = k.sb("o_x1", [128, 8, 512], F32, st)
        tm = k.sb("o_tm", [128, 512], F32, st)
        tr_ = k.sb("o_tr", [128, 512], F32, st)
        lg = k.sb("o_lg", [128, 4, 16], F32, st)
        mx = k.sb("o_mx", [128, 4], F32, st)
        sm = k.sb("o_sm", [128, 4], F32, st)
        p_mix = [k.ps("o_pm%d" % j, [128, 512], F32, st) for j in range(2)]
        p_m = k.ps("o_pmean", [128, 512], F32, st)
        p_v = k.ps("o_pvar", [128, 512], F32, st)
        p_l = k.ps("o_pl", [128, 4, 16], F32, st)
        for i in k.uloop(T // 512, every=4):
            tsl = bass.ts(i, 512)
            k.dma("sp", ob[:], cx.oT.rearrange("(c p) t -> p c t", p=128)[:, :, tsl], writes=[ob])
            k.dma("sp", xf[:], xT_res.rearrange("(c p) t -> p c t", p=128)[:, :, tsl], writes=[xf])
            for mc in range(8):
                pm = p_mix[mc % 2]
                for c in range(8):
                    k.mm(pm[:], wb[:, c, mc * 128:(mc + 1) * 128], ob[:, c, :], start=(c == 0), stop=(c == 7),
                         reads=[wb, ob], writes=[pm])
                k.stt("dve", hb[:, mc, :], xf[:, mc, :], ALPHA, pm[:], ALU.mult, ALU.add, reads=[xf, pm], writes=[hb])
            layernorm_fm(k, cx, st, hb, sq, x1, "ln1g", "ln1b", p_m, p_v, tm, tr_)
            k.dma("sp", cx.x1T.rearrange("(c p) t -> p c t", p=128)[:, :, tsl], x1[:], reads=[x1])
            for tt in range(4):
                for mc in range(8):
                    k.mm(p_l[:, tt, :], x1[:, mc, tt * 128:(tt + 1) * 128], rt[:, mc, :], start=(mc == 0), stop=(mc == 7),
                         reads=[x1, rt], writes=[p_l])
            k.copy("act", lg[:], p_l[:], reads=[p_l], writes=[lg])
            k.op("dve", lambda h: h.tensor_reduce(mx[:], lg[:], AX.X, ALU.max), reads=[lg], writes=[mx])
            k.tt("dve", lg[:], lg[:], bc(mx[:].rearrange("p (t o) -> p t o", o=1), [128, 4, 16]), ALU.subtract, reads=[lg, mx], writes=[lg])
            k.act(lg[:], lg[:], AF.Exp, reads=[lg], writes=[lg])
            k.op("dve", lambda h: h.tensor_reduce(sm[:], lg[:], AX.X, ALU.add), reads=[lg], writes=[sm])
            k.op("dve", lambda h: h.reciprocal(sm[:], sm[:]), reads=[sm], writes=[sm])
            k.tt("dve", lg[:], lg[:], bc(sm[:].rearrange("p (t o) -> p t o", o=1), [128, 4, 16]), ALU.mult, reads=[lg, sm], writes=[lg])
            k.dma("sp", cx.aff.rearrange("(n p) e -> p n e", p=128)[:, bass.ts(i, 4), :], lg[:], reads=[lg])


TB = 2048
NSB = TB // 512
CAP_P = 2 * 4 * 8192 // 16
CAP_S = 2 * 16 * 4096 // 16
N_BISECT = 30


def phase_thresholds(k, cx, aff_all):
    thr_u = [k.sb("thr_u%d" % u, [128, 16], F32) for u in range(3)]
    with k.scope() as st:
        affP = k.sb("t_affP", [128, 256, 16], F32, st)
        affS = k.sb("t_affS", [128, 512, 16], F32, st)
        cmpb = k.sb("t_cmp", [128, 512, 16], BF16, st)
        for c in range(4):
            k.dma("sp", affP[:, c * 64:(c + 1) * 64, :],
                  aff_all[c * T:c * T + 2 * UNIT, :].rearrange("(p j) e -> p j e", p=128), writes=[affP])
            k.dma("sp", affS[:, c * 32:(c + 1) * 32, :],
                  aff_all[c * T + 2 * UNIT:(c + 1) * T, :].rearrange("(p j) e -> p j e", p=128), writes=[affS])
        for c in range(4, 8):
            k.dma("sp", affS[:, 128 + (c - 4) * 96:128 + (c - 3) * 96, :],
                  aff_all[c * T:(c + 1) * T, :].rearrange("(p j) e -> p j e", p=128), writes=[affS])
        res = []
        for gname, aff, J, cap in (("P", affP, 256, CAP_P), ("S", affS, 512, CAP_S)):
            lo = k.sb("t_lo" + gname, [128, 16], F32, st)
            hi = k.sb("t_hi" + gname, [128, 16], F32, st)
            mid = k.sb("t_mid" + gname, [128, 16], F32, st)
            cnt = k.sb("t_cnt" + gname, [128, 16], F32, st)
            ge = k.sb("t_ge" + gname, [128, 16], F32, st)
            d1 = k.sb("t_d1" + gname, [128, 16], F32, st)
            d2 = k.sb("t_d2" + gname, [128, 16], F32, st)
            p_c = k.ps("t_pc" + gname, [128, 16], F32, st)
            k.memset("pool", lo[:], 0.0, writes=[lo])
            k.memset("pool", hi[:], 1.0, writes=[hi])
            for itn in range(N_BISECT):
                k.tt("dve", mid[:], lo[:], hi[:], ALU.add, reads=[lo, hi], writes=[mid])
                k.ts("dve", mid[:], mid[:], 0.5, None, ALU.mult, reads=[mid], writes=[mid])
                k.tt("dve", cmpb[:, 0:J, :], aff[:, 0:J, :], bc(mid[:].rearrange("p (o e) -> p o e", o=1), [128, J, 16]), ALU.is_gt,
                     reads=[aff, mid], writes=[cmpb])
                with k.nc.allow_low_precision("0/1 counts are exact in bf16 inputs, fp32 accumulate"):
                    k.op("dve", lambda h, J=J, cnt=cnt: h.tensor_reduce(cnt[:], cmpb[:, 0:J, :].rearrange("p j e -> p e j"), AX.X, ALU.add),
                         reads=[cmpb], writes=[cnt])
                k.mm(p_c[:], cx.ones[:], cnt[:], reads=[cx.ones, cnt], writes=[p_c])
                k.ts("dve", ge[:], p_c[:], float(cap), None, ALU.is_ge, reads=[p_c], writes=[ge])
                k.tt("pool", d1[:], mid[:], lo[:], ALU.subtract, reads=[mid, lo], writes=[d1])
                k.tt("pool", d2[:], hi[:], mid[:], ALU.subtract, reads=[mid, hi], writes=[d2])
                k.tt("pool", d1[:], d1[:], ge[:], ALU.mult, reads=[d1, ge], writes=[d1])
                k.tt("pool", d2[:], d2[:], ge[:], ALU.mult, reads=[d2, ge], writes=[d2])
                k.tt("pool", lo[:], lo[:], d1[:], ALU.add, reads=[lo, d1], writes=[lo])
                k.tt("pool", hi[:], mid[:], d2[:], ALU.add, reads=[mid, d2], writes=[hi])
            res.append(lo)
        thrP, thrS = res
        dif = k.sb("t_dif", [128, 16], F32, st)
        k.tt("dve", dif[:], thrP[:], thrS[:], ALU.subtract, reads=[thrP, thrS], writes=[dif])
        for u in range(3):
            k.stt("dve", thr_u[u][:], dif[:], pvs(cx, "isP", u), thrS[:], ALU.mult, ALU.add,
                  reads=[dif, thrS, cx.pv], writes=[thr_u[u]])
    return thr_u


def phase_ffn(k, cx, thr_u, x1T, aff_own, pT, xnT):
    nc = k.nc
    with k.scope() as st:
        stg = [k.sb("f_stg%d" % j, [128, 4096], F32, st) for j in range(2)]
        s8 = lambda b_: b_[:].rearrange("p (c n) -> p c n", c=8)
        s4 = lambda b_: b_[:].rearrange("p (c n) -> p c n", c=4)
        w1b = k.sb("f_w1b", [128, 8, 512], BF16, st)
        w3b = k.sb("f_w3b", [128, 8, 512], BF16, st)
        w2b = k.sb("f_w2b", [128, 4, 1024], BF16, st)
        pg = k.sb("f_pg", [128, 8, D], BF16, st)
        pp = k.sb("f_pp", [128, 2, D], BF16, st)
        for jj in range(2):
            k.dma("sp", s8(stg[0]), cx.ple_gate.rearrange("(c p) n -> p c n", p=128)[:, :, jj * 512:(jj + 1) * 512], writes=[stg[0]])
            for c in range(8):
                k.copy_rr(pg[:, c, jj * 512:(jj + 1) * 512], s8(stg[0])[:, c, :], reads=[stg[0]], writes=[pg])
        k.dma("sp", stg[1][:, 0:2048].rearrange("p (c n) -> p c n", c=2), cx.ple_proj.rearrange("(c p) n -> p c n", p=128), writes=[stg[1]])
        k.copy("dve", pp[:], stg[1][:, 0:2048].rearrange("p (c n) -> p c n", c=2), reads=[stg[1]], writes=[pp])
        sel = k.sb("f_sel", [16, 16, 128], BF16, st)
        with k.scope() as st2:
            self_ = k.sb("f_self", [16, 16, 128], F32, st2)
            k.memset("pool", self_[:], 0.0, writes=[self_])
            k.op("pool", lambda h: h.affine_select(self_[:], self_[:], [[1, 16], [0, 128]], ALU.not_equal, 1.0,
                                                   base=0, channel_multiplier=-1), reads=[self_], writes=[self_])
            k.copy("dve", sel[:], self_[:], reads=[self_], writes=[sel])
        x1b = k.sb("f_x1b", [128, 8, TB], BF16, st)
        yacc = k.sb("f_yacc", [128, 8, TB], F32, st)
        gmT = k.sb("f_gmT", [16, TB], BF16, st)
        afft = k.sb("f_afft", [128, 16], F32, st)
        mk_ = k.sb("f_mk", [128, 16], F32, st)
        G = [k.sb("f_G%d" % j, [128, 512], BF16, st) for j in range(2)]
        s1 = [k.sb("f_s1%d" % j, [128, 512], BF16, st) for j in range(2)]
        t3 = [k.sb("f_t3%d" % j, [128, 512], BF16, st) for j in range(2)]
        ub = k.sb("f_ub", [128, 8, 512], BF16, st)
        he = ub
        pb_ = k.sb("f_pb", [128, 2, 512], BF16, st)
        sg = [k.sb("f_sg%d" % j, [128, 512], F32, st) for j in range(2)]
        tm, tr_ = sg[0], sg[1]
        p_h1 = k.ps("f_ph1", [128, 512], F32, st)
        p_h3 = k.ps("f_ph3", [128, 512], F32, st)
        p_y = [k.ps("f_py%d" % j, [128, 512], F32, st) for j in range(2)]
        p_g = k.ps("f_pg_", [128, 512], F32, st)
        p_t = k.ps("f_pt", [16, 128], F32, st)
        p_m = k.ps("f_pm", [128, 512], F32, st)
        p_v = k.ps("f_pv", [128, 512], F32, st)
        x1v = x1T.rearrange("(c p) t -> p c t", p=128)
        for blk in k.uloop(T // TB):
            tb0 = blk * TB
            un = tb0 // UNIT
            for sb in range(NSB):
                s_ = stg[sb % 2]
                k.dma("sp", s8(s_), x1v[:, :, tb0 + sb * 512:tb0 + (sb + 1) * 512], writes=[s_])
                for c in range(8):
                    k.copy_rr(x1b[:, c, sb * 512:(sb + 1) * 512], s8(s_)[:, c, :], reads=[s_], writes=[x1b], engs=("act", "dve", "pool"))
            for tt in range(TB // 128):
                k.dma("sp", afft[:], aff_own[tb0 + tt * 128:tb0 + (tt + 1) * 128, :], writes=[afft])
                k.tt("dve", mk_[:], afft[:], thr_u[un][:], ALU.is_gt, reads=[afft, thr_u[un]], writes=[mk_])
                k.tt("dve", mk_[:], mk_[:], afft[:], ALU.mult, reads=[mk_, afft], writes=[mk_])
                k.op("pe", lambda h: h.transpose(p_t[:], mk_[:], cx.identf[:]), reads=[mk_, cx.identf], writes=[p_t])
                k.copy("act", gmT[:, tt * 128:(tt + 1) * 128], p_t[:], reads=[p_t], writes=[gmT])
            for e in range(16):
                k.dma("sp", s8(stg[0]), cx.w1[e].rearrange("(c p) n -> p c n", p=128), writes=[stg[0]])
                k.dma("sp", s8(stg[1]), cx.w3[e].rearrange("(c p) n -> p c n", p=128), writes=[stg[1]])
                for c in range(8):
                    k.copy_rr(w1b[:, c, :], s8(stg[0])[:, c, :], reads=[stg[0]], writes=[w1b], engs=("act", "dve", "pool"))
                for c in range(8):
                    k.copy_rr(w3b[:, c, :], s8(stg[1])[:, c, :], reads=[stg[1]], writes=[w3b], engs=("act", "dve", "pool"))
                k.dma("sp", s4(stg[0]), cx.w2[e].rearrange("(c p) n -> p c n", p=128), writes=[stg[0]])
                for c in range(4):
                    k.copy_rr(w2b[:, c, :], s4(stg[0])[:, c, :], reads=[stg[0]], writes=[w2b], engs=("act", "dve", "pool"))
                for sb in range(NSB):
                    ssl = slice(sb * 512, (sb + 1) * 512)
                    g_ = G[sb % 2]
                    k.mm(p_g[:], sel[:, e, :], gmT[:, ssl], reads=[sel, gmT], writes=[p_g])
                    k.copy("act", g_[:], p_g[:], reads=[p_g], writes=[g_])
                    for dc in range(4):
                        for c in range(8):
                            k.mm(p_h1[:], w1b[:, c, dc * 128:(dc + 1) * 128], x1b[:, c, ssl], start=(c == 0), stop=(c == 7),
                                 reads=[w1b, x1b], writes=[p_h1])
                        for c in range(8):
                            k.mm(p_h3[:], w3b[:, c, dc * 128:(dc + 1) * 128], x1b[:, c, ssl], start=(c == 0), stop=(c == 7),
                                 reads=[w3b, x1b], writes=[p_h3])
                        a_, b3 = s1[dc % 2], t3[dc % 2]
                        k.act(a_[:], p_h1[:], AF.Silu, reads=[p_h1], writes=[a_])
                        k.tt("dve", b3[:], p_h3[:], g_[:], ALU.mult, reads=[p_h3, g_], writes=[b3])
                        k.tt("pool", he[:, dc, :], a_[:], b3[:], ALU.mult, reads=[a_, b3], writes=[he])
                    for mc in range(8):
                        py = p_y[mc % 2]
                        for dc in range(4):
                            k.mm(py[:], w2b[:, dc, mc * 128:(mc + 1) * 128], he[:, dc, :], start=(dc == 0), stop=(dc == 3),
                                 reads=[w2b, he], writes=[py])
                        if e == 0:
                            k.copy("dve", yacc[:, mc, ssl], py[:], reads=[py], writes=[yacc])
                        else:
                            k.tt("dve", yacc[:, mc, ssl], yacc[:, mc, ssl], py[:], ALU.add, reads=[yacc, py], writes=[yacc])
            for sb in range(NSB):
                ssl = slice(sb * 512, (sb + 1) * 512)
                t0 = tb0 + sb * 512
                xs = stg[0]
                k.dma("sp", s8(xs), x1v[:, :, t0:t0 + 512], writes=[xs])
                k.dma("sp", stg[1][:, 0:1024].rearrange("p (c n) -> p c n", c=2), pT.rearrange("(c p) t -> p c t", p=128)[:, :, t0:t0 + 512],
                      writes=[stg[1]])
                k.copy("pool", pb_[:], stg[1][:, 0:1024].rearrange("p (c n) -> p c n", c=2), reads=[stg[1]], writes=[pb_])
                for mc in range(8):
                    k.stt("dve", yacc[:, mc, ssl], s8(xs)[:, mc, :], ALPHA, yacc[:, mc, ssl], ALU.mult, ALU.add,
                          reads=[xs, yacc], writes=[yacc])
                    k.copy("act" if mc % 2 else "pool", ub[:, mc, :], yacc[:, mc, ssl], reads=[yacc], writes=[ub])
                for mc in range(8):
                    for c in range(8):
                        k.mm(p_h1[:], pg[:, c, mc * 128:(mc + 1) * 128], ub[:, c, :], start=(c == 0), stop=(c == 7),
                             reads=[pg, ub], writes=[p_h1])
                    for c in range(2):
                        k.mm(p_h3[:], pp[:, c, mc * 128:(mc + 1) * 128], pb_[:, c, :], start=(c == 0), stop=(c == 1),
                             reads=[pp, pb_], writes=[p_h3])
                    s_ = sg[mc % 2]
                    k.act(s_[:], p_h1[:], AF.Sigmoid, reads=[p_h1], writes=[s_])
                    k.tt("dve", s_[:], p_h3[:], s_[:], ALU.mult, reads=[p_h3, s_], writes=[s_])
                    k.tt("pool", yacc[:, mc, ssl], yacc[:, mc, ssl], s_[:], ALU.add, reads=[yacc, s_], writes=[yacc])
                hb = stg[1]
                for mc in range(8):
                    k.copy("act" if mc % 2 else "pool", s8(hb)[:, mc, :], yacc[:, mc, ssl], reads=[yacc], writes=[hb])
                ln_in = BufView(hb, s8(hb))
                ln_sq = BufView(stg[0], s8(stg[0]))
                layernorm_fm(k, cx, st, ln_in, ln_sq, ln_sq, "ln2g", "ln2b", p_m, p_v, tm, tr_)
                k.dma("sp", xnT.rearrange("(c p) t -> p c t", p=128)[:, :, t0:t0 + 512], s8(stg[0]), reads=[stg[0]])


class BufView:
    def __init__(self, parent, ap):
        self._p = parent
        self._ap = ap

    def __getitem__(self, idx):
        return self._ap[idx]

    def __getattr__(self, name):
        return getattr(self._p, name)

    def __setattr__(self, name, val):
        if name in ("_p", "_ap"):
            object.__setattr__(self, name, val)
        else:
            setattr(self._p, name, val)


def declare_ffn_inputs(nc, cx, sfx="", fused=False):
    def din(name, shape, dt=F32):
        return nc.dram_tensor(name + sfx, list(shape), dt, kind="ExternalInput").ap()
    if not fused:
        cx.x1in = din("x1in", [D, T])
        cx.aff_own = din("aff_own", [T, 16])
        cx.aff_all = din("aff_all", [NCORES * T, 16])
    cx.pT = din("pT", [256, T])
    cx.w1 = din("w1", [16, D, 512])
    cx.w3 = din("w3", [16, D, 512])
    cx.w2 = din("w2", [16, 512, D])
    cx.ple_gate = din("ple_gate", [D, D])
    cx.ple_proj = din("ple_proj", [256, D])
    cx.pvecF = din("pvecF", [128, NPV])


def ffn_input_arrays(c, l, inp, pT_c):
    return {
        "pT": pT_c,
        "w1": inp["moe_w1"][l], "w3": inp["moe_w3"][l], "w2": inp["moe_w2"][l],
        "ple_gate": inp["ple_gate"][l], "ple_proj": inp["ple_proj"][l],
        "pvecF": pack_pvec(c, l, inp),
    }


def build_stage(kind, dbg={}):
    nc = bass.Bass("TRN2", target_bir_lowering=False)
    cx = Ctx()
    k = KB(nc)
    consts(k, cx)
    if kind in ("B", "C"):
        declare_ffn_inputs(nc, cx)
        cx.pv = k.sb("pvF", [128, NPV], F32)
        k.dma("sp", cx.pv[:], cx.pvecF, writes=[cx.pv])
        if kind == "B":
            xn = nc.dram_tensor("xn", [D, T], F32).ap()
        else:
            xn = nc.dram_tensor("yT", [D, T], F32, kind="ExternalOutput").ap()
        thr_u = phase_thresholds(k, cx, cx.aff_all)
        phase_ffn(k, cx, thr_u, cx.x1in, cx.aff_own, cx.pT, xn)
        k.sync_all()
    if kind in ("A", "B"):
        if kind == "A":
            cx.xT = nc.dram_tensor("xT", [D, T], F32, kind="ExternalInput").ap()
        else:
            cx.xT = xn
        declare_mixer_inputs(nc, cx)
        zgrp = [(0, 512, nc.dram_tensor("zA", [512, T], F32).ap()), (512, 1024, nc.dram_tensor("zB", [512, T], F32).ap()),
                (1024, 2048, nc.dram_tensor("zC", [1024, T], F32).ap()), (2048, 3200, nc.dram_tensor("zD", [1152, T], F32).ap())]

        def zT(r0, r1):
            for a, b, ap in zgrp:
                if a <= r0 and r1 <= b:
                    return ap[r0 - a:r1 - a, :]
            raise ValueError((r0, r1))
        cx.zT = zT
        cx.vi = nc.dram_tensor("vi", [T, 512], F32).ap()
        cx.oT = nc.dram_tensor("oT", [D, T], BF16).ap()
        cx.x1T = nc.dram_tensor("x1T", [D, T], F32, kind="ExternalOutput").ap()
        cx.aff = nc.dram_tensor("aff", [T, 16], F32, kind="ExternalOutput").ap()
        cx.pv = k.sb("pv", [128, NPV], F32)
        k.dma("sp", cx.pv[:], cx.pvec, writes=[cx.pv])
        cx.pv64 = k.sb("pv64", [64, NPV64], F32)
        k.dma("sp", cx.pv64[:], cx.pvec64, writes=[cx.pv64])
        phase_proj(k, cx, cx.xT, cx.wfm, cx.wtm, cx.zT, cx.vi)
        phase_mla(k, cx)
        phase_na(k, cx)
        phase_hgrn(k, cx, 0 if kind == "A" else 1)
        phase_rwkv(k, cx)
        phase_outproj(k, cx, cx.xT)
    k.sync_all()
    dt = k.track("dbgt")
    for name, (src, shape, dtp) in dbg.items():
        o = nc.dram_tensor("dbg_" + name, list(shape), dtp, kind="ExternalOutput").ap()
        k.dma("sp", o, src(cx), writes=[dt], owner=dt)
    k.sync_all()
    return nc, k


def run_stage(kind, in_maps):
    nc, k = build_stage(kind)
    res = run_bass_kernel_spmd(nc, in_maps, core_ids=list(range(NCORES)))
    return res.results


FUSED = False


def kernel(**inp):
    inp = {kk: np.asarray(v) for kk, v in inp.items()}
    if FUSED:
        return kernel_fused(inp)
    xp, xs = inp["x_prompt"], inp["x_sample"]
    pp, ps = inp["p_prompt"], inp["p_sample"]
    maps = []
    for c in range(NCORES):
        m = mixer_input_arrays(c, 0, inp)
        m["xT"] = np.ascontiguousarray(gather_tokens(c, xp, xs).T)
        maps.append(m)
    rA = run_stage("A", maps)
    aff_all = np.ascontiguousarray(np.concatenate([np.asarray(rA[c]["aff"]) for c in range(NCORES)], axis=0))
    maps = []
    for c in range(NCORES):
        pT = np.ascontiguousarray(gather_tokens(c, pp[0], ps[0]).T)
        m = ffn_input_arrays(c, 0, inp, pT)
        m.update(mixer_input_arrays(c, 1, inp))
        m["x1in"] = np.asarray(rA[c]["x1T"])
        m["aff_own"] = np.asarray(rA[c]["aff"])
        m["aff_all"] = aff_all
        maps.append(m)
    del rA
    rB = run_stage("B", maps)
    aff_all = np.ascontiguousarray(np.concatenate([np.asarray(rB[c]["aff"]) for c in range(NCORES)], axis=0))
    maps = []
    for c in range(NCORES):
        pT = np.ascontiguousarray(gather_tokens(c, pp[1], ps[1]).T)
        m = ffn_input_arrays(c, 1, inp, pT)
        m["x1in"] = np.asarray(rB[c]["x1T"])
        m["aff_own"] = np.asarray(rB[c]["aff"])
        m["aff_all"] = aff_all
        maps.append(m)
    del rB
    rC = run_stage("C", maps)
    y_prompt = np.zeros((4, 8192, D), np.float32)
    y_sample = np.zeros((16, 4096, D), np.float32)
    for c in range(NCORES):
        y = np.asarray(rC[c]["yT"]).T
        for u, (kind, i, h) in enumerate(core_units(c)):
            blk = y[u * UNIT:(u + 1) * UNIT]
            if kind == "p":
                y_prompt[i, h * UNIT:(h + 1) * UNIT] = blk
            else:
                y_sample[i] = blk
    return (y_prompt, y_sample)


def all_gather_aff(k, cx, aff, aff_all):
    k.sync_all()
    nc = k.nc
    pool = k.engs["pool"]
    k.semuid = getattr(k, "semuid", 0) + 1
    csem = nc.alloc_semaphore(name="cc_sem%d" % k.semuid)
    k.dsems.append(csem)
    nc.gpsimd.collective_compute("AllGather", ALU.bypass, replica_groups=[list(range(NCORES))],
                                 ins=[aff.opt()], outs=[aff_all.opt()]).then_inc(csem, CC_INC)
    pool.h.wait_ge(csem, CC_INC)
    k.sync_all()


CC_INC = 1


def build_fused():
    nc = bass.Bass("TRN2", target_bir_lowering=False, num_devices=NCORES)
    cx = Ctx()
    k = KB(nc)
    consts(k, cx)
    xin = nc.dram_tensor("xT", [D, T], F32, kind="ExternalInput").ap()
    yT = nc.dram_tensor("yT", [D, T], F32, kind="ExternalOutput").ap()
    xn = nc.dram_tensor("xn", [D, T], F32).ap()
    zgrp = [(0, 512, nc.dram_tensor("zA", [512, T], F32).ap()), (512, 1024, nc.dram_tensor("zB", [512, T], F32).ap()),
            (1024, 2048, nc.dram_tensor("zC", [1024, T], F32).ap()), (2048, 3200, nc.dram_tensor("zD", [1152, T], F32).ap())]

    def zT(r0, r1):
        for a, b, ap in zgrp:
            if a <= r0 and r1 <= b:
                return ap[r0 - a:r1 - a, :]
        raise ValueError((r0, r1))
    cx.zT = zT
    cx.vi = nc.dram_tensor("vi", [T, 512], F32).ap()
    cx.oT = nc.dram_tensor("oT", [D, T], BF16).ap()
    cx.x1T = nc.dram_tensor("x1T", [D, T], F32).ap()
    cx.aff = nc.dram_tensor("aff", [T, 16], F32).ap()
    aff_all = nc.dram_tensor("aff_all", [NCORES * T, 16], F32).ap()
    pvm = k.sb("pvm", [128, NPV], F32)
    pvf = k.sb("pvf", [128, NPV], F32)
    cx.pv64 = k.sb("pv64", [64, NPV64], F32)
    for l in range(DEPTH):
        cx.sfx = "_%d" % l
        cx.xT = xin if l == 0 else xn
        declare_mixer_inputs(nc, cx, cx.sfx)
        cx.pv = pvm
        k.dma("sp", pvm[:], cx.pvec, writes=[pvm])
        k.dma("sp", cx.pv64[:], cx.pvec64, writes=[cx.pv64])
        phase_proj(k, cx, cx.xT, cx.wfm, cx.wtm, cx.zT, cx.vi)
        phase_mla(k, cx)
        phase_na(k, cx)
        phase_hgrn(k, cx, l)
        phase_rwkv(k, cx)
        phase_outproj(k, cx, cx.xT)
        all_gather_aff(k, cx, cx.aff, aff_all)
        declare_ffn_inputs(nc, cx, cx.sfx, fused=True)
        cx.pv = pvf
        k.dma("sp", pvf[:], cx.pvecF, writes=[pvf])
        thr_u = phase_thresholds(k, cx, aff_all)
        phase_ffn(k, cx, thr_u, cx.x1T, cx.aff, cx.pT, xn if l == 0 else yT)
        k.sync_all()
    return nc, k


def kernel_fused(inp):
    xp, xs = inp["x_prompt"], inp["x_sample"]
    pp, ps = inp["p_prompt"], inp["p_sample"]
    maps = []
    for c in range(NCORES):
        m = {"xT": np.ascontiguousarray(gather_tokens(c, xp, xs).T)}
        for l in range(DEPTH):
            sfx = "_%d" % l
            pT = np.ascontiguousarray(gather_tokens(c, pp[l], ps[l]).T)
            for kk, v in mixer_input_arrays(c, l, inp).items():
                m[kk + sfx] = v
            for kk, v in ffn_input_arrays(c, l, inp, pT).items():
                m[kk + sfx] = v
        maps.append(m)
    nc, k = build_fused()
    res = run_bass_kernel_spmd(nc, maps, core_ids=list(range(NCORES))).results
    y_prompt = np.zeros((4, 8192, D), np.float32)
    y_sample = np.zeros((16, 4096, D), np.float32)
    for c in range(NCORES):
        y = np.asarray(res[c]["yT"]).T
        for u, (kind, i, h) in enumerate(core_units(c)):
            blk = y[u * UNIT:(u + 1) * UNIT]
            if kind == "p":
                y_prompt[i, h * UNIT:(h + 1) * UNIT] = blk
            else:
                y_sample[i] = blk
    return (y_prompt, y_sample)
```
